# Optimizing a Trainium2 kernel written in Bass

```python
import math
import jax, jax.numpy as jnp
from jax import lax
import numpy as np

D_MODEL = 1024
BATCH = 8
SEQ = 2048
DEPTH = 4
DEC_BATCH = 128
DEC_SEQ = 8
PAST_LEN = 2048
PAGE_SIZE = 128

N_A = DEPTH // 2
N_B = DEPTH - N_A
N_HEADS = 8
HEAD_DIM = 64
V_DIM = 2 * HEAD_DIM
QK_WIDTH = 2 * N_HEADS * HEAD_DIM
V_WIDTH = N_HEADS * V_DIM
CONV_W = 31
_FF_RAW = -(-8 * D_MODEL // 3)
D_FF = -(-_FF_RAW // 256) * 256
N_BUCKETS = 32
MAX_DISTANCE = 128
Q_BLOCK = 128
EPS = 1e-6

kernel_name = "yoco_conformer_diffattn_decoder_step"


def rms_norm(x, g):
    xf = x.astype(jnp.float32)
    y = xf * lax.rsqrt(jnp.mean(xf * xf, axis=-1, keepdims=True) + EPS)
    return (y * g.astype(jnp.float32)).astype(x.dtype)


def lambda_init(layer_idx):
    return 0.8 - 0.6 * math.exp(-0.3 * layer_idx)


def rel_buckets(n):
    max_exact = N_BUCKETS // 2
    nf = jnp.maximum(n, 1).astype(jnp.float32)
    large = max_exact + (jnp.log(nf / max_exact) / math.log(MAX_DISTANCE / max_exact)
                         * (N_BUCKETS - max_exact)).astype(jnp.int32)
    large = jnp.minimum(large, N_BUCKETS - 1)
    return jnp.where(n < max_exact, n, large)


def diff_weights(scores, q_pos, k_pos, rel_bias, lam):
    dist = q_pos[:, None] - k_pos[None, :]
    bias = jnp.transpose(rel_bias[rel_buckets(jnp.maximum(dist, 0))], (2, 0, 1)).astype(jnp.float32)
    logits = scores + bias[None, :, None]
    logits = jnp.where((dist >= 0)[None, None, None], logits, -jnp.inf)
    p = jax.nn.softmax(logits, axis=-1)
    return p[:, :, 0] - lam * p[:, :, 1]


def prompt_diff_attn(q, k, v, rel_bias, lam):
    B, S = q.shape[0], q.shape[1]
    nqb = S // Q_BLOCK
    qb = q.reshape(B, nqb, Q_BLOCK, N_HEADS, 2, HEAD_DIM).transpose(1, 0, 2, 3, 4, 5)
    k_pos = jnp.arange(S)
    scale = HEAD_DIM ** -0.5

    def block(args):
        i, qi = args
        s = jnp.einsum('bqhcd,bkhcd->bhcqk', qi, k, preferred_element_type=jnp.float32) * scale
        q_pos = i * Q_BLOCK + jnp.arange(Q_BLOCK)
        a = diff_weights(s, q_pos, k_pos, rel_bias, lam)
        return jnp.einsum('bhqk,bkhe->bqhe', a.astype(v.dtype), v)

    out = lax.map(block, (jnp.arange(nqb), qb))
    return out.transpose(1, 0, 2, 3, 4).reshape(B, S, N_HEADS, V_DIM)


def sample_diff_attn(q, k_past, v_past, k_new, v_new, rel_bias, lam):
    P, T = k_past.shape[1], q.shape[1]
    scale = HEAD_DIM ** -0.5
    s_past = jnp.einsum('bqhcd,bkhcd->bhcqk', q, k_past, preferred_element_type=jnp.float32)
    s_new = jnp.einsum('bqhcd,bkhcd->bhcqk', q, k_new, preferred_element_type=jnp.float32)
    s = jnp.concatenate([s_past, s_new], axis=-1) * scale
    q_pos = P + jnp.arange(T)
    k_pos = jnp.arange(P + T)
    a = diff_weights(s, q_pos, k_pos, rel_bias, lam).astype(v_new.dtype)
    return (jnp.einsum('bhqk,bkhe->bqhe', a[..., :P], v_past)
            + jnp.einsum('bhqk,bkhe->bqhe', a[..., P:], v_new))


def conv_module(h, past, g_in, w1, b1, wdw, bdw, g_mid, w2, b2):
    a = rms_norm(h, g_in) @ w1 + b1
    glu = a[..., :D_MODEL] * jax.nn.sigmoid(a[..., D_MODEL:])
    u = jnp.concatenate([past.astype(glu.dtype), glu], axis=1)
    c = lax.conv_general_dilated(u, wdw[:, None, :].astype(u.dtype), window_strides=(1,), padding='VALID',
                                 dimension_numbers=('NWC', 'WIO', 'NWC'),
                                 feature_group_count=D_MODEL) + bdw
    c = jax.nn.silu(rms_norm(c, g_mid))
    return c @ w2 + b2, u[:, -(CONV_W - 1):]


def shared_kv(h, g, w_kv, k_g):
    B, T = h.shape[0], h.shape[1]
    kv = rms_norm(h, g) @ w_kv
    k = rms_norm(kv[..., :QK_WIDTH].reshape(B, T, N_HEADS, 2, HEAD_DIM), k_g)
    v = kv[..., QK_WIDTH:].reshape(B, T, N_HEADS, V_DIM)
    return k, v


def queries(h, g, w, q_g):
    B, T = h.shape[0], h.shape[1]
    q = (rms_norm(h, g) @ w).reshape(B, T, N_HEADS, 2, HEAD_DIM)
    return rms_norm(q, q_g)


def attn_out(o, g, w, lam_init):
    B, T = o.shape[0], o.shape[1]
    o = rms_norm(o, g) * (1.0 - lam_init)
    return o.reshape(B, T, V_WIDTH) @ w


def swiglu(h, g, wg, wu, wd):
    xn = rms_norm(h, g)
    return (jax.nn.silu(xn @ wg) * (xn @ wu)) @ wd


def setup_inputs(seed: int = 0) -> dict:
    key = jax.random.key(seed)
    ks = iter(jax.random.split(key, 48))

    def nrm(shape, scale):
        return jax.random.normal(next(ks), shape, jnp.float32) * scale

    def gain(shape):
        return 1.0 + nrm(shape, 0.02)

    n_pages = PAST_LEN // PAGE_SIZE
    n_used = DEC_BATCH * n_pages
    n_pool = n_used + max(1, n_used // 4)
    perm = jax.random.permutation(next(ks), n_pool)
    page_table = perm[:n_used].reshape(DEC_BATCH, n_pages).astype(jnp.int32)
    d_in = D_MODEL ** -0.5
    return {
        'x_prompt': nrm((BATCH, SEQ, D_MODEL), 1.0),
        'x_sample': nrm((DEC_BATCH, DEC_SEQ, D_MODEL), 1.0),
        'state_conv': nrm((N_A, DEC_BATCH, CONV_W - 1, D_MODEL), 0.5),
        'cache_k': nrm((n_pool, PAGE_SIZE, N_HEADS, 2, HEAD_DIM), 1.0),
        'cache_v': nrm((n_pool, PAGE_SIZE, N_HEADS, V_DIM), 1.0),
        'page_table': page_table,
        'rel_bias': nrm((N_BUCKETS, N_HEADS), 0.5),
        'conv_norm': gain((N_A, D_MODEL)),
        'w_pw1': nrm((N_A, D_MODEL, 2 * D_MODEL), d_in),
        'b_pw1': nrm((N_A, 2 * D_MODEL), 0.02),
        'w_dw': nrm((N_A, CONV_W, D_MODEL), CONV_W ** -0.5),
        'b_dw': nrm((N_A, D_MODEL), 0.02),
        'conv_mid_norm': gain((N_A, D_MODEL)),
        'w_pw2': nrm((N_A, D_MODEL, D_MODEL), d_in),
        'b_pw2': nrm((N_A, D_MODEL), 0.02),
        'kv_norm': gain((D_MODEL,)),
        'w_kv': nrm((D_MODEL, QK_WIDTH + V_WIDTH), d_in),
        'k_norm': gain((HEAD_DIM,)),
        'attn_norm': gain((N_B, D_MODEL)),
        'w_q': nrm((N_B, D_MODEL, QK_WIDTH), d_in),
        'q_norm': gain((N_B, HEAD_DIM)),
        'lambda_q1': nrm((N_B, HEAD_DIM), 0.1),
        'lambda_k1': nrm((N_B, HEAD_DIM), 0.1),
        'lambda_q2': nrm((N_B, HEAD_DIM), 0.1),
        'lambda_k2': nrm((N_B, HEAD_DIM), 0.1),
        'sub_norm': gain((N_B, V_DIM)),
        'w_o': nrm((N_B, V_WIDTH, D_MODEL), V_WIDTH ** -0.5),
        'ffn_norm': gain((DEPTH, D_MODEL)),
        'w_gate': nrm((DEPTH, D_MODEL, D_FF), d_in),
        'w_up': nrm((DEPTH, D_MODEL, D_FF), d_in),
        'w_down': nrm((DEPTH, D_FF, D_MODEL), D_FF ** -0.5),
    }


def reference(x_prompt, x_sample, state_conv, cache_k, cache_v, page_table, rel_bias,
              conv_norm, w_pw1, b_pw1, w_dw, b_dw, conv_mid_norm, w_pw2, b_pw2,
              kv_norm, w_kv, k_norm, attn_norm, w_q, q_norm,
              lambda_q1, lambda_k1, lambda_q2, lambda_k2, sub_norm, w_o,
              ffn_norm, w_gate, w_up, w_down):
    hp, hs = x_prompt, x_sample
    conv_zero = jnp.zeros((hp.shape[0], CONV_W - 1, D_MODEL), hp.dtype)
    dbatch, n_pages = page_table.shape
    past = n_pages * cache_k.shape[1]
    k_past = cache_k[page_table].reshape(dbatch, past, N_HEADS, 2, HEAD_DIM)
    v_past = cache_v[page_table].reshape(dbatch, past, N_HEADS, V_DIM)
    conv_p, conv_s = [], []
    for l in range(DEPTH):
        if l < N_A:
            prm = (conv_norm[l], w_pw1[l], b_pw1[l], w_dw[l], b_dw[l], conv_mid_norm[l], w_pw2[l], b_pw2[l])
            dp, sp = conv_module(hp, conv_zero, *prm)
            ds, ss = conv_module(hs, state_conv[l], *prm)
            hp = hp + dp
            hs = hs + ds
            conv_p.append(sp)
            conv_s.append(ss)
        else:
            j = l - N_A
            if j == 0:
                kp, vp = shared_kv(hp, kv_norm, w_kv, k_norm)
                ksm, vsm = shared_kv(hs, kv_norm, w_kv, k_norm)
            lam_init = lambda_init(l)
            lam = (jnp.exp(jnp.sum(lambda_q1[j].astype(jnp.float32) * lambda_k1[j].astype(jnp.float32)))
                   - jnp.exp(jnp.sum(lambda_q2[j].astype(jnp.float32) * lambda_k2[j].astype(jnp.float32)))
                   + lam_init)
            qp = queries(hp, attn_norm[j], w_q[j], q_norm[j])
            qs = queries(hs, attn_norm[j], w_q[j], q_norm[j])
            op = prompt_diff_attn(qp, kp, vp, rel_bias, lam)
            os_ = sample_diff_attn(qs, k_past, v_past, ksm, vsm, rel_bias, lam)
            hp = hp + attn_out(op, sub_norm[j], w_o[j], lam_init)
            hs = hs + attn_out(os_, sub_norm[j], w_o[j], lam_init)
        hp = hp + swiglu(hp, ffn_norm[l], w_gate[l], w_up[l], w_down[l])
        hs = hs + swiglu(hs, ffn_norm[l], w_gate[l], w_up[l], w_down[l])
    conv_state_p = jnp.stack(conv_p)
    conv_state_s = jnp.stack(conv_s)
    return (hp, hs, conv_state_p, conv_state_s, kp, vp, ksm, vsm)
```

```python
import math
import numpy as np
import concourse.bass as bass
import concourse.mybir as mybir
from concourse.bass_utils import run_bass_kernel_spmd

F32 = mybir.dt.float32
BF16 = mybir.dt.bfloat16
I32 = mybir.dt.int32
AF = mybir.ActivationFunctionType
ALU = mybir.AluOpType
AX = mybir.AxisListType

D = 1024
KC = 8
DFF = 2816
NH = 8
CW = 31
NBS = 16
DS = 8
NS = NBS * DS
EPS = 1e-6
NPG = 16
FUSED = True
SL = 38


def lambda_init(l):
    return 0.8 - 0.6 * math.exp(-0.3 * l)


def bucket_runs(maxd):
    n = np.arange(0, maxd + 1)
    nf = np.maximum(n, 1).astype(np.float32)
    large = 16 + (np.log(nf / np.float32(16)) / np.float32(math.log(8.0)) * np.float32(16)).astype(np.int32)
    large = np.minimum(large, 31)
    bk = np.where(n < 16, n, large)
    runs = []
    s = 0
    for i in range(1, maxd + 2):
        if i == maxd + 1 or bk[i] != bk[s]:
            runs.append((int(bk[s]), s, i - 1))
            s = i
    return runs


class Buf:
    __slots__ = ("w", "r")

    def __init__(self):
        self.w = None
        self.r = []


class Eng:
    def __init__(self, name, handle, sem):
        self.name = name
        self.h = handle
        self.sem = sem
        self.cnt = 0
        self.seen = {}
        self.pr = []
        self.pw = []


class Ctx:
    def __init__(self, nc, n_dma_sems=48):
        self.nc = nc
        self.E = {}
        for name, h in (("pe", nc.tensor), ("act", nc.scalar), ("dve", nc.vector),
                        ("pool", nc.gpsimd), ("sp", nc.sync)):
            self.E[name] = Eng(name, h, nc.alloc_semaphore("c_" + name))
        self.dsem = [[nc.alloc_semaphore("d%d" % i), 0] for i in range(n_dma_sems)]
        self.dnext = 0

    def _wait(self, eng, tok):
        if tok is None:
            return
        kind, key, val = tok
        k = (kind, key)
        if eng.seen.get(k, 0) >= val:
            return
        if kind == "e":
            if key == eng.name and key == "pe":
                return
            eng.h.wait_ge(self.E[key].sem, val)
        else:
            eng.h.wait_ge(self.dsem[key][0], val)
        eng.seen[k] = val

    def _deps(self, eng, reads, writes):
        for b in reads:
            self._wait(eng, b.w)
        for b in writes:
            self._wait(eng, b.w)
            for t in b.r:
                self._wait(eng, t)

    def _commit(self, tok, reads, writes):
        for b in reads:
            b.r = [t for t in b.r if not (t[0] == tok[0] and t[1] == tok[1])]
            b.r.append(tok)
        for b in writes:
            b.w = tok
            b.r = []

    def op(self, ename, fn, reads=(), writes=(), signal=True):
        eng = self.E[ename]
        self._deps(eng, reads, writes)
        inst = fn(eng.h)
        if signal:
            eng.cnt += 1
            inst.then_inc(eng.sem, 1)
            tok = ("e", ename, eng.cnt)
            self._commit(tok, list(reads) + eng.pr, list(writes) + eng.pw)
            eng.pr = []
            eng.pw = []
        else:
            eng.pr.extend(reads)
            eng.pw.extend(writes)
        return inst

    def dma(self, qname, out, in_, reads=(), writes=(), indirect=None):
        eng = self.E[qname]
        self._deps(eng, reads, writes)
        i = self.dnext
        self.dnext = (self.dnext + 1) % len(self.dsem)
        sem, val = self.dsem[i]
        if val > 0:
            self._wait(eng, ("d", i, val))
        if indirect is None:
            inst = eng.h.dma_start(out=out, in_=in_)
        else:
            inst = eng.h.indirect_dma_start(out=out, out_offset=None, in_=in_,
                                            in_offset=bass.IndirectOffsetOnAxis(ap=indirect, axis=0))
        inst.then_inc(sem, 16)
        self.dsem[i][1] = val + 16
        tok = ("d", i, val + 16)
        self._commit(tok, reads, writes)
        return tok

    def barrier(self):
        for eng in self.E.values():
            for i, (sem, val) in enumerate(self.dsem):
                if val > 0:
                    self._wait(eng, ("d", i, val))
            for name, e in self.E.items():
                if name != eng.name and e.cnt > 0:
                    self._wait(eng, ("e", name, e.cnt))

    def finish(self, qname="sp"):
        eng = self.E[qname]
        for i, (sem, val) in enumerate(self.dsem):
            if val > 0:
                self._wait(eng, ("d", i, val))
        for name, e in self.E.items():
            if name != qname and e.cnt > 0:
                self._wait(eng, ("e", name, e.cnt))


def tiles(t0, t1, step=512):
    out = []
    t = t0
    while t < t1:
        n = min(step, t1 - t)
        out.append((t, n))
        t += n
    return out


class Prog:
    def __init__(self, T, rbytes, xbytes):
        self.nc = nc = bass.Bass("TRN2", target_bir_lowering=False)
        self.cx = Ctx(nc)
        self.T = T
        self.banks = [(nc.alloc_psum_tensor("ps%d" % i, [128, 512], F32)[:], Buf()) for i in range(8)]
        self.ring_i = 0
        self.nring = 6
        self.ident_d = self.din("ident", [128, 128])
        self.ident = self.sb("ident_s", [128, 128])
        self.b_const = Buf()
        self.ones = self.sb("ones", [128, 128], BF16)
        self.nhalf = self.sb("nhalf", [128, 512])
        self.cx.dma("sp", self.ident, self.ident_d, writes=[self.b_const])
        self.cx.op("dve", lambda e: e.memset(self.ones, 1.0), writes=[self.b_const])
        self.cx.op("dve", lambda e: e.memset(self.nhalf, -0.5), writes=[self.b_const])
        self.h = self.sb("h", [128, KC, T])
        self.b_h = Buf()
        self.xn = self.sb("xn", [128, KC, T], BF16)
        self.b_xn = Buf()
        self.R = self.sb("R", [128, rbytes // 2], BF16)
        self.b_R = Buf()
        self.X = self.sb("X", [128, xbytes // 2], BF16)
        self.wring = [(self.sb("wr%d" % i, [128, 2048], BF16), Buf()) for i in range(4)]
        self.wi = 0
        self.sq = [(self.sb("sq%d" % i, [128, 512], BF16), Buf()) for i in range(2)]
        self.ms = self.sb("ms", [128, 512])
        self.b_ms = Buf()
        self.pcount = 0

    def sb(self, name, shape, dt=F32):
        return self.nc.alloc_sbuf_tensor(name, list(shape), dt)[:]

    def din(self, name, shape, dt=F32):
        return self.nc.dram_tensor(name, list(shape), dt, kind="ExternalInput").ap()

    def dout(self, name, shape, dt=F32):
        return self.nc.dram_tensor(name, list(shape), dt, kind="ExternalOutput").ap()

    def dint(self, name, shape, dt=F32):
        return self.nc.dram_tensor(name, list(shape), dt, kind="Internal").ap()

    def carve(self, arena, off, shape, dt):
        sz = 2 if dt == BF16 else 4
        n = int(np.prod(shape))
        assert off % 4 == 0
        v = arena[:, off // 2:(off + n * sz) // 2]
        if dt != BF16:
            v = v.bitcast(dt)
        if len(shape) == 2:
            v = v.rearrange("p (a b) -> p a b", a=shape[0])
        elif len(shape) == 3:
            v = v.rearrange("p (a b c) -> p a b c", a=shape[0], b=shape[1])
        return v

    def pget(self):
        t, b = self.banks[self.ring_i]
        self.ring_i = (self.ring_i + 1) % self.nring
        return t, b

    def param_fm(self, name, dram_vec, n):
        t = self.sb(name, [128, n])
        b = Buf()
        with self.nc.allow_non_contiguous_dma(reason="small param"):
            self.cx.dma("sp", t, dram_vec.rearrange("(k p) -> p k", p=128), writes=[b])
        return t, b

    def param_bc(self, name, dram_vec, n, parts=128):
        t = self.sb(name, [parts, n])
        b = Buf()
        with self.nc.allow_non_contiguous_dma(reason="bcast param"):
            self.cx.dma("sp", t, dram_vec.partition_broadcast(parts), writes=[b])
        return t, b

    def load_tm(self, rows_ap, nrows, dst, dstb, col0, xin, xinb):
        cx = self.cx
        cx.dma("sp", xin[:nrows, :], rows_ap, writes=[xinb])
        for k0 in range(0, KC, 4):
            pt, pb = self.pget()
            for j in range(4):
                cx.op("pe", lambda e, j=j: e.transpose(pt[:, j * 128:j * 128 + nrows],
                                                       xin[:nrows, (k0 + j) * 128:(k0 + j + 1) * 128],
                                                       self.ident[:nrows, :nrows]),
                      reads=[xinb, self.b_const], writes=[pb], signal=(j == 3))
            src = pt[:, :].rearrange("p (a b) -> p a b", a=4)[:, :, :nrows]
            cx.op("act", lambda e: e.activation(out=dst[:, k0:k0 + 4, col0:col0 + nrows], in_=src, func=AF.Copy),
                  reads=[pb], writes=[dstb])

    def store_tm(self, src, srcb, col0, nrows, rows_ap, stg, stgb, nk=KC):
        cx = self.cx
        for k0 in range(0, nk, 4):
            pt, pb = self.pget()
            for j in range(4):
                cx.op("pe", lambda e, j=j: e.transpose(pt[:nrows, j * 128:(j + 1) * 128],
                                                       src[:, k0 + j, col0:col0 + nrows], self.ident),
                      reads=[srcb, self.b_const], writes=[pb], signal=(j == 3))
            cx.op("dve", lambda e: e.tensor_copy(out=stg[:nrows, k0 * 128:(k0 + 4) * 128], in_=pt[:nrows, :]),
                  reads=[pb], writes=[stgb])
        cx.dma("sp", rows_ap, stg[:nrows, :nk * 128], reads=[stgb])

    def rmsnorm(self, g, gb, t0, t1, src=None, srcb=None):
        cx = self.cx
        h = self.h if src is None else src
        hb = self.b_h if srcb is None else srcb
        for (a, n) in tiles(t0, t1):
            pt, pb = self.pget()
            for kc in range(KC):
                s, sbf = self.sq[kc % 2]
                cx.op("act", lambda e, kc=kc, s=s: e.activation(out=s[:, :n], in_=h[:, kc, a:a + n], func=AF.Square),
                      reads=[hb], writes=[sbf])
                cx.op("pe", lambda e, kc=kc, s=s: e.matmul(pt[:, :n], lhsT=self.ones, rhs=s[:, :n],
                                                          start=(kc == 0), stop=(kc == KC - 1)),
                      reads=[sbf, self.b_const], writes=[pb])
            self.rstd_from(pt, pb, n, 1.0 / D)
            for kc in range(KC):
                cx.op("dve", lambda e, kc=kc: e.scalar_tensor_tensor(
                    out=self.xn[:, kc, a:a + n], in0=h[:, kc, a:a + n], scalar=g[:, kc:kc + 1],
                    in1=self.ms[:, :n], op0=ALU.mult, op1=ALU.mult),
                    reads=[hb, self.b_ms, gb], writes=[self.b_xn])

    def rstd_from(self, pt, pb, n, inv):
        cx = self.cx
        cx.op("dve", lambda e: e.tensor_scalar(out=self.ms[:, :n], in0=pt[:, :n], scalar1=inv, scalar2=EPS,
                                              op0=ALU.mult, op1=ALU.add), reads=[pb], writes=[self.b_ms])
        cx.op("pool", lambda e: e.tensor_tensor(out=self.ms[:, :n], in0=self.ms[:, :n], in1=self.nhalf[:, :n],
                                               op=ALU.pow), reads=[self.b_ms, self.b_const], writes=[self.b_ms])

    def wfetch(self, W_rows, c0, cb, kcs):
        slot, sbf = self.wring[self.wi]
        self.wi = (self.wi + 1) % len(self.wring)
        wv = slot[:, :kcs * 256].rearrange("p (k c) -> p k c", c=256)
        self.cx.dma("pool", wv[:, :, :cb], W_rows[:, c0:c0 + cb].rearrange("(k p) c -> p k c", p=128), writes=[sbf])
        return wv, sbf

    def linear(self, x, xb, kcs, W_rows, ncols, tok_tiles, epi, LA=2):
        cx = self.cx
        blocks = [(c0, min(256, ncols - c0)) for c0 in range(0, ncols, 256)]
        fetched = []
        for i in range(min(LA, len(blocks))):
            fetched.append(self.wfetch(W_rows, blocks[i][0], blocks[i][1], kcs))
        for i, (c0, cb) in enumerate(blocks):
            if i + LA < len(blocks):
                fetched.append(self.wfetch(W_rows, blocks[i + LA][0], blocks[i + LA][1], kcs))
            wv, sbf = fetched[i]
            for m in range(cb // 128):
                for (a, n) in tok_tiles:
                    pt, pb = self.pget()
                    for k in range(kcs):
                        cx.op("pe", lambda e, k=k: e.matmul(pt[:, :n], lhsT=wv[:, k, m * 128:(m + 1) * 128],
                                                           rhs=x[:, k, a:a + n], start=(k == 0), stop=(k == kcs - 1)),
                              reads=[sbf, xb], writes=[pb], signal=(k == kcs - 1))
                    epi((c0 // 128) + m, a, n, pt, pb)

    def add_to_h(self, m, a, n, pt, pb):
        self.cx.op("dve", lambda e: e.tensor_tensor(out=self.h[:, m, a:a + n], in0=pt[:, :n], in1=self.h[:, m, a:a + n],
                                                   op=ALU.add), reads=[pb, self.b_h], writes=[self.b_h])

    def ffn(self, g, gb, Wg, Wu, Wd, t0, t1, sgt):
        cx = self.cx
        self.rmsnorm(g, gb, t0, t1)
        tt = tiles(t0, t1)
        hid = self.carve(self.R, 0, [KC, self.T], BF16)
        for (f0, f1) in ((0, 8), (8, 15), (15, 22)):
            nf = f1 - f0
            blocks = [(c0, min(256, f1 * 128 - c0)) for c0 in range(f0 * 128, f1 * 128, 256)]
            fetched = []

            def fetch(i):
                c0, cb = blocks[i]
                fetched.append((self.wfetch(Wg, c0, cb, KC), self.wfetch(Wu, c0, cb, KC)))
            fetch(0)
            for i, (c0, cb) in enumerate(blocks):
                if i + 1 < len(blocks):
                    fetch(i + 1)
                (wg, gbf), (wu, ubf) = fetched[i]
                for m in range(cb // 128):
                    fi = (c0 - f0 * 128) // 128 + m
                    for (a, n) in tt:
                        pg, pgb = self.pget()
                        pu, pub = self.pget()
                        for k in range(KC):
                            cx.op("pe", lambda e, k=k: e.matmul(pg[:, :n], lhsT=wg[:, k, m * 128:(m + 1) * 128],
                                                               rhs=self.xn[:, k, a:a + n], start=(k == 0), stop=(k == KC - 1)),
                                  reads=[gbf, self.b_xn], writes=[pgb], signal=(k == KC - 1))
                        for k in range(KC):
                            cx.op("pe", lambda e, k=k: e.matmul(pu[:, :n], lhsT=wu[:, k, m * 128:(m + 1) * 128],
                                                               rhs=self.xn[:, k, a:a + n], start=(k == 0), stop=(k == KC - 1)),
                                  reads=[ubf, self.b_xn], writes=[pub], signal=(k == KC - 1))
                        sg, sgb = sgt[self.pcount % 2]
                        self.pcount += 1
                        cx.op("act", lambda e: e.activation(out=sg[:, :n], in_=pg[:, :n], func=AF.Silu),
                              reads=[pgb], writes=[sgb])
                        cx.op("dve", lambda e: e.tensor_tensor(out=hid[:, fi, a:a + n], in0=sg[:, :n], in1=pu[:, :n],
                                                              op=ALU.mult), reads=[sgb, pub], writes=[self.b_R])
            self.linear(hid, self.b_R, nf, Wd[f0 * 128:f1 * 128, :], D, tt, self.add_to_h)


def build_L1(P, fused=False, npool=0):
    T = P + NS
    UP = 30 + P
    UL = UP + NBS * SL
    rbytes = ((KC * UL * 2 + 63) // 64) * 64
    if fused:
        rbytes = max(rbytes, 40192)
    pr = Prog(T, rbytes, 30720)
    nc, cx = pr.nc, pr.cx
    din, dout = pr.din, pr.dout
    xp = din("xp", [P, D]); xs = din("xs", [NS, D]); sc = din("sc", [2, NBS, 30, D])
    anti_d = din("anti", [128, 128])
    rel_bias = din("rel_bias", [32, 8])
    conv_norm = din("conv_norm", [2, D]); w_pw1 = din("w_pw1", [2, D, 2 * D]); b_pw1 = din("b_pw1", [2, 2 * D])
    w_dw = din("w_dw", [2, CW, D]); b_dw = din("b_dw", [2, D]); conv_mid = din("conv_mid_norm", [2, D])
    w_pw2 = din("w_pw2", [2, D, D]); b_pw2 = din("b_pw2", [2, D])
    kv_norm = din("kv_norm", [D]); w_kv = din("w_kv", [D, 2 * D]); k_norm = din("k_norm", [64])
    attn_norm = din("attn_norm", [2, D]); w_q = din("w_q", [2, D, D]); q_norm = din("q_norm", [2, 64])
    lq1 = din("lambda_q1", [2, 64]); lk1 = din("lambda_k1", [2, 64]); lq2 = din("lambda_q2", [2, 64]); lk2 = din("lambda_k2", [2, 64])
    sub_norm = din("sub_norm", [2, 128]); w_o = din("w_o", [2, D, D])
    ffn_norm = din("ffn_norm", [4, D]); w_gate = din("w_gate", [4, D, DFF]); w_up = din("w_up", [4, D, DFF])
    w_down = din("w_down", [4, DFF, D])
    yp = dout("yp", [P, D]); csp = dout("csp", [2, 30, D]); css = dout("css", [2, NBS, 30, D])
    kp = dout("kp", [P, D]); vp = dout("vp", [P, D]); ks = dout("ks", [NS, D]); vs = dout("vs", [NS, D])
    if fused:
        ys = dout("ys", [NS, D])
        ckv = din("ckv", [NH * npool * 128, 256]); ptab = din("ptab", [NBS * NPG], I32)
        qs = pr.dint("qs", [NS, D]); ksi = pr.dint("ksi", [NS, D]); vsi = pr.dint("vsi", [NS, D]); osd = pr.dint("osd", [NS, D])
    else:
        hs1 = dout("hs1", [NS, D]); q2 = dout("q2", [NS, D])
    KTs = pr.dint("KTs", [NH, 128, P], BF16); Vsc = pr.dint("Vsc", [P, D], BF16); tbl = pr.dint("tbl", [8, 512])

    h, b_h, xn, b_xn = pr.h, pr.b_h, pr.xn, pr.b_xn
    X, R = pr.X, pr.R
    identb = pr.sb("identb", [128, 128], BF16)
    cx.op("dve", lambda e: e.tensor_copy(out=identb, in_=pr.ident), reads=[pr.b_const], writes=[pr.b_const])
    anti = pr.sb("anti_s", [128, 128])
    cx.dma("sp", anti, anti_d, writes=[pr.b_const])

    pfm = {}
    for nm, ap, n in (("cn0", conv_norm[0], 8), ("cn1", conv_norm[1], 8), ("b10", b_pw1[0], 16), ("b11", b_pw1[1], 16),
                      ("bd0", b_dw[0], 8), ("bd1", b_dw[1], 8), ("cm0", conv_mid[0], 8), ("cm1", conv_mid[1], 8),
                      ("b20", b_pw2[0], 8), ("b21", b_pw2[1], 8), ("kvn", kv_norm, 8), ("an0", attn_norm[0], 8),
                      ("an1", attn_norm[1], 8), ("fn0", ffn_norm[0], 8), ("fn1", ffn_norm[1], 8),
                      ("fn2", ffn_norm[2], 8), ("fn3", ffn_norm[3], 8)):
        pfm[nm] = pr.param_fm(nm, ap, n)

    xin = [(pr.carve(X, 0, [D], F32), Buf()), (pr.carve(X, 4096, [D], F32), Buf())]
    ti = 0
    for a in range(0, P, 128):
        xi, xb = xin[ti % 2]; ti += 1
        pr.load_tm(xp[a:a + 128, :], 128, h, b_h, a, xi, xb)
    xi, xb = xin[ti % 2]; ti += 1
    pr.load_tm(xs, NS, h, b_h, P, xi, xb)

    all_tiles = tiles(0, P) + [(P, NS)]

    wdw_t = pr.sb("wdw", [128, KC, CW])
    for l in range(2):
        cx.barrier()
        cn, cnb = pfm["cn%d" % l]; b1, b1b = pfm["b1%d" % l]; bd, bdb = pfm["bd%d" % l]
        cm, cmb = pfm["cm%d" % l]; b2, b2b = pfm["b2%d" % l]
        dg = [(pr.carve(X, 0, [CW, 128], BF16), Buf()), (pr.carve(X, 7936, [CW, 128], BF16), Buf())]
        glu = [(pr.carve(X, 15872, [512], F32), Buf())]
        sig = [(pr.carve(X, 17920, [512], F32), Buf())]
        stA = pr.carve(X, 19968, [KC, 30 + NS], F32); b_stA = Buf()
        sqc = pr.carve(X, 19968 + KC * (30 + NS) * 4, [KC, 256], BF16); b_sqc = Buf()
        tmpc = [(pr.carve(X, 29120, [256], F32), Buf())]
        u = pr.carve(R, 0, [KC, UL], BF16); b_u = pr.b_R
        wdr = pr.carve(X, 0, [D], F32)
        b_wdr = Buf()
        wdw = wdw_t; b_wdw = Buf()
        cx.dma("sp", wdr[:CW, :], w_dw[l], writes=[b_wdr])
        for k0 in range(0, KC, 4):
            pt, pb = pr.pget()
            for j in range(4):
                cx.op("pe", lambda e, j=j: e.transpose(pt[:, j * 128:j * 128 + CW], wdr[:CW, (k0 + j) * 128:(k0 + j + 1) * 128],
                                                       pr.ident[:CW, :CW]), reads=[b_wdr, pr.b_const], writes=[pb], signal=(j == 3))
            cx.op("act", lambda e: e.activation(out=wdw[:, k0:k0 + 4, :], in_=pt[:, :].rearrange("p (a b) -> p a b", a=4)[:, :, :CW],
                                               func=AF.Copy), reads=[pb], writes=[b_wdw])
        cx.barrier()
        pr.rmsnorm(cn, cnb, 0, T)
        cx.op("pool", lambda e: e.memset(u[:, :, 0:30], 0.0), writes=[b_u])
        for g4 in range(NBS // 4):
            xi, xb = xin[ti % 2]; ti += 1
            xi = xin[1][0]; xb = xin[1][1]
            cx.dma("sp", xi[:120, :], sc[l, g4 * 4:(g4 + 1) * 4].rearrange("b s f -> (b s) f"), writes=[xb])
            for k0 in range(0, KC, 4):
                pt, pb = pr.pget()
                for j in range(4):
                    cx.op("pe", lambda e, j=j: e.transpose(pt[:, j * 128:j * 128 + 120], xi[:120, (k0 + j) * 128:(k0 + j + 1) * 128],
                                                           pr.ident[:120, :120]), reads=[xb, pr.b_const], writes=[pb], signal=(j == 3))
                for j in range(4):
                    dstv = u[:, k0 + j, UP + g4 * 4 * SL:UP + (g4 + 1) * 4 * SL].rearrange("p (b s) -> p b s", s=SL)[:, :, 0:30]
                    cx.op("act", lambda e, j=j, dstv=dstv: e.activation(
                        out=dstv, in_=pt[:, j * 128:j * 128 + 120].rearrange("p (b s) -> p b s", s=30), func=AF.Copy),
                        reads=[pb], writes=[b_u])
        W1 = w_pw1[l]
        nblk = D // 256
        fetched = []

        def fetch1(i):
            fetched.append((pr.wfetch(W1, i * 256, 256, KC), pr.wfetch(W1, D + i * 256, 256, KC)))
        fetch1(0)
        for i in range(nblk):
            if i + 1 < nblk:
                fetch1(i + 1)
            (wa, abf), (wg, gbf) = fetched[i]
            for m in range(2):
                mc = i * 2 + m
                for (a, n) in all_tiles:
                    pa, pab = pr.pget()
                    pg, pgb = pr.pget()
                    for k in range(KC):
                        cx.op("pe", lambda e, k=k: e.matmul(pa[:, :n], lhsT=wa[:, k, m * 128:(m + 1) * 128], rhs=xn[:, k, a:a + n],
                                                           start=(k == 0), stop=(k == KC - 1)), reads=[abf, b_xn], writes=[pab], signal=(k == KC - 1))
                    for k in range(KC):
                        cx.op("pe", lambda e, k=k: e.matmul(pg[:, :n], lhsT=wg[:, k, m * 128:(m + 1) * 128], rhs=xn[:, k, a:a + n],
                                                           start=(k == 0), stop=(k == KC - 1)), reads=[gbf, b_xn], writes=[pgb], signal=(k == KC - 1))
                    sg, sgb = sig[0]
                    gl, glb = glu[0]
                    cx.op("act", lambda e: e.activation(out=sg[:, :n], in_=pg[:, :n], func=AF.Sigmoid, bias=b1[:, 8 + mc:9 + mc]),
                          reads=[pgb, b1b], writes=[sgb])
                    cx.op("dve", lambda e: e.scalar_tensor_tensor(out=gl[:, :n], in0=pa[:, :n], scalar=b1[:, mc:mc + 1], in1=sg[:, :n],
                                                                 op0=ALU.add, op1=ALU.mult), reads=[pab, sgb, b1b], writes=[glb])
                    if a < P:
                        cx.op("pool", lambda e: e.tensor_copy(out=u[:, mc, 30 + a:30 + a + n], in_=gl[:, :n]), reads=[glb], writes=[b_u])
                        if a + n == P:
                            cx.op("pool", lambda e: e.tensor_copy(out=stA[:, mc, 0:30], in_=gl[:, n - 30:n]), reads=[glb], writes=[b_stA])
                    else:
                        dstv = u[:, mc, UP:UP + NBS * SL].rearrange("p (b s) -> p b s", s=SL)[:, :, 30:SL]
                        cx.op("pool", lambda e, dstv=dstv: e.tensor_copy(out=dstv, in_=gl[:, :NS].rearrange("p (b t) -> p b t", t=DS)),
                              reads=[glb], writes=[b_u])
                        cx.op("pool", lambda e: e.tensor_copy(out=stA[:, mc, 30:30 + NS], in_=gl[:, :NS]), reads=[glb], writes=[b_stA])
        stg, stgb = xin[1]
        for k0 in range(0, KC, 4):
            pt, pb = pr.pget()
            for j in range(4):
                cx.op("pe", lambda e, j=j: e.transpose(pt[:30, j * 128:(j + 1) * 128], stA[:, k0 + j, 0:30], pr.ident),
                      reads=[b_stA, pr.b_const], writes=[pb], signal=(j == 3))
            cx.op("dve", lambda e: e.tensor_copy(out=stg[:30, k0 * 128:(k0 + 4) * 128], in_=pt[:30, :]), reads=[pb], writes=[stgb])
        cx.dma("sp", csp[l], stg[:30, :], reads=[stgb])
        for k0 in range(0, KC, 4):
            pt, pb = pr.pget()
            for j in range(4):
                cx.op("pe", lambda e, j=j: e.transpose(pt[:, j * 128:(j + 1) * 128], stA[:, k0 + j, 30:30 + NS], pr.ident),
                      reads=[b_stA, pr.b_const], writes=[pb], signal=(j == 3))
            cx.op("dve", lambda e: e.tensor_copy(out=stg[:, k0 * 128:(k0 + 4) * 128], in_=pt[:, :]), reads=[pb], writes=[stgb])
        for b in range(NBS):
            cx.dma("sp", css[l, b, 22:30, :], stg[b * DS:(b + 1) * DS, :], reads=[stgb])
        cx.dma("sp", css[l, :, 0:22, :], sc[l, :, 8:30, :])
        cx.barrier()
        ctiles = [(256 * i, 256, 256, 256 * i, None) for i in range(P // 256)]
        for b0 in range(0, NBS, 6):
            nb = min(6, NBS - b0)
            ctiles.append((UP + SL * b0, SL * nb - 30, DS * nb, P + DS * b0, nb))
        di = 0
        for (o0, nmm, ntok, tok0, nb) in ctiles:
            cb4 = [pr.pget() for _ in range(4)]

            def view(kc):
                bt, _ = cb4[kc // 2]
                off = (kc % 2) * 256
                if nb is None:
                    return bt[:, off:off + 256]
                return bt[:, off:off + SL * nb].rearrange("p (b s) -> p b s", s=SL)[:, :, 0:DS]

            def shp(ap2):
                if nb is None:
                    return ap2
                return ap2.rearrange("p (b t) -> p b t", t=DS)
            for kc in range(KC):
                bt, btb = cb4[kc // 2]
                off = (kc % 2) * 256
                dgt, dgb = dg[di % 2]; di += 1
                cx.op("pool", lambda e, kc=kc, dgt=dgt: e.tensor_tensor(
                    out=dgt, in0=identb.unsqueeze(1).to_broadcast([128, CW, 128]),
                    in1=wdw[:, kc, :].unsqueeze(2).to_broadcast([128, CW, 128]), op=ALU.mult),
                    reads=[pr.b_const, b_wdw], writes=[dgb])
                for j in range(CW):
                    cx.op("pe", lambda e, kc=kc, j=j, dgt=dgt: e.matmul(bt[:, off:off + nmm], lhsT=dgt[:, j, :],
                                                                     rhs=u[:, kc, o0 + j:o0 + j + nmm], start=(j == 0), stop=(j == CW - 1)),
                          reads=[dgb, b_u], writes=[btb], signal=(j == CW - 1))
                cx.op("act", lambda e, kc=kc: e.activation(out=shp(sqc[:, kc, :ntok]), in_=view(kc), func=AF.Square, bias=bd[:, kc:kc + 1]),
                      reads=[btb, bdb], writes=[b_sqc])
            ps_, psb = pr.pget()
            for kc in range(KC):
                cx.op("pe", lambda e, kc=kc: e.matmul(ps_[:, :ntok], lhsT=pr.ones, rhs=sqc[:, kc, :ntok], start=(kc == 0), stop=(kc == KC - 1)),
                      reads=[b_sqc, pr.b_const], writes=[psb], signal=(kc == KC - 1))
            pr.rstd_from(ps_, psb, ntok, 1.0 / D)
            for kc in range(KC):
                bt, btb = cb4[kc // 2]
                tm, tmb = tmpc[0]
                cx.op("dve", lambda e, kc=kc: e.scalar_tensor_tensor(out=shp(tm[:, :ntok]), in0=view(kc), scalar=bd[:, kc:kc + 1],
                                                                    in1=shp(pr.ms[:, :ntok]), op0=ALU.add, op1=ALU.mult),
                      reads=[btb, bdb, pr.b_ms], writes=[tmb])
                cx.op("act", lambda e, kc=kc: e.activation(out=xn[:, kc, tok0:tok0 + ntok], in_=tm[:, :ntok], func=AF.Silu, scale=cm[:, kc:kc + 1]),
                      reads=[tmb, cmb], writes=[b_xn])

        def epi_pw2(m, a, n, pt, pb):
            cx.op("dve", lambda e: e.scalar_tensor_tensor(out=h[:, m, a:a + n], in0=pt[:, :n], scalar=b2[:, m:m + 1], in1=h[:, m, a:a + n],
                                                         op0=ALU.add, op1=ALU.add), reads=[pb, b2b, b_h], writes=[b_h])
        pr.linear(xn, b_xn, KC, w_pw2[l], D, all_tiles, epi_pw2)
        cx.barrier()
        sgt = [(pr.carve(X, 0, [512], F32), Buf()), (pr.carve(X, 2048, [512], F32), Buf())]
        fn, fnb = pfm["fn%d" % l]
        pr.ffn(fn, fnb, w_gate[l], w_up[l], w_down[l], 0, T, sgt)

    cx.barrier()
    stg, stgb = xin[1]
    if not fused:
        pr.store_tm(h, b_h, P, NS, hs1, stg, stgb)

    wres = pr.carve(X, 0, [2, KC, 512], BF16); b_wres = Buf()
    kout = pr.carve(X, 16384, [D], F32); b_kout = Buf()
    sqk = pr.carve(X, 20480, [512], F32); b_sqk = Buf()
    tmpk = pr.carve(X, 22528, [512], F32); b_tmpk = Buf()
    vb = pr.carve(X, 24576, [D], BF16); b_vb = Buf()
    ktile = pr.carve(X, 26624, [NH, 128], BF16); b_ktile = Buf()
    Bp = pr.sb("Bp", [128, NH, 240], BF16); b_Bp = Buf()
    pTs = [(pr.carve(X, 1024 * i, [512], BF16), Buf()) for i in range(4)]
    o1n = pr.carve(X, 4096, [128], F32); b_o1n = Buf()
    odf = pr.carve(X, 4608, [128], F32); b_odf = Buf()
    onr = pr.carve(X, 5120, [128], F32); b_onr = Buf()
    junk = pr.carve(X, 5632, [128], F32); b_junk = Buf()
    QT = pr.carve(R, 0, [NH, P], BF16); b_QT = pr.b_R
    qoff = NH * P * 2
    KTh = pr.carve(R, qoff, [P], BF16); b_KTh = Buf()
    va = pr.carve(R, qoff + P * 2, [P // 128, 130], BF16); b_va = Buf()
    ssk = pr.sb("ssk", [128, 8]); b_ssk = Buf()
    small = pr.sb("small", [128, 8]); b_small = Buf()

    gk, gkb = pr.param_bc("gk", k_norm, 64)
    gq = [pr.param_bc("gq%d" % j, q_norm[j], 64) for j in range(2)]
    gsub = [pr.param_bc("gsub%d" % j, sub_norm[j], 128) for j in range(2)]
    bfar, bfarb = pr.param_bc("bfar", rel_bias[31], 8)
    lam = pr.sb("lam", [128, 4]); b_lam = Buf()
    lt = pr.carve(X, 4096, [4, 64], F32); b_lt = Buf()
    for j in range(2):
        with nc.allow_non_contiguous_dma(reason="bcast"):
            for i, src in enumerate((lq1[j], lk1[j], lq2[j], lk2[j])):
                cx.dma("sp", lt[:, i, :], src.partition_broadcast(128), writes=[b_lt])
        cx.op("dve", lambda e: e.tensor_tensor(out=lt[:, 0, :], in0=lt[:, 0, :], in1=lt[:, 1, :], op=ALU.mult), reads=[b_lt], writes=[b_lt])
        cx.op("dve", lambda e: e.tensor_tensor(out=lt[:, 2, :], in0=lt[:, 2, :], in1=lt[:, 3, :], op=ALU.mult), reads=[b_lt], writes=[b_lt])
        cx.op("dve", lambda e: e.tensor_reduce(out=small[:, 0:1], in_=lt[:, 0, :], axis=AX.X, op=ALU.add), reads=[b_lt], writes=[b_small])
        cx.op("dve", lambda e: e.tensor_reduce(out=small[:, 1:2], in_=lt[:, 2, :], axis=AX.X, op=ALU.add), reads=[b_lt], writes=[b_small])
        cx.op("act", lambda e: e.activation(out=small[:, 0:2], in_=small[:, 0:2], func=AF.Exp), reads=[b_small], writes=[b_small])
        cx.op("dve", lambda e, j=j: e.scalar_tensor_tensor(out=lam[:, j:j + 1], in0=small[:, 1:2], scalar=-lambda_init(2 + j),
                                                          in1=small[:, 0:1], op0=ALU.add, op1=ALU.subtract),
              reads=[b_small], writes=[b_lam])
        g_, gb_ = gsub[j]
        cx.op("dve", lambda e, j=j, g_=g_: e.tensor_scalar(out=g_, in0=g_, scalar1=1.0 - lambda_init(2 + j), scalar2=None, op0=ALU.mult),
              reads=[gb_], writes=[gb_])

    cx.barrier()
    tblS = pr.carve(X, 8192, [512], F32)[0:8, :]; b_tbl = Buf()
    nfar = pr.sb("nfar", [8, 1])
    cx.op("dve", lambda e: e.memset(tblS, 0.0), writes=[b_tbl])
    rbT = pr.sb("rbT", [8, 32])
    with nc.allow_non_contiguous_dma(reason="bias table"):
        cx.dma("sp", rbT, rel_bias.rearrange("b h -> h b"), writes=[b_tbl])
    for (bk, d0, d1) in bucket_runs(384):
        cx.op("dve", lambda e, bk=bk, d0=d0, d1=d1: e.tensor_copy(out=tblS[:, 127 + d0:128 + d1],
                                                                 in_=rbT[:, bk:bk + 1].to_broadcast([8, d1 - d0 + 1])),
              reads=[b_tbl], writes=[b_tbl])
    cx.op("dve", lambda e: e.tensor_scalar(out=nfar, in0=rbT[:, 31:32], scalar1=-1.0, scalar2=None, op0=ALU.mult), reads=[b_tbl], writes=[b_tbl])
    cx.op("act", lambda e: e.activation(out=tblS[:, 127:512], in_=tblS[:, 127:512], func=AF.Exp, bias=nfar[:, 0:1]), reads=[b_tbl], writes=[b_tbl])
    b_tbld = Buf()
    cx.dma("sp", tbl, tblS, reads=[b_tbl], writes=[b_tbld])
    for hd in range(NH):
        hk = pr.carve(X, 0, [240], F32)
        cx.dma("sp", hk, bass.AP(tensor=tbl.tensor, offset=hd * 512, ap=[[1, 128], [1, 240]]), reads=[b_tbld], writes=[b_wres])
        pt, pb = pr.pget()
        cx.op("pe", lambda e: e.matmul(pt[:, :240], lhsT=anti, rhs=hk, start=True, stop=True), reads=[b_wres, pr.b_const], writes=[pb])
        cx.op("act", lambda e, hd=hd: e.activation(out=Bp[:, hd, :], in_=pt[:, :240], func=AF.Copy), reads=[pb], writes=[b_Bp])
    cx.op("dve", lambda e: e.memset(va[:, :, 128:130], 1.0), writes=[b_va])
    if fused:
        Bs_all = pr.sb("Bs_all", [128, NH, 8]); Bn_all = pr.sb("Bn_all", [8, NH, 8]); b_Bsn = Buf()
        for hd in range(NH):
            hs_ = pr.carve(X, 0, [8], F32); hn_ = pr.carve(X, 1024, [8], F32)[0:8, :]
            bh_ = Buf()
            with nc.allow_non_contiguous_dma(reason="hankel"):
                cx.dma("sp", hs_, bass.AP(tensor=tbl.tensor, offset=hd * 512 + 128, ap=[[1, 128], [1, 8]]), reads=[b_tbld], writes=[bh_, b_wres])
                cx.dma("sp", hn_, bass.AP(tensor=tbl.tensor, offset=hd * 512 + 120, ap=[[1, 8], [1, 8]]), reads=[b_tbld], writes=[bh_, b_wres])
            pt, pb = pr.pget()
            cx.op("pe", lambda e: e.matmul(pt[:, 0:8], lhsT=anti, rhs=hs_, start=True, stop=True), reads=[bh_, pr.b_const], writes=[pb])
            cx.op("act", lambda e, hd=hd: e.activation(out=Bs_all[:, hd, :], in_=pt[:, 0:8], func=AF.Copy), reads=[pb], writes=[b_Bsn])
            pt, pb = pr.pget()
            cx.op("pe", lambda e: e.matmul(pt[0:8, 0:8], lhsT=anti[0:8, 120:128], rhs=hn_, start=True, stop=True), reads=[bh_, pr.b_const], writes=[pb])
            cx.op("act", lambda e, hd=hd: e.activation(out=Bn_all[:, hd, :], in_=pt[0:8, 0:8], func=AF.Copy), reads=[pb], writes=[b_Bsn])
    cx.barrier()

    def proj_tm(Wcols, g64, g64b, tok_list, sink):
        for hf in range(2):
            cx.dma("pool", wres[:, hf, :, :], Wcols[:, hf * 512:(hf + 1) * 512].rearrange("(k p) c -> p k c", p=128), writes=[b_wres])
        for a in tok_list:
            for hf in range(2):
                pt, pb = pr.pget()
                for k in range(KC):
                    cx.op("pe", lambda e, k=k: e.matmul(pt[:, :], lhsT=xn[:, k, a:a + 128], rhs=wres[:, hf, k, :],
                                                       start=(k == 0), stop=(k == KC - 1)), reads=[b_xn, b_wres], writes=[pb], signal=(k == KC - 1))
                ko = kout[:, hf * 512:(hf + 1) * 512]
                if g64 is None:
                    cx.op("act", lambda e: e.activation(out=ko, in_=pt[:, :], func=AF.Copy), reads=[pb], writes=[b_kout])
                else:
                    cx.op("act", lambda e: e.activation(out=sqk, in_=pt[:, :], func=AF.Square), reads=[pb], writes=[b_sqk])
                    cx.op("dve", lambda e: e.tensor_reduce(out=ssk, in_=sqk.rearrange("p (g d) -> p g d", d=64), axis=AX.X, op=ALU.add),
                          reads=[b_sqk], writes=[b_ssk])
                    cx.op("dve", lambda e: e.tensor_scalar(out=ssk, in0=ssk, scalar1=1.0 / 64, scalar2=EPS, op0=ALU.mult, op1=ALU.add),
                          reads=[b_ssk], writes=[b_ssk])
                    cx.op("pool", lambda e: e.tensor_tensor(out=ssk, in0=ssk, in1=pr.nhalf[:, 0:8], op=ALU.pow),
                          reads=[b_ssk, pr.b_const], writes=[b_ssk])
                    cx.op("dve", lambda e: e.tensor_tensor(out=tmpk.rearrange("p (g d) -> p g d", d=64),
                                                          in0=pt[:, :].rearrange("p (g d) -> p g d", d=64),
                                                          in1=ssk.unsqueeze(2).to_broadcast([128, 8, 64]), op=ALU.mult),
                          reads=[pb, b_ssk], writes=[b_tmpk])
                    cx.op("dve", lambda e: e.tensor_tensor(out=ko.rearrange("p (g d) -> p g d", d=64),
                                                          in0=tmpk.rearrange("p (g d) -> p g d", d=64),
                                                          in1=g64.unsqueeze(1).to_broadcast([128, 8, 64]), op=ALU.mult),
                          reads=[b_tmpk, g64b], writes=[b_kout])
            sink(a)

    def head_transposes(a, dst_fn, dstb):
        for h0 in range(0, NH, 4):
            pt, pb = pr.pget()
            for j in range(4):
                cx.op("pe", lambda e, j=j: e.transpose(pt[:, j * 128:(j + 1) * 128], kout[:, (h0 + j) * 128:(h0 + j + 1) * 128], pr.ident),
                      reads=[b_kout, pr.b_const], writes=[pb], signal=(j == 3))
            cx.op("act", lambda e: e.activation(out=dst_fn(h0), in_=pt[:, :].rearrange("p (a b) -> p a b", a=4), func=AF.Copy),
                  reads=[pb], writes=[dstb])


    def sample_attention(j):
        NPAIR = NBS * NH
        idxh = pr.carve(R, 0, [NH, NBS * NPG], I32); b_idx = Buf()
        Qblk = pr.carve(R, 8192, [NPAIR, 16], BF16); b_Q = Buf()
        KnT = pr.carve(R, 12288, [NH, NS], BF16); b_KnT = Buf()
        kvs = [(pr.carve(R, 14336 + 516 * i, [258], BF16), Buf()) for i in range(32)]
        ktT = [(pr.carve(R, 30848 + 4096 * i, [NPG, 128], BF16), Buf()) for i in range(2)]
        pTt = [(pr.carve(R, 39040 + 512 * i, [256], BF16), Buf()) for i in range(2)]
        xq = pr.carve(X, 0, [D], F32); b_xq = Buf()
        pNt = [(pr.carve(X, 4096 + 64 * i, [16], BF16)[0:8, :], Buf()) for i in range(2)]
        Vnt = [(pr.carve(X, 4352 + 264 * i, [130], BF16)[0:8, :], Buf()) for i in range(4)]
        t1s = [(pr.carve(X, 8192 + 512 * i, [128], F32)[0:8, :], Buf()) for i in range(2)]
        t2s = [(pr.carve(X, 9216 + 512 * i, [128], F32)[0:8, :], Buf()) for i in range(2)]
        t3s = [(pr.carve(X, 10240 + 512 * i, [128], F32)[0:8, :], Buf()) for i in range(2)]
        jk = pr.carve(X, 11264, [128], F32)[0:8, :]; b_jk = Buf()
        sm2 = [(pr.carve(X, 11776 + 32 * i, [8], F32)[0:8, :], Buf()) for i in range(2)]
        otm = pr.carve(X, 16384, [KC, NS], F32); b_otm = Buf()
        pts = pr.carve(X, 20480, [NBS * NPG], I32); b_pts = Buf()
        ioti = pr.carve(X, 21504, [NH], I32); iotf = pr.carve(X, 21568, [NH], F32); b_io = Buf()
        with nc.allow_non_contiguous_dma(reason="bcast"):
            cx.dma("sp", pts, ptab.partition_broadcast(128), writes=[b_pts])
        iotp = pr.carve(X, 21632, [1], I32); iotpf = pr.carve(X, 21664, [1], F32)
        cx.op("pool", lambda e: e.iota(ioti, pattern=[[1, NH]], base=0, channel_multiplier=0), writes=[b_io])
        cx.op("pool", lambda e: e.iota(iotp, pattern=[[0, 1]], base=0, channel_multiplier=1), writes=[b_io])
        cx.op("dve", lambda e: e.tensor_copy(out=iotf, in_=ioti), reads=[b_io], writes=[b_io])
        cx.op("dve", lambda e: e.tensor_copy(out=iotpf, in_=iotp), reads=[b_io], writes=[b_io])
        cx.op("dve", lambda e: e.tensor_scalar(out=iotf, in0=iotf, scalar1=float(npool * 128), scalar2=iotpf[:, 0:1],
                                              op0=ALU.mult, op1=ALU.add), reads=[b_io], writes=[b_io])
        for hd in range(NH):
            cx.op("dve", lambda e, hd=hd: e.tensor_scalar(out=idxh[:, hd, :], in0=pts, scalar1=128.0, scalar2=iotf[:, hd:hd + 1],
                                                         op0=ALU.mult, op1=ALU.add), reads=[b_pts, b_io], writes=[b_idx])
        for t_, bf_ in kvs:
            cx.op("dve", lambda e, t_=t_: e.memset(t_[:, 256:258], 1.0), writes=[bf_])
        for t_, bf_ in Vnt:
            cx.op("dve", lambda e, t_=t_: e.memset(t_[:, 128:130], 1.0), writes=[bf_])
        cx.op("dve", lambda e: e.memset(Qblk, 0.0), writes=[b_Q])
        cx.dma("sp", xq, qs, writes=[b_xq])
        for hd in range(NH):
            pt, pb = pr.pget()
            cx.op("pe", lambda e, hd=hd: e.transpose(pt[:, 0:128], xq[:, hd * 128:(hd + 1) * 128], pr.ident), reads=[b_xq, pr.b_const], writes=[pb])
            for c in range(2):
                cx.op("act", lambda e, c=c, hd=hd: e.activation(out=Qblk[c * 64:(c + 1) * 64, hd * NBS:(hd + 1) * NBS, c * 8:(c + 1) * 8],
                                                                in_=pt[c * 64:(c + 1) * 64, 0:128].rearrange("p (b t) -> p b t", t=DS), func=AF.Copy),
                      reads=[pb], writes=[b_Q])
        cx.dma("sp", xq, ksi, writes=[b_xq])
        for hd in range(NH):
            pt, pb = pr.pget()
            cx.op("pe", lambda e, hd=hd: e.transpose(pt[:, 0:128], xq[:, hd * 128:(hd + 1) * 128], pr.ident), reads=[b_xq, pr.b_const], writes=[pb])
            cx.op("act", lambda e, hd=hd: e.activation(out=KnT[:, hd, :], in_=pt[:, 0:128], func=AF.Copy), reads=[pb], writes=[b_KnT])
        gs_, gsb_ = gsub[j]
        pi = 0
        for hd in range(NH):
            for b in range(NBS):
                pair = hd * NBS + b
                sl = []
                for pg in range(NPG):
                    t_, bf_ = kvs[(pair * NPG + pg) % 32]
                    cx.dma("pool", t_[:, 0:256], ckv, reads=[b_idx], writes=[bf_], indirect=idxh[:, hd, b * NPG + pg:b * NPG + pg + 1])
                    sl.append((t_, bf_))
                vn_, vnb = Vnt[pair % 4]
                cx.dma("pool", vn_[:, 0:128], vsi[b * DS:(b + 1) * DS, hd * 128:(hd + 1) * 128], writes=[vnb])
                kt_, ktb = ktT[pair % 2]
                for g in range(4):
                    pt, pb = pr.pget()
                    ptb = pt.bitcast(BF16)
                    for jj in range(4):
                        t_, bf_ = sl[g * 4 + jj]
                        cx.op("pe", lambda e, jj=jj, t_=t_: e.transpose(ptb[:, jj * 128:(jj + 1) * 128], t_[:, 0:128], identb),
                              reads=[bf_, pr.b_const], writes=[pb], signal=(jj == 3))
                    if g % 2 == 0:
                        cx.op("act", lambda e, g=g: e.activation(out=kt_[:, g * 4:(g + 1) * 4, :], in_=ptb[:, 0:512].rearrange("p (a b) -> p a b", a=4), func=AF.Copy),
                              reads=[pb], writes=[ktb])
                    else:
                        cx.op("dve", lambda e, g=g: e.tensor_copy(out=kt_[:, g * 4:(g + 1) * 4, :], in_=ptb[:, 0:512].rearrange("p (a b) -> p a b", a=4)),
                              reads=[pb], writes=[ktb])
                sp_, spb = pr.pget()
                for pg in range(NPG):
                    cx.op("pe", lambda e, pg=pg: e.matmul(sp_[:, pg * 16:(pg + 1) * 16], lhsT=kt_[:, pg, :], rhs=Qblk[:, pair, :], start=True, stop=True),
                          reads=[ktb, b_Q], writes=[spb], signal=False)
                cx.op("pe", lambda e: e.matmul(sp_[0:8, 256:272], lhsT=KnT[:, hd, b * DS:(b + 1) * DS], rhs=Qblk[:, pair, :], start=True, stop=True),
                      reads=[b_KnT, b_Q], writes=[spb])
                pT, pTb = pTt[pair % 2]
                pN, pNb = pNt[pair % 2]
                cx.op("act", lambda e: e.activation(out=pT, in_=sp_[:, 0:256], func=AF.Exp, bias=bfar[:, hd:hd + 1], scale=0.125), reads=[spb, bfarb], writes=[pTb])
                cx.op("act", lambda e: e.activation(out=pN, in_=sp_[0:8, 256:272], func=AF.Exp, bias=bfar[0:8, hd:hd + 1], scale=0.125), reads=[spb, bfarb], writes=[pNb])
                cx.op("dve", lambda e: e.tensor_tensor(out=pT[:, 240:256].rearrange("p (c t) -> p c t", c=2), in0=pT[:, 240:256].rearrange("p (c t) -> p c t", c=2),
                                                      in1=Bs_all[:, hd, :].unsqueeze(1).to_broadcast([128, 2, 8]), op=ALU.mult), reads=[pTb, b_Bsn], writes=[pTb])
                cx.op("dve", lambda e: e.tensor_tensor(out=pN.rearrange("p (c t) -> p c t", c=2), in0=pN.rearrange("p (c t) -> p c t", c=2),
                                                      in1=Bn_all[:, hd, :].unsqueeze(1).to_broadcast([8, 2, 8]), op=ALU.mult), reads=[pNb, b_Bsn], writes=[pNb])
                ac, acb = pr.pget()
                for c in range(2):
                    for pg in range(NPG):
                        t_, bf_ = sl[pg]
                        cx.op("pe", lambda e, c=c, pg=pg, t_=t_: e.matmul(ac[0:8, c * 129:(c + 1) * 129], lhsT=pT[:, pg * 16 + c * 8:pg * 16 + c * 8 + 8],
                                                                         rhs=t_[:, 128:257], start=(pg == 0), stop=False),
                              reads=[pTb, bf_], writes=[acb], signal=False)
                    cx.op("pe", lambda e, c=c: e.matmul(ac[0:8, c * 129:(c + 1) * 129], lhsT=pN[:, c * 8:(c + 1) * 8], rhs=vn_[:, 0:129],
                                                       start=False, stop=True), reads=[pNb, vnb], writes=[acb])
                s2, s2b = sm2[pair % 2]
                t1, t1b = t1s[pair % 2]; t2, t2b = t2s[pair % 2]; t3, t3b = t3s[pair % 2]
                cx.op("dve", lambda e: e.reciprocal(out=s2[:, 0:2], in_=ac[0:8, 0:258].rearrange("p (c e) -> p c e", c=2)[:, :, 128]), reads=[acb], writes=[s2b])
                cx.op("dve", lambda e: e.tensor_tensor(out=s2[:, 2:3], in0=s2[:, 1:2], in1=lam[0:8, j:j + 1], op=ALU.mult), reads=[s2b, b_lam], writes=[s2b])
                cx.op("dve", lambda e: e.tensor_scalar(out=t1, in0=ac[0:8, 0:128], scalar1=s2[:, 0:1], scalar2=None, op0=ALU.mult), reads=[acb, s2b], writes=[t1b])
                cx.op("dve", lambda e: e.scalar_tensor_tensor(out=t2, in0=ac[0:8, 129:257], scalar=s2[:, 2:3], in1=t1, op0=ALU.mult, op1=ALU.add),
                      reads=[acb, s2b, t1b], writes=[t2b])
                cx.op("act", lambda e: e.activation(out=jk, in_=t2, func=AF.Square, accum_out=s2[:, 3:4]), reads=[t2b], writes=[b_jk, s2b])
                cx.op("dve", lambda e: e.tensor_scalar(out=s2[:, 3:4], in0=s2[:, 3:4], scalar1=1.0 / 128, scalar2=EPS, op0=ALU.mult, op1=ALU.add), reads=[s2b], writes=[s2b])
                cx.op("pool", lambda e: e.tensor_tensor(out=s2[:, 3:4], in0=s2[:, 3:4], in1=pr.nhalf[0:8, 0:1], op=ALU.pow), reads=[s2b, pr.b_const], writes=[s2b])
                cx.op("dve", lambda e: e.scalar_tensor_tensor(out=t3, in0=t2, scalar=s2[:, 3:4], in1=gs_[0:8, :], op0=ALU.mult, op1=ALU.mult),
                      reads=[t2b, s2b, gsb_], writes=[t3b])
                cx.dma("sp", osd[b * DS:(b + 1) * DS, hd * 128:(hd + 1) * 128], t3, reads=[t3b])
        cx.barrier()
        pr.load_tm(osd, NS, otm, b_otm, 0, xq, b_xq)
        for kc in range(KC):
            cx.op("dve", lambda e, kc=kc: e.tensor_copy(out=xn[:, kc, P:T], in_=otm[:, kc, :]), reads=[b_otm], writes=[b_xn])

    ptok = list(range(0, P, 128))
    for j in range(2):
        l = 2 + j
        if j == 0:
            kvn, kvnb = pfm["kvn"]
            pr.rmsnorm(kvn, kvnb, 0, T)

            def sink_k(a):
                if a < P:
                    cx.dma("sp", kp[a:a + 128, :], kout, reads=[b_kout])
                    head_transposes(a, lambda h0: ktile[:, h0:h0 + 4, :], b_ktile)
                    cx.dma("sp", KTs[:, :, a:a + 128].rearrange("h p t -> p h t"), ktile, reads=[b_ktile])
                else:
                    cx.dma("sp", ks, kout, reads=[b_kout])
                    if fused:
                        cx.dma("sp", ksi, kout, reads=[b_kout])
            proj_tm(w_kv[:, 0:D], gk, gkb, ptok + [P], sink_k)
            cx.barrier()

            def sink_v(a):
                if a < P:
                    cx.dma("sp", vp[a:a + 128, :], kout, reads=[b_kout])
                    cx.op("act", lambda e: e.activation(out=vb, in_=kout, func=AF.Copy), reads=[b_kout], writes=[b_vb])
                    cx.dma("sp", Vsc[a:a + 128, :], vb, reads=[b_vb])
                else:
                    cx.dma("sp", vs, kout, reads=[b_kout])
                    if fused:
                        cx.dma("sp", vsi, kout, reads=[b_kout])
            proj_tm(w_kv[:, D:2 * D], None, None, ptok + [P], sink_v)
            cx.barrier()
        b_scr = Buf()
        an, anb = pfm["an%d" % j]
        pr.rmsnorm(an, anb, 0, T if (j == 0 or fused) else P)

        def sink_q(a):
            if a < P:
                head_transposes(a, lambda h0: QT[:, h0:h0 + 4, a:a + 128], b_QT)
            else:
                cx.dma("sp", qs if fused else q2, kout, reads=[b_kout])
        gqj, gqjb = gq[j]
        proj_tm(w_q[j], gqj, gqjb, ptok + ([P] if (j == 0 or fused) else []), sink_q)
        cx.barrier()
        QC = 256
        acc = [pr.banks[6], pr.banks[7]]
        gs_, gsb_ = gsub[j]
        pti = 0
        for hd in range(NH):
            cx.dma("sp", KTh, KTs[hd], writes=[b_KTh])
            cx.dma("sp", va[:, :, 0:128], Vsc.rearrange("(kt p) (h e) -> p kt h e", p=128, h=NH)[:, :, hd, :], writes=[b_va])
            for q0 in range(0, P, QC):
                nqb = QC // 128
                for c in range(2):
                    cx.op("dve", lambda e, c=c: e.memset(acc[c][0][:, 0:nqb * 129], 0.0), writes=[acc[c][1]])
                for kt in range((q0 + QC) // 128):
                    ks_ = kt * 128
                    q_lo = max(q0, ks_)
                    n = q0 + QC - q_lo
                    for c in range(2):
                        pt, pb = pr.pget()
                        cx.op("pe", lambda e, c=c: e.matmul(pt[:, :n], lhsT=KTh[c * 64:(c + 1) * 64, ks_:ks_ + 128],
                                                           rhs=QT[c * 64:(c + 1) * 64, hd, q_lo:q_lo + n], start=True, stop=True),
                              reads=[b_KTh, b_QT], writes=[pb])
                        pT, pTb = pTs[pti % 4]; pti += 1
                        cx.op("act", lambda e, pT=pT: e.activation(out=pT[:, :n], in_=pt[:, :n], func=AF.Exp, bias=bfar[:, hd:hd + 1], scale=0.125),
                              reads=[pb, bfarb], writes=[pTb])
                        w0 = max(q_lo, ks_); w1 = min(q_lo + n, ks_ + 240)
                        if w1 > w0:
                            cx.op("dve", lambda e, pT=pT: e.tensor_tensor(out=pT[:, w0 - q_lo:w1 - q_lo], in0=pT[:, w0 - q_lo:w1 - q_lo],
                                                                        in1=Bp[:, hd, w0 - ks_:w1 - ks_], op=ALU.mult),
                                  reads=[pTb, b_Bp], writes=[pTb])
                        for qb in range((q_lo - q0) // 128, nqb):
                            gqb = q0 // 128 + qb
                            col = q0 + qb * 128 - q_lo
                            cx.op("pe", lambda e, c=c, qb=qb, col=col, pT=pT: e.matmul(
                                acc[c][0][:, qb * 129:(qb + 1) * 129], lhsT=pT[:, col:col + 128], rhs=va[:, kt, 0:129],
                                start=False, stop=(kt == gqb), skip_group_check=True), reads=[pTb, b_va], writes=[acc[c][1]])
                for qb in range(nqb):
                    a = q0 + qb * 128
                    o0_, o1_ = qb * 129, qb * 129 + 128
                    for c in range(2):
                        cx.op("dve", lambda e, c=c: e.reciprocal(out=small[:, c:c + 1], in_=acc[c][0][:, o1_:o1_ + 1]),
                              reads=[acc[c][1]], writes=[b_small])
                    cx.op("dve", lambda e: e.tensor_tensor(out=small[:, 2:3], in0=small[:, 1:2], in1=lam[:, j:j + 1], op=ALU.mult),
                          reads=[b_small, b_lam], writes=[b_small])
                    cx.op("dve", lambda e: e.tensor_scalar(out=o1n, in0=acc[0][0][:, o0_:o1_], scalar1=small[:, 0:1], scalar2=None, op0=ALU.mult),
                          reads=[acc[0][1], b_small], writes=[b_o1n])
                    cx.op("dve", lambda e: e.scalar_tensor_tensor(out=odf, in0=acc[1][0][:, o0_:o1_], scalar=small[:, 2:3], in1=o1n,
                                                                 op0=ALU.mult, op1=ALU.add), reads=[acc[1][1], b_small, b_o1n], writes=[b_odf])
                    cx.op("act", lambda e: e.activation(out=junk, in_=odf, func=AF.Square, accum_out=small[:, 3:4]),
                          reads=[b_odf], writes=[b_junk, b_small])
                    cx.op("dve", lambda e: e.tensor_scalar(out=small[:, 3:4], in0=small[:, 3:4], scalar1=1.0 / 128, scalar2=EPS,
                                                          op0=ALU.mult, op1=ALU.add), reads=[b_small], writes=[b_small])
                    cx.op("pool", lambda e: e.tensor_tensor(out=small[:, 3:4], in0=small[:, 3:4], in1=pr.nhalf[:, 0:1], op=ALU.pow),
                          reads=[b_small, pr.b_const], writes=[b_small])
                    cx.op("dve", lambda e: e.scalar_tensor_tensor(out=onr, in0=odf, scalar=small[:, 3:4], in1=gs_, op0=ALU.mult, op1=ALU.mult),
                          reads=[b_odf, b_small, gsb_], writes=[b_onr])
                    pt, pb = pr.pget()
                    cx.op("pe", lambda e: e.transpose(pt[:, 0:128], onr, pr.ident), reads=[b_onr, pr.b_const], writes=[pb])
                    cx.op("act", lambda e: e.activation(out=xn[:, hd, a:a + 128], in_=pt[:, 0:128], func=AF.Copy), reads=[pb], writes=[b_xn])
        cx.barrier()
        if fused:
            sample_attention(j)
            cx.barrier()
        pr.linear(xn, b_xn, KC, w_o[j], D, (tiles(0, P) + [(P, NS)]) if fused else tiles(0, P), pr.add_to_h)
        sgt = [(pr.carve(X, 0, [512], F32), Buf()), (pr.carve(X, 2048, [512], F32), Buf())]
        fn, fnb = pfm["fn%d" % l]
        pr.ffn(fn, fnb, w_gate[l], w_up[l], w_down[l], 0, T if fused else P, sgt)
        cx.barrier()
        if j == 0:
            cx.op("dve", lambda e: e.memset(va[:, :, 128:130], 1.0), writes=[b_va])
            cx.barrier()

    if fused:
        stg, stgb = xin[1]
        pr.store_tm(h, b_h, P, NS, ys, stg, stgb)
    for i, a in enumerate(range(0, P, 128)):
        stg, stgb = xin[i % 2]
        pr.store_tm(h, b_h, a, 128, yp[a:a + 128, :], stg, stgb)
    cx.finish("sp")
    return nc


def build_L3(with_q):
    T = NS
    pr = Prog(T, KC * T * 2, 32768)
    nc, cx = pr.nc, pr.cx
    din, dout = pr.din, pr.dout
    hs = din("hs", [NS, D]); oat = din("oat", [NS, D])
    w_o = din("w_o", [D, D]); ffn_norm = din("ffn_norm", [D])
    wg = din("w_gate", [D, DFF]); wu = din("w_up", [D, DFF]); wd = din("w_down", [DFF, D])
    hs_out = dout("hs_out", [NS, D])
    if with_q:
        attn_norm = din("attn_norm", [D]); w_q = din("w_q", [D, D]); q_norm = din("q_norm", [64])
        q_out = dout("q_out", [NS, D])
    h, b_h, xn, b_xn, X = pr.h, pr.b_h, pr.xn, pr.b_xn, pr.X
    xin = (pr.carve(X, 0, [D], F32), Buf())
    otm = pr.sb("otm", [128, KC, T]); b_otm = Buf()
    pr.load_tm(hs, NS, h, b_h, 0, xin[0], xin[1])
    pr.load_tm(oat, NS, otm, b_otm, 0, xin[0], xin[1])
    for kc in range(KC):
        cx.op("dve", lambda e, kc=kc: e.tensor_copy(out=xn[:, kc, :], in_=otm[:, kc, :]), reads=[b_otm], writes=[b_xn])
    pr.linear(xn, b_xn, KC, w_o, D, [(0, NS)], pr.add_to_h)
    fn, fnb = pr.param_fm("fn", ffn_norm, 8)
    sgt = [(pr.carve(X, 8192, [512], F32), Buf()), (pr.carve(X, 10240, [512], F32), Buf())]
    pr.ffn(fn, fnb, wg, wu, wd, 0, T, sgt)
    stg = (pr.carve(X, 4096, [D], F32), Buf())
    pr.store_tm(h, b_h, 0, NS, hs_out, stg[0], stg[1])
    if with_q:
        an, anb = pr.param_fm("an", attn_norm, 8)
        pr.rmsnorm(an, anb, 0, T)
        gq, gqb = pr.param_bc("gq", q_norm, 64)
        wres = pr.carve(X, 12288, [2, KC, 512], BF16); b_wres = Buf()
        kout = pr.carve(X, 28672, [D], F32); b_kout = Buf()
        sqk = pr.sb("sqk", [128, 512]); b_sqk = Buf()
        tmpk = pr.sb("tmpk", [128, 512]); b_tmpk = Buf()
        ssk = pr.sb("ssk", [128, 8]); b_ssk = Buf()
        for hf in range(2):
            cx.dma("pool", wres[:, hf, :, :], w_q[:, hf * 512:(hf + 1) * 512].rearrange("(k p) c -> p k c", p=128), writes=[b_wres])
        for hf in range(2):
            pt, pb = pr.pget()
            for k in range(KC):
                cx.op("pe", lambda e, k=k: e.matmul(pt[:, :], lhsT=xn[:, k, 0:128], rhs=wres[:, hf, k, :], start=(k == 0), stop=(k == KC - 1)),
                      reads=[b_xn, b_wres], writes=[pb], signal=(k == KC - 1))
            ko = kout[:, hf * 512:(hf + 1) * 512]
            cx.op("act", lambda e: e.activation(out=sqk, in_=pt[:, :], func=AF.Square), reads=[pb], writes=[b_sqk])
            cx.op("dve", lambda e: e.tensor_reduce(out=ssk, in_=sqk.rearrange("p (g d) -> p g d", d=64), axis=AX.X, op=ALU.add),
                  reads=[b_sqk], writes=[b_ssk])
            cx.op("dve", lambda e: e.tensor_scalar(out=ssk, in0=ssk, scalar1=1.0 / 64, scalar2=EPS, op0=ALU.mult, op1=ALU.add),
                  reads=[b_ssk], writes=[b_ssk])
            cx.op("pool", lambda e: e.tensor_tensor(out=ssk, in0=ssk, in1=pr.nhalf[:, 0:8], op=ALU.pow), reads=[b_ssk, pr.b_const], writes=[b_ssk])
            cx.op("dve", lambda e: e.tensor_tensor(out=tmpk.rearrange("p (g d) -> p g d", d=64), in0=pt[:, :].rearrange("p (g d) -> p g d", d=64),
                                                  in1=ssk.unsqueeze(2).to_broadcast([128, 8, 64]), op=ALU.mult), reads=[pb, b_ssk], writes=[b_tmpk])
            cx.op("dve", lambda e: e.tensor_tensor(out=ko.rearrange("p (g d) -> p g d", d=64), in0=tmpk.rearrange("p (g d) -> p g d", d=64),
                                                  in1=gq.unsqueeze(1).to_broadcast([128, 8, 64]), op=ALU.mult), reads=[b_tmpk, gqb], writes=[b_kout])
        cx.dma("sp", q_out, kout, reads=[b_kout])
    cx.finish("sp")
    return nc


def build_ATT(npool, layer):
    NB = 128
    nc = bass.Bass("TRN2", target_bir_lowering=False)
    cx = Ctx(nc)
    sb = lambda name, shape, dt=F32: nc.alloc_sbuf_tensor(name, list(shape), dt)[:]
    din = lambda name, shape, dt=F32: nc.dram_tensor(name, list(shape), dt, kind="ExternalInput").ap()
    ckv = din("ckv", [npool * 128, 256]); ptab = din("ptab", [NB * NPG], I32)
    qd = din("q", [NB * DS, 128]); knd = din("kn", [NB * DS, 128]); vnd = din("vn", [NB * DS, 128])
    rb = din("rb", [32]); lq1 = din("lq1", [64]); lk1 = din("lk1", [64]); lq2 = din("lq2", [64]); lk2 = din("lk2", [64])
    gsd = din("gsub", [128]); ident_d = din("ident", [128, 128]); anti_d = din("anti", [128, 128])
    od = nc.dram_tensor("o", [NB * DS, 128], F32, kind="ExternalOutput").ap()
    tbl = nc.dram_tensor("tbl", [512], F32, kind="Internal").ap()
    banks = [(nc.alloc_psum_tensor("ps%d" % i, [128, 512], F32)[:], Buf()) for i in range(8)]
    st = {"i": 0}

    def pget():
        t, b = banks[st["i"]]
        st["i"] = (st["i"] + 1) % 8
        return t, b
    b_c = Buf()
    ident = sb("ident_s", [128, 128]); anti = sb("anti_s", [128, 128]); identb = sb("identb", [128, 128], BF16)
    nhalf = sb("nhalf", [128, 128])
    xin = sb("xin", [128, 128]); b_xin = Buf()
    cx.dma("sp", ident, ident_d, writes=[b_c])
    cx.dma("sp", anti, anti_d, writes=[b_c])
    cx.op("dve", lambda e: e.tensor_copy(out=identb, in_=ident), reads=[b_c], writes=[b_c])
    cx.op("dve", lambda e: e.memset(nhalf, -0.5), writes=[b_c])
    pts = sb("pts", [128, NB * NPG], I32); b_pts = Buf()
    with nc.allow_non_contiguous_dma(reason="bcast"):
        cx.dma("sp", pts, ptab.partition_broadcast(128), writes=[b_pts])
    ioti = sb("ioti", [128, 1], I32); iotf = sb("iotf", [128, 1]); b_io = Buf()
    cx.op("pool", lambda e: e.iota(ioti, pattern=[[0, 1]], base=0, channel_multiplier=1), writes=[b_io])
    cx.op("dve", lambda e: e.tensor_copy(out=iotf, in_=ioti), reads=[b_io], writes=[b_io])
    idx = sb("idx", [128, NB * NPG], I32); b_idx = Buf()
    cx.op("dve", lambda e: e.tensor_scalar(out=idx, in0=pts, scalar1=128.0, scalar2=iotf[:, 0:1], op0=ALU.mult, op1=ALU.add),
          reads=[b_pts, b_io], writes=[b_idx])
    Qblk = sb("Qblk", [128, NB, 16], BF16); b_Q = Buf()
    KnT = sb("KnT", [128, NB * DS], BF16); b_KnT = Buf()
    cx.op("dve", lambda e: e.memset(Qblk, 0.0), writes=[b_Q])
    for i in range(NB * DS // 128):
        cx.dma("sp", xin, qd[i * 128:(i + 1) * 128, :], writes=[b_xin])
        pt, pb = pget()
        cx.op("pe", lambda e: e.transpose(pt[:, 0:128], xin, ident), reads=[b_xin, b_c], writes=[pb])
        for c in range(2):
            cx.op("act", lambda e, c=c: e.activation(out=Qblk[c * 64:(c + 1) * 64, i * 16:(i + 1) * 16, c * 8:(c + 1) * 8],
                                                     in_=pt[c * 64:(c + 1) * 64, 0:128].rearrange("p (b t) -> p b t", t=DS), func=AF.Copy),
                  reads=[pb], writes=[b_Q])
        cx.dma("sp", xin, knd[i * 128:(i + 1) * 128, :], writes=[b_xin])
        pt, pb = pget()
        cx.op("pe", lambda e: e.transpose(pt[:, 0:128], xin, ident), reads=[b_xin, b_c], writes=[pb])
        cx.op("act", lambda e: e.activation(out=KnT[:, i * 128:(i + 1) * 128], in_=pt[:, 0:128], func=AF.Copy), reads=[pb], writes=[b_KnT])
    Vn = sb("Vn", [DS, NB, 130], BF16); b_Vn = Buf()
    cx.op("dve", lambda e: e.memset(Vn[:, :, 128:130], 1.0), writes=[b_Vn])
    cx.dma("pool", Vn[:, :, 0:128], vnd.rearrange("(b t) e -> t b e", t=DS), writes=[b_Vn])
    tblS = sb("tblS", [1, 512]); b_tbl = Buf(); nfar = sb("nfar", [1, 1])
    cx.op("dve", lambda e: e.memset(tblS, 0.0), writes=[b_tbl])
    rbb = sb("rbb", [128, 32]); bfar = sb("bfar", [128, 1]); b_bfar = Buf()
    with nc.allow_non_contiguous_dma(reason="bias table"):
        cx.dma("sp", rbb, rb.partition_broadcast(128), writes=[b_tbl])
    for (bk, d0, d1) in bucket_runs(384):
        cx.op("dve", lambda e, bk=bk, d0=d0, d1=d1: e.tensor_copy(out=tblS[:, 127 + d0:128 + d1],
                                                                 in_=rbb[0:1, bk:bk + 1].to_broadcast([1, d1 - d0 + 1])),
              reads=[b_tbl], writes=[b_tbl])
    cx.op("dve", lambda e: e.tensor_scalar(out=nfar, in0=rbb[0:1, 31:32], scalar1=-1.0, scalar2=None, op0=ALU.mult), reads=[b_tbl], writes=[b_tbl])
    cx.op("dve", lambda e: e.tensor_copy(out=bfar, in_=rbb[:, 31:32]), reads=[b_tbl], writes=[b_bfar])
    cx.op("act", lambda e: e.activation(out=tblS[:, 127:512], in_=tblS[:, 127:512], func=AF.Exp, bias=nfar[:, 0:1]), reads=[b_tbl], writes=[b_tbl])
    b_tbld = Buf()
    cx.dma("sp", tbl.rearrange("(a n) -> a n", a=1), tblS, reads=[b_tbl], writes=[b_tbld])
    Hs = sb("Hs", [128, 8]); Hn = sb("Hn", [8, 8]); b_H = Buf()
    Bs = sb("Bs", [128, 8]); Bn = sb("Bn", [8, 8]); b_B = Buf()
    with nc.allow_non_contiguous_dma(reason="hankel"):
        cx.dma("sp", Hs, bass.AP(tensor=tbl.tensor, offset=128, ap=[[1, 128], [1, 8]]), reads=[b_tbld], writes=[b_H])
        cx.dma("sp", Hn, bass.AP(tensor=tbl.tensor, offset=120, ap=[[1, 8], [1, 8]]), reads=[b_tbld], writes=[b_H])
    pt, pb = pget()
    cx.op("pe", lambda e: e.matmul(pt[:, 0:8], lhsT=anti, rhs=Hs, start=True, stop=True), reads=[b_H, b_c], writes=[pb])
    cx.op("act", lambda e: e.activation(out=Bs, in_=pt[:, 0:8], func=AF.Copy), reads=[pb], writes=[b_B])
    pt, pb = pget()
    cx.op("pe", lambda e: e.matmul(pt[0:8, 0:8], lhsT=anti[0:8, 120:128], rhs=Hn, start=True, stop=True), reads=[b_H, b_c], writes=[pb])
    cx.op("act", lambda e: e.activation(out=Bn, in_=pt[0:8, 0:8], func=AF.Copy), reads=[pb], writes=[b_B])
    lt = sb("lt", [8, 4, 64]); b_lt = Buf(); sm = sb("sm", [8, 4]); b_sm = Buf()
    gs = sb("gs", [8, 128]); b_gs = Buf()
    with nc.allow_non_contiguous_dma(reason="bcast"):
        for i, src in enumerate((lq1, lk1, lq2, lk2)):
            cx.dma("sp", lt[:, i, :], src.partition_broadcast(8), writes=[b_lt])
        cx.dma("sp", gs, gsd.partition_broadcast(8), writes=[b_gs])
    cx.op("dve", lambda e: e.tensor_tensor(out=lt[:, 0, :], in0=lt[:, 0, :], in1=lt[:, 1, :], op=ALU.mult), reads=[b_lt], writes=[b_lt])
    cx.op("dve", lambda e: e.tensor_tensor(out=lt[:, 2, :], in0=lt[:, 2, :], in1=lt[:, 3, :], op=ALU.mult), reads=[b_lt], writes=[b_lt])
    cx.op("dve", lambda e: e.tensor_reduce(out=sm[:, 0:1], in_=lt[:, 0, :], axis=AX.X, op=ALU.add), reads=[b_lt], writes=[b_sm])
    cx.op("dve", lambda e: e.tensor_reduce(out=sm[:, 1:2], in_=lt[:, 2, :], axis=AX.X, op=ALU.add), reads=[b_lt], writes=[b_sm])
    cx.op("act", lambda e: e.activation(out=sm[:, 0:2], in_=sm[:, 0:2], func=AF.Exp), reads=[b_sm], writes=[b_sm])
    cx.op("dve", lambda e: e.scalar_tensor_tensor(out=sm[:, 2:3], in0=sm[:, 1:2], scalar=-lambda_init(layer), in1=sm[:, 0:1],
                                                 op0=ALU.add, op1=ALU.subtract), reads=[b_sm], writes=[b_sm])
    cx.op("dve", lambda e: e.tensor_scalar(out=gs, in0=gs, scalar1=1.0 - lambda_init(layer), scalar2=None, op0=ALU.mult),
          reads=[b_gs], writes=[b_gs])
    NSL = 32
    kvs = []
    for i in range(NSL):
        t = sb("kv%d" % i, [128, 258], BF16)
        kvs.append((t, Buf()))
    b_ones = Buf()
    for t, _ in kvs:
        cx.op("dve", lambda e, t=t: e.memset(t[:, 256:258], 1.0), writes=[b_ones])
    for t, bf in kvs:
        bf.w = b_ones.w
    ktT = [(sb("ktT%d" % i, [128, NPG, 128], BF16), Buf()) for i in range(2)]
    pTt = [(sb("pT%d" % i, [128, 256], BF16), Buf()) for i in range(2)]
    pNt = [(sb("pN%d" % i, [8, 16], BF16), Buf()) for i in range(2)]
    HB = 64
    obuf = sb("obuf", [8, HB, 258]); b_ob = Buf()
    rden = sb("rden", [8, HB, 2]); b_rd = Buf()
    ss = sb("ss", [8, HB]); b_ss = Buf()

    def post(b0):
        ov = obuf.rearrange("p b (c e) -> p b c e", c=2)
        cx.op("dve", lambda e: e.reciprocal(out=rden, in_=ov[:, :, :, 128]), reads=[b_ob], writes=[b_rd])
        o1 = ov[:, :, 0, 0:128]
        o2 = ov[:, :, 1, 0:128]
        cx.op("dve", lambda e: e.tensor_tensor(out=o1, in0=o1, in1=rden[:, :, 0:1].to_broadcast([8, HB, 128]), op=ALU.mult), reads=[b_ob, b_rd], writes=[b_ob])
        cx.op("dve", lambda e: e.tensor_tensor(out=o2, in0=o2, in1=rden[:, :, 1:2].to_broadcast([8, HB, 128]), op=ALU.mult), reads=[b_ob, b_rd], writes=[b_ob])
        cx.op("dve", lambda e: e.scalar_tensor_tensor(out=o1, in0=o2, scalar=sm[:, 2:3], in1=o1, op0=ALU.mult, op1=ALU.add), reads=[b_ob, b_sm], writes=[b_ob])
        cx.op("dve", lambda e: e.tensor_tensor(out=o2, in0=o1, in1=o1, op=ALU.mult), reads=[b_ob], writes=[b_ob])
        cx.op("dve", lambda e: e.tensor_reduce(out=ss, in_=o2, axis=AX.X, op=ALU.add), reads=[b_ob], writes=[b_ss])
        cx.op("dve", lambda e: e.tensor_scalar(out=ss, in0=ss, scalar1=1.0 / 128, scalar2=EPS, op0=ALU.mult, op1=ALU.add), reads=[b_ss], writes=[b_ss])
        cx.op("pool", lambda e: e.tensor_tensor(out=ss, in0=ss, in1=nhalf[0:8, 0:HB], op=ALU.pow), reads=[b_ss, b_c], writes=[b_ss])
        cx.op("dve", lambda e: e.tensor_tensor(out=o1, in0=o1, in1=ss.unsqueeze(2).to_broadcast([8, HB, 128]), op=ALU.mult), reads=[b_ob, b_ss], writes=[b_ob])
        cx.op("dve", lambda e: e.tensor_tensor(out=o2, in0=o1, in1=gs.unsqueeze(1).to_broadcast([8, HB, 128]), op=ALU.mult), reads=[b_ob, b_gs], writes=[b_ob])
        cx.dma("sp", od[b0 * DS:(b0 + HB) * DS, :].rearrange("(b t) e -> t b e", t=DS), o2, reads=[b_ob])
    rows = ckv
    for b in range(NB):
        sl = []
        for j in range(NPG):
            t, bf = kvs[(b * NPG + j) % NSL]
            cx.dma("pool", t[:, 0:256], rows, reads=[b_idx], writes=[bf], indirect=idx[:, b * NPG + j:b * NPG + j + 1])
            sl.append((t, bf))
        kt_, ktb = ktT[b % 2]
        for g in range(4):
            pt, pb = pget()
            ptb = pt.bitcast(BF16)
            for jj in range(4):
                t, bf = sl[g * 4 + jj]
                cx.op("pe", lambda e, jj=jj, t=t: e.transpose(ptb[:, jj * 128:(jj + 1) * 128], t[:, 0:128], identb),
                      reads=[bf, b_c], writes=[pb], signal=(jj == 3))
            eng = "act" if g % 2 == 0 else "dve"
            if eng == "act":
                cx.op("act", lambda e, g=g: e.activation(out=kt_[:, g * 4:(g + 1) * 4, :], in_=ptb[:, 0:512].rearrange("p (a b) -> p a b", a=4), func=AF.Copy),
                      reads=[pb], writes=[ktb])
            else:
                cx.op("dve", lambda e, g=g: e.tensor_copy(out=kt_[:, g * 4:(g + 1) * 4, :], in_=ptb[:, 0:512].rearrange("p (a b) -> p a b", a=4)),
                      reads=[pb], writes=[ktb])
        sp_, spb = pget()
        for j in range(NPG):
            cx.op("pe", lambda e, j=j: e.matmul(sp_[:, j * 16:(j + 1) * 16], lhsT=kt_[:, j, :], rhs=Qblk[:, b, :], start=True, stop=True),
                  reads=[ktb, b_Q], writes=[spb], signal=False)
        cx.op("pe", lambda e: e.matmul(sp_[0:8, 256:272], lhsT=KnT[:, b * DS:(b + 1) * DS], rhs=Qblk[:, b, :], start=True, stop=True),
              reads=[b_KnT, b_Q], writes=[spb])
        pT, pTb = pTt[b % 2]
        pN, pNb = pNt[b % 2]
        cx.op("act", lambda e: e.activation(out=pT, in_=sp_[:, 0:256], func=AF.Exp, bias=bfar[:, 0:1], scale=0.125), reads=[spb, b_bfar], writes=[pTb])
        cx.op("act", lambda e: e.activation(out=pN, in_=sp_[0:8, 256:272], func=AF.Exp, bias=bfar[0:8, 0:1], scale=0.125), reads=[spb, b_bfar], writes=[pNb])
        cx.op("dve", lambda e: e.tensor_tensor(out=pT[:, 240:256].rearrange("p (c t) -> p c t", c=2), in0=pT[:, 240:256].rearrange("p (c t) -> p c t", c=2),
                                              in1=Bs.unsqueeze(1).to_broadcast([128, 2, 8]), op=ALU.mult), reads=[pTb, b_B], writes=[pTb])
        cx.op("dve", lambda e: e.tensor_tensor(out=pN.rearrange("p (c t) -> p c t", c=2), in0=pN.rearrange("p (c t) -> p c t", c=2),
                                              in1=Bn.unsqueeze(1).to_broadcast([8, 2, 8]), op=ALU.mult), reads=[pNb, b_B], writes=[pNb])
        ac, acb = pget()
        for c in range(2):
            for j in range(NPG):
                t, bf = sl[j]
                cx.op("pe", lambda e, c=c, j=j, t=t: e.matmul(ac[0:8, c * 129:(c + 1) * 129], lhsT=pT[:, j * 16 + c * 8:j * 16 + c * 8 + 8],
                                                             rhs=t[:, 128:257], start=(j == 0), stop=False),
                      reads=[pTb, bf], writes=[acb], signal=False)
            cx.op("pe", lambda e, c=c: e.matmul(ac[0:8, c * 129:(c + 1) * 129], lhsT=pN[0:8, c * 8:(c + 1) * 8], rhs=Vn[:, b, 0:129],
                                               start=False, stop=True), reads=[pNb, b_Vn], writes=[acb])
        cx.op("act", lambda e: e.activation(out=obuf[:, b % HB, :], in_=ac[0:8, 0:258], func=AF.Copy), reads=[acb], writes=[b_ob])
        if b % HB == HB - 1:
            post(b - HB + 1)
    cx.finish("sp")
    return nc


def kernel(x_prompt, x_sample, state_conv, cache_k, cache_v, page_table, rel_bias,
           conv_norm, w_pw1, b_pw1, w_dw, b_dw, conv_mid_norm, w_pw2, b_pw2,
           kv_norm, w_kv, k_norm, attn_norm, w_q, q_norm,
           lambda_q1, lambda_k1, lambda_q2, lambda_k2, sub_norm, w_o,
           ffn_norm, w_gate, w_up, w_down):
    f = lambda a: np.ascontiguousarray(np.asarray(a, dtype=np.float32))
    NCORE = 8
    x_prompt = f(x_prompt); x_sample = f(x_sample); state_conv = f(state_conv)
    B, P = x_prompt.shape[0], x_prompt.shape[1]
    assert B == NCORE and x_sample.shape[0] == NCORE * NBS
    ident = np.eye(128, dtype=np.float32)
    anti = np.ascontiguousarray(ident[::-1])
    wts = dict(rel_bias=f(rel_bias), conv_norm=f(conv_norm), w_pw1=f(w_pw1), b_pw1=f(b_pw1), w_dw=f(w_dw), b_dw=f(b_dw),
               conv_mid_norm=f(conv_mid_norm), w_pw2=f(w_pw2), b_pw2=f(b_pw2), kv_norm=f(kv_norm), w_kv=f(w_kv),
               k_norm=f(k_norm), attn_norm=f(attn_norm), w_q=f(w_q), q_norm=f(q_norm), lambda_q1=f(lambda_q1),
               lambda_k1=f(lambda_k1), lambda_q2=f(lambda_q2), lambda_k2=f(lambda_k2), sub_norm=f(sub_norm), w_o=f(w_o),
               ffn_norm=f(ffn_norm), w_gate=f(w_gate), w_up=f(w_up), w_down=f(w_down))
    cores = list(range(NCORE))
    if FUSED:
        ck = np.asarray(cache_k, dtype=np.float32); cv = np.asarray(cache_v, dtype=np.float32)
        npool = ck.shape[0]
        ckv = np.empty((NH, npool * 128, 256), np.float32)
        for hd in range(NH):
            ckv[hd, :, 0:128] = ck[:, :, hd].reshape(npool * 128, 128)
            ckv[hd, :, 128:256] = cv[:, :, hd].reshape(npool * 128, 128)
        ckv = ckv.reshape(NH * npool * 128, 256)
        pt = np.asarray(page_table, dtype=np.int32)
        ncf = build_L1(P, fused=True, npool=npool)
        ins = []
        for c in cores:
            m = dict(wts)
            m.update(xp=x_prompt[c], xs=x_sample[c * NBS:(c + 1) * NBS].reshape(NS, D),
                     sc=np.ascontiguousarray(state_conv[:, c * NBS:(c + 1) * NBS]), ident=ident, anti=anti,
                     ckv=ckv, ptab=np.ascontiguousarray(pt[c * NBS:(c + 1) * NBS].reshape(-1)))
            ins.append(m)
        r1 = run_bass_kernel_spmd(ncf, ins, core_ids=cores).results
        y_prompt = np.stack([r1[c]["yp"] for c in cores])
        y_sample = np.concatenate([r1[c]["ys"] for c in cores], axis=0).reshape(NCORE * NBS, DS, D)
        conv_state_p = np.stack([r1[c]["csp"] for c in cores], axis=1)
        conv_state_s = np.concatenate([r1[c]["css"] for c in cores], axis=1)
        k_prompt = np.stack([r1[c]["kp"] for c in cores]).reshape(B, P, NH, 2, 64)
        v_prompt = np.stack([r1[c]["vp"] for c in cores]).reshape(B, P, NH, 128)
        k_sample = np.concatenate([r1[c]["ks"] for c in cores], axis=0).reshape(NCORE * NBS, DS, NH, 2, 64)
        v_sample = np.concatenate([r1[c]["vs"] for c in cores], axis=0).reshape(NCORE * NBS, DS, NH, 128)
        return (y_prompt, y_sample, conv_state_p, conv_state_s, k_prompt, v_prompt, k_sample, v_sample)
    nc1 = build_L1(P)
    ins = []
    for c in cores:
        m = dict(wts)
        m.update(xp=x_prompt[c], xs=x_sample[c * NBS:(c + 1) * NBS].reshape(NS, D),
                 sc=np.ascontiguousarray(state_conv[:, c * NBS:(c + 1) * NBS]), ident=ident, anti=anti)
        ins.append(m)
    r1 = run_bass_kernel_spmd(nc1, ins, core_ids=cores).results
    y_prompt = np.stack([r1[c]["yp"] for c in cores])
    conv_state_p = np.stack([r1[c]["csp"] for c in cores], axis=1)
    conv_state_s = np.concatenate([r1[c]["css"] for c in cores], axis=1)
    k_prompt = np.stack([r1[c]["kp"] for c in cores]).reshape(B, P, NH, 2, 64)
    v_prompt = np.stack([r1[c]["vp"] for c in cores]).reshape(B, P, NH, 128)
    ks_all = np.concatenate([r1[c]["ks"] for c in cores], axis=0)
    vs_all = np.concatenate([r1[c]["vs"] for c in cores], axis=0)
    k_sample = ks_all.reshape(NCORE * NBS, DS, NH, 2, 64)
    v_sample = vs_all.reshape(NCORE * NBS, DS, NH, 128)
    hs = [r1[c]["hs1"] for c in cores]
    q_all = np.concatenate([r1[c]["q2"] for c in cores], axis=0)
    ck = np.asarray(cache_k, dtype=np.float32); cv = np.asarray(cache_v, dtype=np.float32)
    npool = ck.shape[0]
    ckv = [np.ascontiguousarray(np.concatenate([ck[:, :, hd].reshape(npool * 128, 128), cv[:, :, hd].reshape(npool * 128, 128)], axis=1))
           for hd in range(NH)]
    ptab = np.ascontiguousarray(np.asarray(page_table, dtype=np.int32).reshape(-1))
    y_sample = None
    for j in range(2):
        nca = build_ATT(npool, 2 + j)
        ins = []
        for hd in cores:
            sl = slice(hd * 128, (hd + 1) * 128)
            ins.append(dict(ckv=ckv[hd], ptab=ptab, q=np.ascontiguousarray(q_all[:, sl]), kn=np.ascontiguousarray(ks_all[:, sl]),
                            vn=np.ascontiguousarray(vs_all[:, sl]), rb=np.ascontiguousarray(wts["rel_bias"][:, hd]),
                            lq1=wts["lambda_q1"][j], lk1=wts["lambda_k1"][j], lq2=wts["lambda_q2"][j], lk2=wts["lambda_k2"][j],
                            gsub=wts["sub_norm"][j], ident=ident, anti=anti))
        ra = run_bass_kernel_spmd(nca, ins, core_ids=cores).results
        o_all = np.concatenate([ra[hd]["o"] for hd in cores], axis=1)
        nc3 = build_L3(with_q=(j == 0))
        ins = []
        for c in cores:
            m = dict(hs=hs[c], oat=np.ascontiguousarray(o_all[c * NS:(c + 1) * NS]), w_o=wts["w_o"][j], ffn_norm=wts["ffn_norm"][2 + j],
                     w_gate=wts["w_gate"][2 + j], w_up=wts["w_up"][2 + j], w_down=wts["w_down"][2 + j], ident=ident)
            if j == 0:
                m.update(attn_norm=wts["attn_norm"][1], w_q=wts["w_q"][1], q_norm=wts["q_norm"][1])
            ins.append(m)
        r3 = run_bass_kernel_spmd(nc3, ins, core_ids=cores).results
        hs = [r3[c]["hs_out"] for c in cores]
        if j == 0:
            q_all = np.concatenate([r3[c]["q_out"] for c in cores], axis=0)
    y_sample = np.concatenate(hs, axis=0).reshape(NCORE * NBS, DS, D)
    return (y_prompt, y_sample, conv_state_p, conv_state_s, k_prompt, v_prompt, k_sample, v_sample)
```

```python
import math
import numpy as np
import concourse.bass as bass
import concourse.mybir as mybir
from concourse.bass_utils import run_bass_kernel_spmd

F32 = mybir.dt.float32
BF16 = mybir.dt.bfloat16
I32 = mybir.dt.int32
AF = mybir.ActivationFunctionType
ALU = mybir.AluOpType
AX = mybir.AxisListType

D = 1024
KC = 8
DFF = 2816
NH = 8
CW = 31
NBS = 16
DS = 8
NS = NBS * DS
EPS = 1e-6
NPG = 16
FUSED = True
SL = 38


def lambda_init(l):
    return 0.8 - 0.6 * math.exp(-0.3 * l)


def bucket_runs(maxd):
    n = np.arange(0, maxd + 1)
    nf = np.maximum(n, 1).astype(np.float32)
    large = 16 + (np.log(nf / np.float32(16)) / np.float32(math.log(8.0)) * np.float32(16)).astype(np.int32)
    large = np.minimum(large, 31)
    bk = np.where(n < 16, n, large)
    runs = []
    s = 0
    for i in range(1, maxd + 2):
        if i == maxd + 1 or bk[i] != bk[s]:
            runs.append((int(bk[s]), s, i - 1))
            s = i
    return runs


class Buf:
    __slots__ = ("w", "r")

    def __init__(self):
        self.w = None
        self.r = []


class Eng:
    def __init__(self, name, handle, sem):
        self.name = name
        self.h = handle
        self.sem = sem
        self.cnt = 0
        self.seen = {}
        self.pr = []
        self.pw = []


class Ctx:
    def __init__(self, nc, n_dma_sems=48):
        self.nc = nc
        self.E = {}
        for name, h in (("pe", nc.tensor), ("act", nc.scalar), ("dve", nc.vector),
                        ("pool", nc.gpsimd), ("sp", nc.sync)):
            self.E[name] = Eng(name, h, nc.alloc_semaphore("c_" + name))
        self.dsem = [[nc.alloc_semaphore("d%d" % i), 0] for i in range(n_dma_sems)]
        self.dnext = 0

    def _wait(self, eng, tok):
        if tok is None:
            return
        kind, key, val = tok
        k = (kind, key)
        if eng.seen.get(k, 0) >= val:
            return
        if kind == "e":
            if key == eng.name and key == "pe":
                return
            eng.h.wait_ge(self.E[key].sem, val)
        else:
            eng.h.wait_ge(self.dsem[key][0], val)
        eng.seen[k] = val

    def _deps(self, eng, reads, writes):
        for b in reads:
            self._wait(eng, b.w)
        for b in writes:
            self._wait(eng, b.w)
            for t in b.r:
                self._wait(eng, t)

    def _commit(self, tok, reads, writes):
        for b in reads:
            b.r = [t for t in b.r if not (t[0] == tok[0] and t[1] == tok[1])]
            b.r.append(tok)
        for b in writes:
            b.w = tok
            b.r = []

    def op(self, ename, fn, reads=(), writes=(), signal=True):
        eng = self.E[ename]
        self._deps(eng, reads, writes)
        inst = fn(eng.h)
        if signal:
            eng.cnt += 1
            inst.then_inc(eng.sem, 1)
            tok = ("e", ename, eng.cnt)
            self._commit(tok, list(reads) + eng.pr, list(writes) + eng.pw)
            eng.pr = []
            eng.pw = []
        else:
            eng.pr.extend(reads)
            eng.pw.extend(writes)
        return inst

    def dma(self, qname, out, in_, reads=(), writes=(), indirect=None):
        eng = self.E[qname]
        self._deps(eng, reads, writes)
        i = self.dnext
        self.dnext = (self.dnext + 1) % len(self.dsem)
        sem, val = self.dsem[i]
        if val > 0:
            self._wait(eng, ("d", i, val))
        if indirect is None:
            inst = eng.h.dma_start(out=out, in_=in_)
        else:
            inst = eng.h.indirect_dma_start(out=out, out_offset=None, in_=in_,
                                            in_offset=bass.IndirectOffsetOnAxis(ap=indirect, axis=0))
        inst.then_inc(sem, 16)
        self.dsem[i][1] = val + 16
        tok = ("d", i, val + 16)
        self._commit(tok, reads, writes)
        return tok

    def barrier(self):
        for eng in self.E.values():
            for i, (sem, val) in enumerate(self.dsem):
                if val > 0:
                    self._wait(eng, ("d", i, val))
            for name, e in self.E.items():
                if name != eng.name and e.cnt > 0:
                    self._wait(eng, ("e", name, e.cnt))

    def finish(self, qname="sp"):
        eng = self.E[qname]
        for i, (sem, val) in enumerate(self.dsem):
            if val > 0:
                self._wait(eng, ("d", i, val))
        for name, e in self.E.items():
            if name != qname and e.cnt > 0:
                self._wait(eng, ("e", name, e.cnt))


def tiles(t0, t1, step=512):
    out = []
    t = t0
    while t < t1:
        n = min(step, t1 - t)
        out.append((t, n))
        t += n
    return out


class Prog:
    def __init__(self, T, rbytes, xbytes):
        self.nc = nc = bass.Bass("TRN2", target_bir_lowering=False)
        self.cx = Ctx(nc)
        self.T = T
        self.banks = [(nc.alloc_psum_tensor("ps%d" % i, [128, 512], F32)[:], Buf()) for i in range(8)]
        self.ring_i = 0
        self.nring = 6
        self.ident_d = self.din("ident", [128, 128])
        self.ident = self.sb("ident_s", [128, 128])
        self.b_const = Buf()
        self.ones = self.sb("ones", [128, 128], BF16)
        self.nhalf = self.sb("nhalf", [128, 512])
        self.cx.dma("sp", self.ident, self.ident_d, writes=[self.b_const])
        self.cx.op("dve", lambda e: e.memset(self.ones, 1.0), writes=[self.b_const])
        self.cx.op("dve", lambda e: e.memset(self.nhalf, -0.5), writes=[self.b_const])
        self.h = self.sb("h", [128, KC, T])
        self.b_h = Buf()
        self.xn = self.sb("xn", [128, KC, T], BF16)
        self.b_xn = Buf()
        self.R = self.sb("R", [128, rbytes // 2], BF16)
        self.b_R = Buf()
        self.X = self.sb("X", [128, xbytes // 2], BF16)
        self.wring = [(self.sb("wr%d" % i, [128, 2048], BF16), Buf()) for i in range(4)]
        self.wi = 0
        self.sq = [(self.sb("sq%d" % i, [128, 512], BF16), Buf()) for i in range(2)]
        self.ms = self.sb("ms", [128, 512])
        self.b_ms = Buf()
        self.pcount = 0

    def sb(self, name, shape, dt=F32):
        return self.nc.alloc_sbuf_tensor(name, list(shape), dt)[:]

    def din(self, name, shape, dt=F32):
        return self.nc.dram_tensor(name, list(shape), dt, kind="ExternalInput").ap()

    def dout(self, name, shape, dt=F32):
        return self.nc.dram_tensor(name, list(shape), dt, kind="ExternalOutput").ap()

    def dint(self, name, shape, dt=F32):
        return self.nc.dram_tensor(name, list(shape), dt, kind="Internal").ap()

    def carve(self, arena, off, shape, dt):
        sz = 2 if dt == BF16 else 4
        n = int(np.prod(shape))
        assert off % 4 == 0
        v = arena[:, off // 2:(off + n * sz) // 2]
        if dt != BF16:
            v = v.bitcast(dt)
        if len(shape) == 2:
            v = v.rearrange("p (a b) -> p a b", a=shape[0])
        elif len(shape) == 3:
            v = v.rearrange("p (a b c) -> p a b c", a=shape[0], b=shape[1])
        return v

    def pget(self):
        t, b = self.banks[self.ring_i]
        self.ring_i = (self.ring_i + 1) % self.nring
        return t, b

    def param_fm(self, name, dram_vec, n):
        t = self.sb(name, [128, n])
        b = Buf()
        with self.nc.allow_non_contiguous_dma(reason="small param"):
            self.cx.dma("sp", t, dram_vec.rearrange("(k p) -> p k", p=128), writes=[b])
        return t, b

    def param_bc(self, name, dram_vec, n, parts=128):
        t = self.sb(name, [parts, n])
        b = Buf()
        with self.nc.allow_non_contiguous_dma(reason="bcast param"):
            self.cx.dma("sp", t, dram_vec.partition_broadcast(parts), writes=[b])
        return t, b

    def load_tm(self, rows_ap, nrows, dst, dstb, col0, xin, xinb):
        cx = self.cx
        cx.dma("sp", xin[:nrows, :], rows_ap, writes=[xinb])
        for k0 in range(0, KC, 4):
            pt, pb = self.pget()
            for j in range(4):
                cx.op("pe", lambda e, j=j: e.transpose(pt[:, j * 128:j * 128 + nrows],
                                                       xin[:nrows, (k0 + j) * 128:(k0 + j + 1) * 128],
                                                       self.ident[:nrows, :nrows]),
                      reads=[xinb, self.b_const], writes=[pb], signal=(j == 3))
            src = pt[:, :].rearrange("p (a b) -> p a b", a=4)[:, :, :nrows]
            cx.op("act", lambda e: e.activation(out=dst[:, k0:k0 + 4, col0:col0 + nrows], in_=src, func=AF.Copy),
                  reads=[pb], writes=[dstb])

    def store_tm(self, src, srcb, col0, nrows, rows_ap, stg, stgb, nk=KC):
        cx = self.cx
        for k0 in range(0, nk, 4):
            pt, pb = self.pget()
            for j in range(4):
                cx.op("pe", lambda e, j=j: e.transpose(pt[:nrows, j * 128:(j + 1) * 128],
                                                       src[:, k0 + j, col0:col0 + nrows], self.ident),
                      reads=[srcb, self.b_const], writes=[pb], signal=(j == 3))
            cx.op("dve", lambda e: e.tensor_copy(out=stg[:nrows, k0 * 128:(k0 + 4) * 128], in_=pt[:nrows, :]),
                  reads=[pb], writes=[stgb])
        cx.dma("sp", rows_ap, stg[:nrows, :nk * 128], reads=[stgb])

    def rmsnorm(self, g, gb, t0, t1, src=None, srcb=None):
        cx = self.cx
        h = self.h if src is None else src
        hb = self.b_h if srcb is None else srcb
        for (a, n) in tiles(t0, t1):
            pt, pb = self.pget()
            for kc in range(KC):
                s, sbf = self.sq[kc % 2]
                cx.op("act", lambda e, kc=kc, s=s: e.activation(out=s[:, :n], in_=h[:, kc, a:a + n], func=AF.Square),
                      reads=[hb], writes=[sbf])
                cx.op("pe", lambda e, kc=kc, s=s: e.matmul(pt[:, :n], lhsT=self.ones, rhs=s[:, :n],
                                                          start=(kc == 0), stop=(kc == KC - 1)),
                      reads=[sbf, self.b_const], writes=[pb])
            self.rstd_from(pt, pb, n, 1.0 / D)
            for kc in range(KC):
                cx.op("dve", lambda e, kc=kc: e.scalar_tensor_tensor(
                    out=self.xn[:, kc, a:a + n], in0=h[:, kc, a:a + n], scalar=g[:, kc:kc + 1],
                    in1=self.ms[:, :n], op0=ALU.mult, op1=ALU.mult),
                    reads=[hb, self.b_ms, gb], writes=[self.b_xn])

    def rstd_from(self, pt, pb, n, inv):
        cx = self.cx
        cx.op("dve", lambda e: e.tensor_scalar(out=self.ms[:, :n], in0=pt[:, :n], scalar1=inv, scalar2=EPS,
                                              op0=ALU.mult, op1=ALU.add), reads=[pb], writes=[self.b_ms])
        cx.op("pool", lambda e: e.tensor_tensor(out=self.ms[:, :n], in0=self.ms[:, :n], in1=self.nhalf[:, :n],
                                               op=ALU.pow), reads=[self.b_ms, self.b_const], writes=[self.b_ms])

    def wfetch(self, W_rows, c0, cb, kcs):
        slot, sbf = self.wring[self.wi]
        self.wi = (self.wi + 1) % len(self.wring)
        wv = slot[:, :kcs * 256].rearrange("p (k c) -> p k c", c=256)
        self.cx.dma("pool", wv[:, :, :cb], W_rows[:, c0:c0 + cb].rearrange("(k p) c -> p k c", p=128), writes=[sbf])
        return wv, sbf

    def linear(self, x, xb, kcs, W_rows, ncols, tok_tiles, epi, LA=2):
        cx = self.cx
        blocks = [(c0, min(256, ncols - c0)) for c0 in range(0, ncols, 256)]
        fetched = []
        for i in range(min(LA, len(blocks))):
            fetched.append(self.wfetch(W_rows, blocks[i][0], blocks[i][1], kcs))
        for i, (c0, cb) in enumerate(blocks):
            if i + LA < len(blocks):
                fetched.append(self.wfetch(W_rows, blocks[i + LA][0], blocks[i + LA][1], kcs))
            wv, sbf = fetched[i]
            for m in range(cb // 128):
                for (a, n) in tok_tiles:
                    pt, pb = self.pget()
                    for k in range(kcs):
                        cx.op("pe", lambda e, k=k: e.matmul(pt[:, :n], lhsT=wv[:, k, m * 128:(m + 1) * 128],
                                                           rhs=x[:, k, a:a + n], start=(k == 0), stop=(k == kcs - 1)),
                              reads=[sbf, xb], writes=[pb], signal=(k == kcs - 1))
                    epi((c0 // 128) + m, a, n, pt, pb)

    def add_to_h(self, m, a, n, pt, pb):
        self.cx.op("dve", lambda e: e.tensor_tensor(out=self.h[:, m, a:a + n], in0=pt[:, :n], in1=self.h[:, m, a:a + n],
                                                   op=ALU.add), reads=[pb, self.b_h], writes=[self.b_h])

    def ffn(self, g, gb, Wg, Wu, Wd, t0, t1, sgt):
        cx = self.cx
        self.rmsnorm(g, gb, t0, t1)
        tt = tiles(t0, t1)
        hid = self.carve(self.R, 0, [KC, self.T], BF16)
        for (f0, f1) in ((0, 8), (8, 15), (15, 22)):
            nf = f1 - f0
            blocks = [(c0, min(256, f1 * 128 - c0)) for c0 in range(f0 * 128, f1 * 128, 256)]
            fetched = []

            def fetch(i):
                c0, cb = blocks[i]
                fetched.append((self.wfetch(Wg, c0, cb, KC), self.wfetch(Wu, c0, cb, KC)))
            fetch(0)
            for i, (c0, cb) in enumerate(blocks):
                if i + 1 < len(blocks):
                    fetch(i + 1)
                (wg, gbf), (wu, ubf) = fetched[i]
                for m in range(cb // 128):
                    fi = (c0 - f0 * 128) // 128 + m
                    for (a, n) in tt:
                        pg, pgb = self.pget()
                        pu, pub = self.pget()
                        for k in range(KC):
                            cx.op("pe", lambda e, k=k: e.matmul(pg[:, :n], lhsT=wg[:, k, m * 128:(m + 1) * 128],
                                                               rhs=self.xn[:, k, a:a + n], start=(k == 0), stop=(k == KC - 1)),
                                  reads=[gbf, self.b_xn], writes=[pgb], signal=(k == KC - 1))
                        for k in range(KC):
                            cx.op("pe", lambda e, k=k: e.matmul(pu[:, :n], lhsT=wu[:, k, m * 128:(m + 1) * 128],
                                                               rhs=self.xn[:, k, a:a + n], start=(k == 0), stop=(k == KC - 1)),
                                  reads=[ubf, self.b_xn], writes=[pub], signal=(k == KC - 1))
                        sg, sgb = sgt[self.pcount % 2]
                        self.pcount += 1
                        cx.op("act", lambda e: e.activation(out=sg[:, :n], in_=pg[:, :n], func=AF.Silu),
                              reads=[pgb], writes=[sgb])
                        cx.op("dve", lambda e: e.tensor_tensor(out=hid[:, fi, a:a + n], in0=sg[:, :n], in1=pu[:, :n],
                                                              op=ALU.mult), reads=[sgb, pub], writes=[self.b_R])
            self.linear(hid, self.b_R, nf, Wd[f0 * 128:f1 * 128, :], D, tt, self.add_to_h)


def build_L1(P, fused=False, npool=0):
    T = P + NS
    UP = 30 + P
    UL = UP + NBS * SL
    rbytes = ((KC * UL * 2 + 63) // 64) * 64
    if fused:
        rbytes = max(rbytes, 40192)
    pr = Prog(T, rbytes, 30720)
    nc, cx = pr.nc, pr.cx
    din, dout = pr.din, pr.dout
    xp = din("xp", [P, D]); xs = din("xs", [NS, D]); sc = din("sc", [2, NBS, 30, D])
    anti_d = din("anti", [128, 128])
    rel_bias = din("rel_bias", [32, 8])
    conv_norm = din("conv_norm", [2, D]); w_pw1 = din("w_pw1", [2, D, 2 * D]); b_pw1 = din("b_pw1", [2, 2 * D])
    w_dw = din("w_dw", [2, CW, D]); b_dw = din("b_dw", [2, D]); conv_mid = din("conv_mid_norm", [2, D])
    w_pw2 = din("w_pw2", [2, D, D]); b_pw2 = din("b_pw2", [2, D])
    kv_norm = din("kv_norm", [D]); w_kv = din("w_kv", [D, 2 * D]); k_norm = din("k_norm", [64])
    attn_norm = din("attn_norm", [2, D]); w_q = din("w_q", [2, D, D]); q_norm = din("q_norm", [2, 64])
    lq1 = din("lambda_q1", [2, 64]); lk1 = din("lambda_k1", [2, 64]); lq2 = din("lambda_q2", [2, 64]); lk2 = din("lambda_k2", [2, 64])
    sub_norm = din("sub_norm", [2, 128]); w_o = din("w_o", [2, D, D])
    ffn_norm = din("ffn_norm", [4, D]); w_gate = din("w_gate", [4, D, DFF]); w_up = din("w_up", [4, D, DFF])
    w_down = din("w_down", [4, DFF, D])
    yp = dout("yp", [P, D]); csp = dout("csp", [2, 30, D]); css = dout("css", [2, NBS, 30, D])
    kp = dout("kp", [P, D]); vp = dout("vp", [P, D]); ks = dout("ks", [NS, D]); vs = dout("vs", [NS, D])
    if fused:
        ys = dout("ys", [NS, D])
        ckvA = din("ckvA", [npool * 128, 1024]); ckvB = din("ckvB", [npool * 128, 1024]); ptab = din("ptab", [NBS * NPG], I32)
        qs = pr.dint("qs", [NS, D]); ksi = pr.dint("ksi", [NS, D]); vsi = pr.dint("vsi", [NS, D]); osd = pr.dint("osd", [NS, D])
    else:
        hs1 = dout("hs1", [NS, D]); q2 = dout("q2", [NS, D])
    KTs = pr.dint("KTs", [NH, 128, P], BF16); Vsc = pr.dint("Vsc", [P, D], BF16); tbl = pr.dint("tbl", [8, 512])

    h, b_h, xn, b_xn = pr.h, pr.b_h, pr.xn, pr.b_xn
    X, R = pr.X, pr.R
    identb = pr.sb("identb", [128, 128], BF16)
    cx.op("dve", lambda e: e.tensor_copy(out=identb, in_=pr.ident), reads=[pr.b_const], writes=[pr.b_const])
    anti = pr.sb("anti_s", [128, 128])
    cx.dma("sp", anti, anti_d, writes=[pr.b_const])

    pfm = {}
    for nm, ap, n in (("cn0", conv_norm[0], 8), ("cn1", conv_norm[1], 8), ("b10", b_pw1[0], 16), ("b11", b_pw1[1], 16),
                      ("bd0", b_dw[0], 8), ("bd1", b_dw[1], 8), ("cm0", conv_mid[0], 8), ("cm1", conv_mid[1], 8),
                      ("b20", b_pw2[0], 8), ("b21", b_pw2[1], 8), ("kvn", kv_norm, 8), ("an0", attn_norm[0], 8),
                      ("an1", attn_norm[1], 8), ("fn0", ffn_norm[0], 8), ("fn1", ffn_norm[1], 8),
                      ("fn2", ffn_norm[2], 8), ("fn3", ffn_norm[3], 8)):
        pfm[nm] = pr.param_fm(nm, ap, n)

    xin = [(pr.carve(X, 0, [D], F32), Buf()), (pr.carve(X, 4096, [D], F32), Buf())]
    ti = 0
    for a in range(0, P, 128):
        xi, xb = xin[ti % 2]; ti += 1
        pr.load_tm(xp[a:a + 128, :], 128, h, b_h, a, xi, xb)
    xi, xb = xin[ti % 2]; ti += 1
    pr.load_tm(xs, NS, h, b_h, P, xi, xb)

    all_tiles = tiles(0, P) + [(P, NS)]

    wdw_t = pr.sb("wdw", [128, KC, CW])
    for l in range(2):
        cx.barrier()
        cn, cnb = pfm["cn%d" % l]; b1, b1b = pfm["b1%d" % l]; bd, bdb = pfm["bd%d" % l]
        cm, cmb = pfm["cm%d" % l]; b2, b2b = pfm["b2%d" % l]
        dg = [(pr.carve(X, 0, [CW, 128], BF16), Buf()), (pr.carve(X, 7936, [CW, 128], BF16), Buf())]
        glu = [(pr.carve(X, 15872, [512], F32), Buf())]
        sig = [(pr.carve(X, 17920, [512], F32), Buf())]
        stA = pr.carve(X, 19968, [KC, 30 + NS], F32); b_stA = Buf()
        sqc = pr.carve(X, 19968 + KC * (30 + NS) * 4, [KC, 256], BF16); b_sqc = Buf()
        tmpc = [(pr.carve(X, 29120, [256], F32), Buf())]
        u = pr.carve(R, 0, [KC, UL], BF16); b_u = pr.b_R
        wdr = pr.carve(X, 0, [D], F32)
        b_wdr = Buf()
        wdw = wdw_t; b_wdw = Buf()
        cx.dma("sp", wdr[:CW, :], w_dw[l], writes=[b_wdr])
        for k0 in range(0, KC, 4):
            pt, pb = pr.pget()
            for j in range(4):
                cx.op("pe", lambda e, j=j: e.transpose(pt[:, j * 128:j * 128 + CW], wdr[:CW, (k0 + j) * 128:(k0 + j + 1) * 128],
                                                       pr.ident[:CW, :CW]), reads=[b_wdr, pr.b_const], writes=[pb], signal=(j == 3))
            cx.op("act", lambda e: e.activation(out=wdw[:, k0:k0 + 4, :], in_=pt[:, :].rearrange("p (a b) -> p a b", a=4)[:, :, :CW],
                                               func=AF.Copy), reads=[pb], writes=[b_wdw])
        cx.barrier()
        pr.rmsnorm(cn, cnb, 0, T)
        cx.op("pool", lambda e: e.memset(u[:, :, 0:30], 0.0), writes=[b_u])
        for g4 in range(NBS // 4):
            xi, xb = xin[ti % 2]; ti += 1
            xi = xin[1][0]; xb = xin[1][1]
            cx.dma("sp", xi[:120, :], sc[l, g4 * 4:(g4 + 1) * 4].rearrange("b s f -> (b s) f"), writes=[xb])
            for k0 in range(0, KC, 4):
                pt, pb = pr.pget()
                for j in range(4):
                    cx.op("pe", lambda e, j=j: e.transpose(pt[:, j * 128:j * 128 + 120], xi[:120, (k0 + j) * 128:(k0 + j + 1) * 128],
                                                           pr.ident[:120, :120]), reads=[xb, pr.b_const], writes=[pb], signal=(j == 3))
                for j in range(4):
                    dstv = u[:, k0 + j, UP + g4 * 4 * SL:UP + (g4 + 1) * 4 * SL].rearrange("p (b s) -> p b s", s=SL)[:, :, 0:30]
                    cx.op("act", lambda e, j=j, dstv=dstv: e.activation(
                        out=dstv, in_=pt[:, j * 128:j * 128 + 120].rearrange("p (b s) -> p b s", s=30), func=AF.Copy),
                        reads=[pb], writes=[b_u])
        W1 = w_pw1[l]
        nblk = D // 256
        fetched = []

        def fetch1(i):
            fetched.append((pr.wfetch(W1, i * 256, 256, KC), pr.wfetch(W1, D + i * 256, 256, KC)))
        fetch1(0)
        for i in range(nblk):
            if i + 1 < nblk:
                fetch1(i + 1)
            (wa, abf), (wg, gbf) = fetched[i]
            for m in range(2):
                mc = i * 2 + m
                for (a, n) in all_tiles:
                    pa, pab = pr.pget()
                    pg, pgb = pr.pget()
                    for k in range(KC):
                        cx.op("pe", lambda e, k=k: e.matmul(pa[:, :n], lhsT=wa[:, k, m * 128:(m + 1) * 128], rhs=xn[:, k, a:a + n],
                                                           start=(k == 0), stop=(k == KC - 1)), reads=[abf, b_xn], writes=[pab], signal=(k == KC - 1))
                    for k in range(KC):
                        cx.op("pe", lambda e, k=k: e.matmul(pg[:, :n], lhsT=wg[:, k, m * 128:(m + 1) * 128], rhs=xn[:, k, a:a + n],
                                                           start=(k == 0), stop=(k == KC - 1)), reads=[gbf, b_xn], writes=[pgb], signal=(k == KC - 1))
                    sg, sgb = sig[0]
                    gl, glb = glu[0]
                    cx.op("act", lambda e: e.activation(out=sg[:, :n], in_=pg[:, :n], func=AF.Sigmoid, bias=b1[:, 8 + mc:9 + mc]),
                          reads=[pgb, b1b], writes=[sgb])
                    cx.op("dve", lambda e: e.scalar_tensor_tensor(out=gl[:, :n], in0=pa[:, :n], scalar=b1[:, mc:mc + 1], in1=sg[:, :n],
                                                                 op0=ALU.add, op1=ALU.mult), reads=[pab, sgb, b1b], writes=[glb])
                    if a < P:
                        cx.op("pool", lambda e: e.tensor_copy(out=u[:, mc, 30 + a:30 + a + n], in_=gl[:, :n]), reads=[glb], writes=[b_u])
                        if a + n == P:
                            cx.op("pool", lambda e: e.tensor_copy(out=stA[:, mc, 0:30], in_=gl[:, n - 30:n]), reads=[glb], writes=[b_stA])
                    else:
                        dstv = u[:, mc, UP:UP + NBS * SL].rearrange("p (b s) -> p b s", s=SL)[:, :, 30:SL]
                        cx.op("pool", lambda e, dstv=dstv: e.tensor_copy(out=dstv, in_=gl[:, :NS].rearrange("p (b t) -> p b t", t=DS)),
                              reads=[glb], writes=[b_u])
                        cx.op("pool", lambda e: e.tensor_copy(out=stA[:, mc, 30:30 + NS], in_=gl[:, :NS]), reads=[glb], writes=[b_stA])
        stg, stgb = xin[1]
        for k0 in range(0, KC, 4):
            pt, pb = pr.pget()
            for j in range(4):
                cx.op("pe", lambda e, j=j: e.transpose(pt[:30, j * 128:(j + 1) * 128], stA[:, k0 + j, 0:30], pr.ident),
                      reads=[b_stA, pr.b_const], writes=[pb], signal=(j == 3))
            cx.op("dve", lambda e: e.tensor_copy(out=stg[:30, k0 * 128:(k0 + 4) * 128], in_=pt[:30, :]), reads=[pb], writes=[stgb])
        cx.dma("sp", csp[l], stg[:30, :], reads=[stgb])
        for k0 in range(0, KC, 4):
            pt, pb = pr.pget()
            for j in range(4):
                cx.op("pe", lambda e, j=j: e.transpose(pt[:, j * 128:(j + 1) * 128], stA[:, k0 + j, 30:30 + NS], pr.ident),
                      reads=[b_stA, pr.b_const], writes=[pb], signal=(j == 3))
            cx.op("dve", lambda e: e.tensor_copy(out=stg[:, k0 * 128:(k0 + 4) * 128], in_=pt[:, :]), reads=[pb], writes=[stgb])
        for b in range(NBS):
            cx.dma("sp", css[l, b, 22:30, :], stg[b * DS:(b + 1) * DS, :], reads=[stgb])
        cx.dma("sp", css[l, :, 0:22, :], sc[l, :, 8:30, :])
        cx.barrier()
        ctiles = [(256 * i, 256, 256, 256 * i, None) for i in range(P // 256)]
        for b0 in range(0, NBS, 6):
            nb = min(6, NBS - b0)
            ctiles.append((UP + SL * b0, SL * nb - 30, DS * nb, P + DS * b0, nb))
        di = 0
        for (o0, nmm, ntok, tok0, nb) in ctiles:
            cb4 = [pr.pget() for _ in range(4)]

            def view(kc):
                bt, _ = cb4[kc // 2]
                off = (kc % 2) * 256
                if nb is None:
                    return bt[:, off:off + 256]
                return bt[:, off:off + SL * nb].rearrange("p (b s) -> p b s", s=SL)[:, :, 0:DS]

            def shp(ap2):
                if nb is None:
                    return ap2
                return ap2.rearrange("p (b t) -> p b t", t=DS)
            for kc in range(KC):
                bt, btb = cb4[kc // 2]
                off = (kc % 2) * 256
                dgt, dgb = dg[di % 2]; di += 1
                cx.op("pool", lambda e, kc=kc, dgt=dgt: e.tensor_tensor(
                    out=dgt, in0=identb.unsqueeze(1).to_broadcast([128, CW, 128]),
                    in1=wdw[:, kc, :].unsqueeze(2).to_broadcast([128, CW, 128]), op=ALU.mult),
                    reads=[pr.b_const, b_wdw], writes=[dgb])
                for j in range(CW):
                    cx.op("pe", lambda e, kc=kc, j=j, dgt=dgt: e.matmul(bt[:, off:off + nmm], lhsT=dgt[:, j, :],
                                                                     rhs=u[:, kc, o0 + j:o0 + j + nmm], start=(j == 0), stop=(j == CW - 1)),
                          reads=[dgb, b_u], writes=[btb], signal=(j == CW - 1))
                cx.op("act", lambda e, kc=kc: e.activation(out=shp(sqc[:, kc, :ntok]), in_=view(kc), func=AF.Square, bias=bd[:, kc:kc + 1]),
                      reads=[btb, bdb], writes=[b_sqc])
            ps_, psb = pr.pget()
            for kc in range(KC):
                cx.op("pe", lambda e, kc=kc: e.matmul(ps_[:, :ntok], lhsT=pr.ones, rhs=sqc[:, kc, :ntok], start=(kc == 0), stop=(kc == KC - 1)),
                      reads=[b_sqc, pr.b_const], writes=[psb], signal=(kc == KC - 1))
            pr.rstd_from(ps_, psb, ntok, 1.0 / D)
            for kc in range(KC):
                bt, btb = cb4[kc // 2]
                tm, tmb = tmpc[0]
                cx.op("dve", lambda e, kc=kc: e.scalar_tensor_tensor(out=shp(tm[:, :ntok]), in0=view(kc), scalar=bd[:, kc:kc + 1],
                                                                    in1=shp(pr.ms[:, :ntok]), op0=ALU.add, op1=ALU.mult),
                      reads=[btb, bdb, pr.b_ms], writes=[tmb])
                cx.op("act", lambda e, kc=kc: e.activation(out=xn[:, kc, tok0:tok0 + ntok], in_=tm[:, :ntok], func=AF.Silu, scale=cm[:, kc:kc + 1]),
                      reads=[tmb, cmb], writes=[b_xn])

        def epi_pw2(m, a, n, pt, pb):
            cx.op("dve", lambda e: e.scalar_tensor_tensor(out=h[:, m, a:a + n], in0=pt[:, :n], scalar=b2[:, m:m + 1], in1=h[:, m, a:a + n],
                                                         op0=ALU.add, op1=ALU.add), reads=[pb, b2b, b_h], writes=[b_h])
        pr.linear(xn, b_xn, KC, w_pw2[l], D, all_tiles, epi_pw2)
        cx.barrier()
        sgt = [(pr.carve(X, 0, [512], F32), Buf()), (pr.carve(X, 2048, [512], F32), Buf())]
        fn, fnb = pfm["fn%d" % l]
        pr.ffn(fn, fnb, w_gate[l], w_up[l], w_down[l], 0, T, sgt)

    cx.barrier()
    stg, stgb = xin[1]
    if not fused:
        pr.store_tm(h, b_h, P, NS, hs1, stg, stgb)

    wres = pr.carve(X, 0, [2, KC, 512], BF16); b_wres = Buf()
    kout = pr.carve(X, 16384, [D], F32); b_kout = Buf()
    sqk = pr.carve(X, 20480, [512], F32); b_sqk = Buf()
    tmpk = pr.carve(X, 22528, [512], F32); b_tmpk = Buf()
    vb = pr.carve(X, 24576, [D], BF16); b_vb = Buf()
    ktile = pr.carve(X, 26624, [NH, 128], BF16); b_ktile = Buf()
    Bp = pr.sb("Bp", [128, NH, 240], BF16); b_Bp = Buf()
    pTs = [(pr.carve(X, 1024 * i, [512], BF16), Buf()) for i in range(4)]
    o1n = pr.carve(X, 4096, [128], F32); b_o1n = Buf()
    odf = pr.carve(X, 4608, [128], F32); b_odf = Buf()
    onr = pr.carve(X, 5120, [128], F32); b_onr = Buf()
    junk = pr.carve(X, 5632, [128], F32); b_junk = Buf()
    QT = pr.carve(R, 0, [NH, P], BF16); b_QT = pr.b_R
    qoff = NH * P * 2
    KTh = pr.carve(R, qoff, [P], BF16); b_KTh = Buf()
    va = pr.carve(R, qoff + P * 2, [P // 128, 130], BF16); b_va = Buf()
    ssk = pr.sb("ssk", [128, 8]); b_ssk = Buf()
    small = pr.sb("small", [128, 8]); b_small = Buf()

    gk, gkb = pr.param_bc("gk", k_norm, 64)
    gq = [pr.param_bc("gq%d" % j, q_norm[j], 64) for j in range(2)]
    gsub = [pr.param_bc("gsub%d" % j, sub_norm[j], 128) for j in range(2)]
    bfar, bfarb = pr.param_bc("bfar", rel_bias[31], 8)
    lam = pr.sb("lam", [128, 4]); b_lam = Buf()
    lt = pr.carve(X, 4096, [4, 64], F32); b_lt = Buf()
    for j in range(2):
        with nc.allow_non_contiguous_dma(reason="bcast"):
            for i, src in enumerate((lq1[j], lk1[j], lq2[j], lk2[j])):
                cx.dma("sp", lt[:, i, :], src.partition_broadcast(128), writes=[b_lt])
        cx.op("dve", lambda e: e.tensor_tensor(out=lt[:, 0, :], in0=lt[:, 0, :], in1=lt[:, 1, :], op=ALU.mult), reads=[b_lt], writes=[b_lt])
        cx.op("dve", lambda e: e.tensor_tensor(out=lt[:, 2, :], in0=lt[:, 2, :], in1=lt[:, 3, :], op=ALU.mult), reads=[b_lt], writes=[b_lt])
        cx.op("dve", lambda e: e.tensor_reduce(out=small[:, 0:1], in_=lt[:, 0, :], axis=AX.X, op=ALU.add), reads=[b_lt], writes=[b_small])
        cx.op("dve", lambda e: e.tensor_reduce(out=small[:, 1:2], in_=lt[:, 2, :], axis=AX.X, op=ALU.add), reads=[b_lt], writes=[b_small])
        cx.op("act", lambda e: e.activation(out=small[:, 0:2], in_=small[:, 0:2], func=AF.Exp), reads=[b_small], writes=[b_small])
        cx.op("dve", lambda e, j=j: e.scalar_tensor_tensor(out=lam[:, j:j + 1], in0=small[:, 1:2], scalar=-lambda_init(2 + j),
                                                          in1=small[:, 0:1], op0=ALU.add, op1=ALU.subtract),
              reads=[b_small], writes=[b_lam])
        g_, gb_ = gsub[j]
        cx.op("dve", lambda e, j=j, g_=g_: e.tensor_scalar(out=g_, in0=g_, scalar1=1.0 - lambda_init(2 + j), scalar2=None, op0=ALU.mult),
              reads=[gb_], writes=[gb_])

    cx.barrier()
    tblS = pr.carve(X, 8192, [512], F32)[0:8, :]; b_tbl = Buf()
    nfar = pr.sb("nfar", [8, 1])
    cx.op("dve", lambda e: e.memset(tblS, 0.0), writes=[b_tbl])
    rbT = pr.sb("rbT", [8, 32])
    with nc.allow_non_contiguous_dma(reason="bias table"):
        cx.dma("sp", rbT, rel_bias.rearrange("b h -> h b"), writes=[b_tbl])
    for (bk, d0, d1) in bucket_runs(384):
        cx.op("dve", lambda e, bk=bk, d0=d0, d1=d1: e.tensor_copy(out=tblS[:, 127 + d0:128 + d1],
                                                                 in_=rbT[:, bk:bk + 1].to_broadcast([8, d1 - d0 + 1])),
              reads=[b_tbl], writes=[b_tbl])
    cx.op("dve", lambda e: e.tensor_scalar(out=nfar, in0=rbT[:, 31:32], scalar1=-1.0, scalar2=None, op0=ALU.mult), reads=[b_tbl], writes=[b_tbl])
    cx.op("act", lambda e: e.activation(out=tblS[:, 127:512], in_=tblS[:, 127:512], func=AF.Exp, bias=nfar[:, 0:1]), reads=[b_tbl], writes=[b_tbl])
    b_tbld = Buf()
    cx.dma("sp", tbl, tblS, reads=[b_tbl], writes=[b_tbld])
    for hd in range(NH):
        hk = pr.carve(X, 0, [240], F32)
        cx.dma("sp", hk, bass.AP(tensor=tbl.tensor, offset=hd * 512, ap=[[1, 128], [1, 240]]), reads=[b_tbld], writes=[b_wres])
        pt, pb = pr.pget()
        cx.op("pe", lambda e: e.matmul(pt[:, :240], lhsT=anti, rhs=hk, start=True, stop=True), reads=[b_wres, pr.b_const], writes=[pb])
        cx.op("act", lambda e, hd=hd: e.activation(out=Bp[:, hd, :], in_=pt[:, :240], func=AF.Copy), reads=[pb], writes=[b_Bp])
    cx.op("dve", lambda e: e.memset(va[:, :, 128:130], 1.0), writes=[b_va])
    if fused:
        Bs_all = pr.sb("Bs_all", [128, NH, 8]); Bn_all = pr.sb("Bn_all", [8, NH, 8]); b_Bsn = Buf()
        for hd in range(NH):
            hs_ = pr.carve(X, 0, [8], F32); hn_ = pr.carve(X, 1024, [8], F32)[0:8, :]
            bh_ = Buf()
            with nc.allow_non_contiguous_dma(reason="hankel"):
                cx.dma("sp", hs_, bass.AP(tensor=tbl.tensor, offset=hd * 512 + 128, ap=[[1, 128], [1, 8]]), reads=[b_tbld], writes=[bh_, b_wres])
                cx.dma("sp", hn_, bass.AP(tensor=tbl.tensor, offset=hd * 512 + 120, ap=[[1, 8], [1, 8]]), reads=[b_tbld], writes=[bh_, b_wres])
            pt, pb = pr.pget()
            cx.op("pe", lambda e: e.matmul(pt[:, 0:8], lhsT=anti, rhs=hs_, start=True, stop=True), reads=[bh_, pr.b_const], writes=[pb])
            cx.op("act", lambda e, hd=hd: e.activation(out=Bs_all[:, hd, :], in_=pt[:, 0:8], func=AF.Copy), reads=[pb], writes=[b_Bsn])
            pt, pb = pr.pget()
            cx.op("pe", lambda e: e.matmul(pt[0:8, 0:8], lhsT=anti[0:8, 120:128], rhs=hn_, start=True, stop=True), reads=[bh_, pr.b_const], writes=[pb])
            cx.op("act", lambda e, hd=hd: e.activation(out=Bn_all[:, hd, :], in_=pt[0:8, 0:8], func=AF.Copy), reads=[pb], writes=[b_Bsn])
    cx.barrier()

    def proj_tm(Wcols, g64, g64b, tok_list, sink):
        for hf in range(2):
            cx.dma("pool", wres[:, hf, :, :], Wcols[:, hf * 512:(hf + 1) * 512].rearrange("(k p) c -> p k c", p=128), writes=[b_wres])
        for a in tok_list:
            for hf in range(2):
                pt, pb = pr.pget()
                for k in range(KC):
                    cx.op("pe", lambda e, k=k: e.matmul(pt[:, :], lhsT=xn[:, k, a:a + 128], rhs=wres[:, hf, k, :],
                                                       start=(k == 0), stop=(k == KC - 1)), reads=[b_xn, b_wres], writes=[pb], signal=(k == KC - 1))
                ko = kout[:, hf * 512:(hf + 1) * 512]
                if g64 is None:
                    cx.op("act", lambda e: e.activation(out=ko, in_=pt[:, :], func=AF.Copy), reads=[pb], writes=[b_kout])
                else:
                    cx.op("act", lambda e: e.activation(out=sqk, in_=pt[:, :], func=AF.Square), reads=[pb], writes=[b_sqk])
                    cx.op("dve", lambda e: e.tensor_reduce(out=ssk, in_=sqk.rearrange("p (g d) -> p g d", d=64), axis=AX.X, op=ALU.add),
                          reads=[b_sqk], writes=[b_ssk])
                    cx.op("dve", lambda e: e.tensor_scalar(out=ssk, in0=ssk, scalar1=1.0 / 64, scalar2=EPS, op0=ALU.mult, op1=ALU.add),
                          reads=[b_ssk], writes=[b_ssk])
                    cx.op("pool", lambda e: e.tensor_tensor(out=ssk, in0=ssk, in1=pr.nhalf[:, 0:8], op=ALU.pow),
                          reads=[b_ssk, pr.b_const], writes=[b_ssk])
                    cx.op("dve", lambda e: e.tensor_tensor(out=tmpk.rearrange("p (g d) -> p g d", d=64),
                                                          in0=pt[:, :].rearrange("p (g d) -> p g d", d=64),
                                                          in1=ssk.unsqueeze(2).to_broadcast([128, 8, 64]), op=ALU.mult),
                          reads=[pb, b_ssk], writes=[b_tmpk])
                    cx.op("dve", lambda e: e.tensor_tensor(out=ko.rearrange("p (g d) -> p g d", d=64),
                                                          in0=tmpk.rearrange("p (g d) -> p g d", d=64),
                                                          in1=g64.unsqueeze(1).to_broadcast([128, 8, 64]), op=ALU.mult),
                          reads=[b_tmpk, g64b], writes=[b_kout])
            sink(a)

    def head_transposes(a, dst_fn, dstb):
        for h0 in range(0, NH, 4):
            pt, pb = pr.pget()
            for j in range(4):
                cx.op("pe", lambda e, j=j: e.transpose(pt[:, j * 128:(j + 1) * 128], kout[:, (h0 + j) * 128:(h0 + j + 1) * 128], pr.ident),
                      reads=[b_kout, pr.b_const], writes=[pb], signal=(j == 3))
            cx.op("act", lambda e: e.activation(out=dst_fn(h0), in_=pt[:, :].rearrange("p (a b) -> p a b", a=4), func=AF.Copy),
                  reads=[pb], writes=[dstb])


    def sample_attention(j):
        idx = pr.carve(R, 0, [NBS * NPG], I32); b_idx = Buf()
        Qblk = pr.carve(R, 1024, [NBS * NH, 16], BF16); b_Q = Buf()
        KnT = pr.carve(R, 5120, [NH, NS], BF16); b_KnT = Buf()
        kvs = [(pr.carve(R, 7168 + 4096 * i, [NH, 256], BF16), Buf()) for i in range(8)]
        xq = pr.carve(X, 0, [D], F32); b_xq = Buf()
        ktq = [(pr.carve(X, 4096 + 1024 * i, [4, 128], BF16), Buf()) for i in range(2)]
        pTt = [(pr.carve(X, 6144 + 128 * i, [64], BF16), Buf()) for i in range(2)]
        pNt = [(pr.carve(X, 6400 + 64 * i, [16], BF16)[0:8, :], Buf()) for i in range(2)]
        Vnt = [(pr.carve(X, 6656 + 2048 * i, [D], BF16)[0:8, :], Buf()) for i in range(2)]
        accS = pr.carve(X, 10752, [NH, 258], F32)[0:8]; b_acc = Buf()
        t1 = pr.carve(X, 19008, [NH, 128], F32)[0:8]; b_t1 = Buf()
        t2 = pr.carve(X, 23104, [NH, 128], F32)[0:8]; b_t2 = Buf()
        s8 = pr.carve(X, 27200, [4, NH], F32)[0:8]; b_s8 = Buf()
        otm = pr.carve(X, 19008, [KC, NS], F32); b_otm = Buf()
        pts = pr.carve(X, 27392, [NBS * NPG], I32); b_pts = Buf()
        iotp = pr.carve(X, 28416, [1], I32); iotpf = pr.carve(X, 28448, [1], F32); b_io = Buf()
        with nc.allow_non_contiguous_dma(reason="bcast"):
            cx.dma("sp", pts, ptab.partition_broadcast(128), writes=[b_pts])
        cx.op("pool", lambda e: e.iota(iotp, pattern=[[0, 1]], base=0, channel_multiplier=1), writes=[b_io])
        cx.op("dve", lambda e: e.tensor_copy(out=iotpf, in_=iotp), reads=[b_io], writes=[b_io])
        cx.op("dve", lambda e: e.tensor_scalar(out=idx, in0=pts, scalar1=128.0, scalar2=iotpf[:, 0:1], op0=ALU.mult, op1=ALU.add),
              reads=[b_pts, b_io], writes=[b_idx])
        cx.op("dve", lambda e: e.memset(Qblk, 0.0), writes=[b_Q])
        cx.dma("sp", xq, qs, writes=[b_xq])
        for hd in range(NH):
            pt, pb = pr.pget()
            cx.op("pe", lambda e, hd=hd: e.transpose(pt[:, 0:128], xq[:, hd * 128:(hd + 1) * 128], pr.ident), reads=[b_xq, pr.b_const], writes=[pb])
            for c in range(2):
                dst = Qblk.rearrange("p (b h) s -> p b h s", h=NH)[c * 64:(c + 1) * 64, :, hd, c * 8:(c + 1) * 8]
                cx.op("act", lambda e, c=c, dst=dst: e.activation(out=dst, in_=pt[c * 64:(c + 1) * 64, 0:128].rearrange("p (b t) -> p b t", t=DS), func=AF.Copy),
                      reads=[pb], writes=[b_Q])
        cx.dma("sp", xq, ksi, writes=[b_xq])
        for hd in range(NH):
            pt, pb = pr.pget()
            cx.op("pe", lambda e, hd=hd: e.transpose(pt[:, 0:128], xq[:, hd * 128:(hd + 1) * 128], pr.ident), reads=[b_xq, pr.b_const], writes=[pb])
            cx.op("act", lambda e, hd=hd: e.activation(out=KnT[:, hd, :], in_=pt[:, 0:128], func=AF.Copy), reads=[pb], writes=[b_KnT])
        gs_, gsb_ = gsub[j]
        cnt = 0
        for b in range(NBS):
            vn_, vnb = Vnt[b % 2]
            cx.dma("pool", vn_, vsi[b * DS:(b + 1) * DS, :], writes=[vnb])
            for g in range(4):
                sl = []
                for i in range(4):
                    t_, bf_ = kvs[(b * NPG + g * 4 + i) % 8]
                    col = b * NPG + g * 4 + i
                    cx.dma("pool", t_[:, 0:4, :].rearrange("p h c -> p (h c)"), ckvA, reads=[b_idx], writes=[bf_], indirect=idx[:, col:col + 1])
                    cx.dma("pool", t_[:, 4:8, :].rearrange("p h c -> p (h c)"), ckvB, reads=[b_idx], writes=[bf_], indirect=idx[:, col:col + 1])
                    sl.append((t_, bf_))
                for hd in range(NH):
                    pair = b * NH + hd
                    kt_, ktb = ktq[cnt % 2]
                    pT, pTb = pTt[cnt % 2]
                    pN, pNb = pNt[cnt % 2]
                    cnt += 1
                    pt, pb = pr.pget()
                    ptb = pt.bitcast(BF16)
                    for i in range(4):
                        t_, bf_ = sl[i]
                        cx.op("pe", lambda e, i=i, t_=t_: e.transpose(ptb[:, i * 128:(i + 1) * 128], t_[:, hd, 0:128], identb),
                              reads=[bf_, pr.b_const], writes=[pb], signal=(i == 3))
                    if cnt % 2 == 0:
                        cx.op("act", lambda e: e.activation(out=kt_, in_=ptb[:, 0:512].rearrange("p (a b) -> p a b", a=4), func=AF.Copy), reads=[pb], writes=[ktb])
                    else:
                        cx.op("dve", lambda e: e.tensor_copy(out=kt_, in_=ptb[:, 0:512].rearrange("p (a b) -> p a b", a=4)), reads=[pb], writes=[ktb])
                    sp_, spb = pr.pget()
                    for i in range(4):
                        cx.op("pe", lambda e, i=i: e.matmul(sp_[:, i * 16:(i + 1) * 16], lhsT=kt_[:, i, :], rhs=Qblk[:, pair, :], start=True, stop=True),
                              reads=[ktb, b_Q], writes=[spb], signal=(i == 3 and g < 3))
                    if g == 3:
                        cx.op("pe", lambda e: e.matmul(sp_[0:8, 64:80], lhsT=KnT[:, hd, b * DS:(b + 1) * DS], rhs=Qblk[:, pair, :], start=True, stop=True),
                              reads=[b_KnT, b_Q], writes=[spb])
                    cx.op("act", lambda e: e.activation(out=pT, in_=sp_[:, 0:64], func=AF.Exp, bias=bfar[:, hd:hd + 1], scale=0.125), reads=[spb, bfarb], writes=[pTb])
                    if g == 3:
                        cx.op("act", lambda e: e.activation(out=pN, in_=sp_[0:8, 64:80], func=AF.Exp, bias=bfar[0:8, hd:hd + 1], scale=0.125), reads=[spb, bfarb], writes=[pNb])
                        cx.op("dve", lambda e: e.tensor_tensor(out=pT[:, 48:64].rearrange("p (c t) -> p c t", c=2), in0=pT[:, 48:64].rearrange("p (c t) -> p c t", c=2),
                                                              in1=Bs_all[:, hd, :].unsqueeze(1).to_broadcast([128, 2, 8]), op=ALU.mult), reads=[pTb, b_Bsn], writes=[pTb])
                        cx.op("dve", lambda e: e.tensor_tensor(out=pN.rearrange("p (c t) -> p c t", c=2), in0=pN.rearrange("p (c t) -> p c t", c=2),
                                                              in1=Bn_all[:, hd, :].unsqueeze(1).to_broadcast([8, 2, 8]), op=ALU.mult), reads=[pNb, b_Bsn], writes=[pNb])
                    ac, acb = pr.pget()
                    for c in range(2):
                        for i in range(4):
                            t_, bf_ = sl[i]
                            cx.op("pe", lambda e, c=c, i=i, t_=t_: e.matmul(ac[0:8, c * 129:c * 129 + 128], lhsT=pT[:, i * 16 + c * 8:i * 16 + c * 8 + 8],
                                                                           rhs=t_[:, hd, 128:256], start=(i == 0), stop=(i == 3 and g < 3)),
                                  reads=[pTb, bf_], writes=[acb], signal=False)
                        if g == 3:
                            cx.op("pe", lambda e, c=c: e.matmul(ac[0:8, c * 129:c * 129 + 128], lhsT=pN[:, c * 8:(c + 1) * 8], rhs=vn_[:, hd * 128:(hd + 1) * 128],
                                                               start=False, stop=True), reads=[pNb, vnb], writes=[acb], signal=False)
                        for i in range(4):
                            cx.op("pe", lambda e, c=c, i=i: e.matmul(ac[0:8, c * 129 + 128:c * 129 + 129], lhsT=pT[:, i * 16 + c * 8:i * 16 + c * 8 + 8],
                                                                    rhs=pr.ones[:, 0:1], start=(i == 0), stop=(i == 3 and g < 3)),
                                  reads=[pTb, pr.b_const], writes=[acb], signal=(i == 3 and g < 3 and c == 1))
                        if g == 3:
                            cx.op("pe", lambda e, c=c: e.matmul(ac[0:8, c * 129 + 128:c * 129 + 129], lhsT=pN[:, c * 8:(c + 1) * 8], rhs=pr.ones[0:8, 0:1],
                                                               start=False, stop=True), reads=[pNb, pr.b_const], writes=[acb], signal=(c == 1))
                    if g == 0:
                        cx.op("dve", lambda e: e.tensor_copy(out=accS[:, hd, :], in_=ac[0:8, 0:258]), reads=[acb], writes=[b_acc])
                    else:
                        cx.op("dve", lambda e: e.tensor_tensor(out=accS[:, hd, :], in0=ac[0:8, 0:258], in1=accS[:, hd, :], op=ALU.add),
                              reads=[acb, b_acc], writes=[b_acc])
            av = accS.rearrange("p h (c e) -> p h c e", c=2)
            cx.op("dve", lambda e: e.reciprocal(out=s8[:, 0:2, :].rearrange("p c h -> p h c"), in_=av[:, :, :, 128]), reads=[b_acc], writes=[b_s8])
            cx.op("dve", lambda e: e.tensor_scalar(out=s8[:, 2, :], in0=s8[:, 1, :], scalar1=lam[0:8, j:j + 1], scalar2=None, op0=ALU.mult), reads=[b_s8, b_lam], writes=[b_s8])
            cx.op("dve", lambda e: e.tensor_tensor(out=t1, in0=av[:, :, 0, 0:128], in1=s8[:, 0, :].unsqueeze(2).to_broadcast([8, NH, 128]), op=ALU.mult),
                  reads=[b_acc, b_s8], writes=[b_t1])
            cx.op("dve", lambda e: e.tensor_tensor(out=t2, in0=av[:, :, 1, 0:128], in1=s8[:, 2, :].unsqueeze(2).to_broadcast([8, NH, 128]), op=ALU.mult),
                  reads=[b_acc, b_s8], writes=[b_t2])
            cx.op("dve", lambda e: e.tensor_tensor(out=t2, in0=t2, in1=t1, op=ALU.add), reads=[b_t1, b_t2], writes=[b_t2])
            cx.op("dve", lambda e: e.tensor_tensor(out=t1, in0=t2, in1=t2, op=ALU.mult), reads=[b_t2], writes=[b_t1])
            cx.op("dve", lambda e: e.tensor_reduce(out=s8[:, 3, :], in_=t1, axis=AX.X, op=ALU.add), reads=[b_t1], writes=[b_s8])
            cx.op("dve", lambda e: e.tensor_scalar(out=s8[:, 3, :], in0=s8[:, 3, :], scalar1=1.0 / 128, scalar2=EPS, op0=ALU.mult, op1=ALU.add), reads=[b_s8], writes=[b_s8])
            cx.op("pool", lambda e: e.tensor_tensor(out=s8[:, 3, :], in0=s8[:, 3, :], in1=pr.nhalf[0:8, 0:NH], op=ALU.pow), reads=[b_s8, pr.b_const], writes=[b_s8])
            cx.op("dve", lambda e: e.tensor_tensor(out=t1, in0=t2, in1=s8[:, 3, :].unsqueeze(2).to_broadcast([8, NH, 128]), op=ALU.mult), reads=[b_t2, b_s8], writes=[b_t1])
            cx.op("dve", lambda e: e.tensor_tensor(out=t2, in0=t1, in1=gs_[0:8, :].unsqueeze(1).to_broadcast([8, NH, 128]), op=ALU.mult), reads=[b_t1, gsb_], writes=[b_t2])
            cx.dma("sp", osd[b * DS:(b + 1) * DS, :], t2, reads=[b_t2])
        cx.barrier()
        pr.load_tm(osd, NS, otm, b_otm, 0, xq, b_xq)
        for kc in range(KC):
            cx.op("dve", lambda e, kc=kc: e.tensor_copy(out=xn[:, kc, P:T], in_=otm[:, kc, :]), reads=[b_otm], writes=[b_xn])

    ptok = list(range(0, P, 128))
    for j in range(2):
        l = 2 + j
        if j == 0:
            kvn, kvnb = pfm["kvn"]
            pr.rmsnorm(kvn, kvnb, 0, T)

            def sink_k(a):
                if a < P:
                    cx.dma("sp", kp[a:a + 128, :], kout, reads=[b_kout])
                    head_transposes(a, lambda h0: ktile[:, h0:h0 + 4, :], b_ktile)
                    cx.dma("sp", KTs[:, :, a:a + 128].rearrange("h p t -> p h t"), ktile, reads=[b_ktile])
                else:
                    cx.dma("sp", ks, kout, reads=[b_kout])
                    if fused:
                        cx.dma("sp", ksi, kout, reads=[b_kout])
            proj_tm(w_kv[:, 0:D], gk, gkb, ptok + [P], sink_k)
            cx.barrier()

            def sink_v(a):
                if a < P:
                    cx.dma("sp", vp[a:a + 128, :], kout, reads=[b_kout])
                    cx.op("act", lambda e: e.activation(out=vb, in_=kout, func=AF.Copy), reads=[b_kout], writes=[b_vb])
                    cx.dma("sp", Vsc[a:a + 128, :], vb, reads=[b_vb])
                else:
                    cx.dma("sp", vs, kout, reads=[b_kout])
                    if fused:
                        cx.dma("sp", vsi, kout, reads=[b_kout])
            proj_tm(w_kv[:, D:2 * D], None, None, ptok + [P], sink_v)
            cx.barrier()
        b_scr = Buf()
        an, anb = pfm["an%d" % j]
        pr.rmsnorm(an, anb, 0, T if (j == 0 or fused) else P)

        def sink_q(a):
            if a < P:
                head_transposes(a, lambda h0: QT[:, h0:h0 + 4, a:a + 128], b_QT)
            else:
                cx.dma("sp", qs if fused else q2, kout, reads=[b_kout])
        gqj, gqjb = gq[j]
        proj_tm(w_q[j], gqj, gqjb, ptok + ([P] if (j == 0 or fused) else []), sink_q)
        cx.barrier()
        QC = 256
        acc = [pr.banks[6], pr.banks[7]]
        gs_, gsb_ = gsub[j]
        pti = 0
        for hd in range(NH):
            cx.dma("sp", KTh, KTs[hd], writes=[b_KTh])
            cx.dma("sp", va[:, :, 0:128], Vsc.rearrange("(kt p) (h e) -> p kt h e", p=128, h=NH)[:, :, hd, :], writes=[b_va])
            for q0 in range(0, P, QC):
                nqb = QC // 128
                for c in range(2):
                    cx.op("dve", lambda e, c=c: e.memset(acc[c][0][:, 0:nqb * 129], 0.0), writes=[acc[c][1]])
                for kt in range((q0 + QC) // 128):
                    ks_ = kt * 128
                    q_lo = max(q0, ks_)
                    n = q0 + QC - q_lo
                    for c in range(2):
                        pt, pb = pr.pget()
                        cx.op("pe", lambda e, c=c: e.matmul(pt[:, :n], lhsT=KTh[c * 64:(c + 1) * 64, ks_:ks_ + 128],
                                                           rhs=QT[c * 64:(c + 1) * 64, hd, q_lo:q_lo + n], start=True, stop=True),
                              reads=[b_KTh, b_QT], writes=[pb])
                        pT, pTb = pTs[pti % 4]; pti += 1
                        cx.op("act", lambda e, pT=pT: e.activation(out=pT[:, :n], in_=pt[:, :n], func=AF.Exp, bias=bfar[:, hd:hd + 1], scale=0.125),
                              reads=[pb, bfarb], writes=[pTb])
                        w0 = max(q_lo, ks_); w1 = min(q_lo + n, ks_ + 240)
                        if w1 > w0:
                            cx.op("dve", lambda e, pT=pT: e.tensor_tensor(out=pT[:, w0 - q_lo:w1 - q_lo], in0=pT[:, w0 - q_lo:w1 - q_lo],
                                                                        in1=Bp[:, hd, w0 - ks_:w1 - ks_], op=ALU.mult),
                                  reads=[pTb, b_Bp], writes=[pTb])
                        for qb in range((q_lo - q0) // 128, nqb):
                            gqb = q0 // 128 + qb
                            col = q0 + qb * 128 - q_lo
                            cx.op("pe", lambda e, c=c, qb=qb, col=col, pT=pT: e.matmul(
                                acc[c][0][:, qb * 129:(qb + 1) * 129], lhsT=pT[:, col:col + 128], rhs=va[:, kt, 0:129],
                                start=False, stop=(kt == gqb), skip_group_check=True), reads=[pTb, b_va], writes=[acc[c][1]])
                for qb in range(nqb):
                    a = q0 + qb * 128
                    o0_, o1_ = qb * 129, qb * 129 + 128
                    for c in range(2):
                        cx.op("dve", lambda e, c=c: e.reciprocal(out=small[:, c:c + 1], in_=acc[c][0][:, o1_:o1_ + 1]),
                              reads=[acc[c][1]], writes=[b_small])
                    cx.op("dve", lambda e: e.tensor_tensor(out=small[:, 2:3], in0=small[:, 1:2], in1=lam[:, j:j + 1], op=ALU.mult),
                          reads=[b_small, b_lam], writes=[b_small])
                    cx.op("dve", lambda e: e.tensor_scalar(out=o1n, in0=acc[0][0][:, o0_:o1_], scalar1=small[:, 0:1], scalar2=None, op0=ALU.mult),
                          reads=[acc[0][1], b_small], writes=[b_o1n])
                    cx.op("dve", lambda e: e.scalar_tensor_tensor(out=odf, in0=acc[1][0][:, o0_:o1_], scalar=small[:, 2:3], in1=o1n,
                                                                 op0=ALU.mult, op1=ALU.add), reads=[acc[1][1], b_small, b_o1n], writes=[b_odf])
                    cx.op("act", lambda e: e.activation(out=junk, in_=odf, func=AF.Square, accum_out=small[:, 3:4]),
                          reads=[b_odf], writes=[b_junk, b_small])
                    cx.op("dve", lambda e: e.tensor_scalar(out=small[:, 3:4], in0=small[:, 3:4], scalar1=1.0 / 128, scalar2=EPS,
                                                          op0=ALU.mult, op1=ALU.add), reads=[b_small], writes=[b_small])
                    cx.op("pool", lambda e: e.tensor_tensor(out=small[:, 3:4], in0=small[:, 3:4], in1=pr.nhalf[:, 0:1], op=ALU.pow),
                          reads=[b_small, pr.b_const], writes=[b_small])
                    cx.op("dve", lambda e: e.scalar_tensor_tensor(out=onr, in0=odf, scalar=small[:, 3:4], in1=gs_, op0=ALU.mult, op1=ALU.mult),
                          reads=[b_odf, b_small, gsb_], writes=[b_onr])
                    pt, pb = pr.pget()
                    cx.op("pe", lambda e: e.transpose(pt[:, 0:128], onr, pr.ident), reads=[b_onr, pr.b_const], writes=[pb])
                    cx.op("act", lambda e: e.activation(out=xn[:, hd, a:a + 128], in_=pt[:, 0:128], func=AF.Copy), reads=[pb], writes=[b_xn])
        cx.barrier()
        if fused:
            sample_attention(j)
            cx.barrier()
        pr.linear(xn, b_xn, KC, w_o[j], D, (tiles(0, P) + [(P, NS)]) if fused else tiles(0, P), pr.add_to_h)
        sgt = [(pr.carve(X, 0, [512], F32), Buf()), (pr.carve(X, 2048, [512], F32), Buf())]
        fn, fnb = pfm["fn%d" % l]
        pr.ffn(fn, fnb, w_gate[l], w_up[l], w_down[l], 0, T if fused else P, sgt)
        cx.barrier()
        if j == 0:
            cx.op("dve", lambda e: e.memset(va[:, :, 128:130], 1.0), writes=[b_va])
            cx.barrier()

    if fused:
        stg, stgb = xin[1]
        pr.store_tm(h, b_h, P, NS, ys, stg, stgb)
    for i, a in enumerate(range(0, P, 128)):
        stg, stgb = xin[i % 2]
        pr.store_tm(h, b_h, a, 128, yp[a:a + 128, :], stg, stgb)
    cx.finish("sp")
    return nc


def build_L3(with_q):
    T = NS
    pr = Prog(T, KC * T * 2, 32768)
    nc, cx = pr.nc, pr.cx
    din, dout = pr.din, pr.dout
    hs = din("hs", [NS, D]); oat = din("oat", [NS, D])
    w_o = din("w_o", [D, D]); ffn_norm = din("ffn_norm", [D])
    wg = din("w_gate", [D, DFF]); wu = din("w_up", [D, DFF]); wd = din("w_down", [DFF, D])
    hs_out = dout("hs_out", [NS, D])
    if with_q:
        attn_norm = din("attn_norm", [D]); w_q = din("w_q", [D, D]); q_norm = din("q_norm", [64])
        q_out = dout("q_out", [NS, D])
    h, b_h, xn, b_xn, X = pr.h, pr.b_h, pr.xn, pr.b_xn, pr.X
    xin = (pr.carve(X, 0, [D], F32), Buf())
    otm = pr.sb("otm", [128, KC, T]); b_otm = Buf()
    pr.load_tm(hs, NS, h, b_h, 0, xin[0], xin[1])
    pr.load_tm(oat, NS, otm, b_otm, 0, xin[0], xin[1])
    for kc in range(KC):
        cx.op("dve", lambda e, kc=kc: e.tensor_copy(out=xn[:, kc, :], in_=otm[:, kc, :]), reads=[b_otm], writes=[b_xn])
    pr.linear(xn, b_xn, KC, w_o, D, [(0, NS)], pr.add_to_h)
    fn, fnb = pr.param_fm("fn", ffn_norm, 8)
    sgt = [(pr.carve(X, 8192, [512], F32), Buf()), (pr.carve(X, 10240, [512], F32), Buf())]
    pr.ffn(fn, fnb, wg, wu, wd, 0, T, sgt)
    stg = (pr.carve(X, 4096, [D], F32), Buf())
    pr.store_tm(h, b_h, 0, NS, hs_out, stg[0], stg[1])
    if with_q:
        an, anb = pr.param_fm("an", attn_norm, 8)
        pr.rmsnorm(an, anb, 0, T)
        gq, gqb = pr.param_bc("gq", q_norm, 64)
        wres = pr.carve(X, 12288, [2, KC, 512], BF16); b_wres = Buf()
        kout = pr.carve(X, 28672, [D], F32); b_kout = Buf()
        sqk = pr.sb("sqk", [128, 512]); b_sqk = Buf()
        tmpk = pr.sb("tmpk", [128, 512]); b_tmpk = Buf()
        ssk = pr.sb("ssk", [128, 8]); b_ssk = Buf()
        for hf in range(2):
            cx.dma("pool", wres[:, hf, :, :], w_q[:, hf * 512:(hf + 1) * 512].rearrange("(k p) c -> p k c", p=128), writes=[b_wres])
        for hf in range(2):
            pt, pb = pr.pget()
            for k in range(KC):
                cx.op("pe", lambda e, k=k: e.matmul(pt[:, :], lhsT=xn[:, k, 0:128], rhs=wres[:, hf, k, :], start=(k == 0), stop=(k == KC - 1)),
                      reads=[b_xn, b_wres], writes=[pb], signal=(k == KC - 1))
            ko = kout[:, hf * 512:(hf + 1) * 512]
            cx.op("act", lambda e: e.activation(out=sqk, in_=pt[:, :], func=AF.Square), reads=[pb], writes=[b_sqk])
            cx.op("dve", lambda e: e.tensor_reduce(out=ssk, in_=sqk.rearrange("p (g d) -> p g d", d=64), axis=AX.X, op=ALU.add),
                  reads=[b_sqk], writes=[b_ssk])
            cx.op("dve", lambda e: e.tensor_scalar(out=ssk, in0=ssk, scalar1=1.0 / 64, scalar2=EPS, op0=ALU.mult, op1=ALU.add),
                  reads=[b_ssk], writes=[b_ssk])
            cx.op("pool", lambda e: e.tensor_tensor(out=ssk, in0=ssk, in1=pr.nhalf[:, 0:8], op=ALU.pow), reads=[b_ssk, pr.b_const], writes=[b_ssk])
            cx.op("dve", lambda e: e.tensor_tensor(out=tmpk.rearrange("p (g d) -> p g d", d=64), in0=pt[:, :].rearrange("p (g d) -> p g d", d=64),
                                                  in1=ssk.unsqueeze(2).to_broadcast([128, 8, 64]), op=ALU.mult), reads=[pb, b_ssk], writes=[b_tmpk])
            cx.op("dve", lambda e: e.tensor_tensor(out=ko.rearrange("p (g d) -> p g d", d=64), in0=tmpk.rearrange("p (g d) -> p g d", d=64),
                                                  in1=gq.unsqueeze(1).to_broadcast([128, 8, 64]), op=ALU.mult), reads=[b_tmpk, gqb], writes=[b_kout])
        cx.dma("sp", q_out, kout, reads=[b_kout])
    cx.finish("sp")
    return nc


def build_ATT(npool, layer):
    NB = 128
    nc = bass.Bass("TRN2", target_bir_lowering=False)
    cx = Ctx(nc)
    sb = lambda name, shape, dt=F32: nc.alloc_sbuf_tensor(name, list(shape), dt)[:]
    din = lambda name, shape, dt=F32: nc.dram_tensor(name, list(shape), dt, kind="ExternalInput").ap()
    ckv = din("ckv", [npool * 128, 256]); ptab = din("ptab", [NB * NPG], I32)
    qd = din("q", [NB * DS, 128]); knd = din("kn", [NB * DS, 128]); vnd = din("vn", [NB * DS, 128])
    rb = din("rb", [32]); lq1 = din("lq1", [64]); lk1 = din("lk1", [64]); lq2 = din("lq2", [64]); lk2 = din("lk2", [64])
    gsd = din("gsub", [128]); ident_d = din("ident", [128, 128]); anti_d = din("anti", [128, 128])
    od = nc.dram_tensor("o", [NB * DS, 128], F32, kind="ExternalOutput").ap()
    tbl = nc.dram_tensor("tbl", [512], F32, kind="Internal").ap()
    banks = [(nc.alloc_psum_tensor("ps%d" % i, [128, 512], F32)[:], Buf()) for i in range(8)]
    st = {"i": 0}

    def pget():
        t, b = banks[st["i"]]
        st["i"] = (st["i"] + 1) % 8
        return t, b
    b_c = Buf()
    ident = sb("ident_s", [128, 128]); anti = sb("anti_s", [128, 128]); identb = sb("identb", [128, 128], BF16)
    nhalf = sb("nhalf", [128, 128])
    xin = sb("xin", [128, 128]); b_xin = Buf()
    cx.dma("sp", ident, ident_d, writes=[b_c])
    cx.dma("sp", anti, anti_d, writes=[b_c])
    cx.op("dve", lambda e: e.tensor_copy(out=identb, in_=ident), reads=[b_c], writes=[b_c])
    cx.op("dve", lambda e: e.memset(nhalf, -0.5), writes=[b_c])
    pts = sb("pts", [128, NB * NPG], I32); b_pts = Buf()
    with nc.allow_non_contiguous_dma(reason="bcast"):
        cx.dma("sp", pts, ptab.partition_broadcast(128), writes=[b_pts])
    ioti = sb("ioti", [128, 1], I32); iotf = sb("iotf", [128, 1]); b_io = Buf()
    cx.op("pool", lambda e: e.iota(ioti, pattern=[[0, 1]], base=0, channel_multiplier=1), writes=[b_io])
    cx.op("dve", lambda e: e.tensor_copy(out=iotf, in_=ioti), reads=[b_io], writes=[b_io])
    idx = sb("idx", [128, NB * NPG], I32); b_idx = Buf()
    cx.op("dve", lambda e: e.tensor_scalar(out=idx, in0=pts, scalar1=128.0, scalar2=iotf[:, 0:1], op0=ALU.mult, op1=ALU.add),
          reads=[b_pts, b_io], writes=[b_idx])
    Qblk = sb("Qblk", [128, NB, 16], BF16); b_Q = Buf()
    KnT = sb("KnT", [128, NB * DS], BF16); b_KnT = Buf()
    cx.op("dve", lambda e: e.memset(Qblk, 0.0), writes=[b_Q])
    for i in range(NB * DS // 128):
        cx.dma("sp", xin, qd[i * 128:(i + 1) * 128, :], writes=[b_xin])
        pt, pb = pget()
        cx.op("pe", lambda e: e.transpose(pt[:, 0:128], xin, ident), reads=[b_xin, b_c], writes=[pb])
        for c in range(2):
            cx.op("act", lambda e, c=c: e.activation(out=Qblk[c * 64:(c + 1) * 64, i * 16:(i + 1) * 16, c * 8:(c + 1) * 8],
                                                     in_=pt[c * 64:(c + 1) * 64, 0:128].rearrange("p (b t) -> p b t", t=DS), func=AF.Copy),
                  reads=[pb], writes=[b_Q])
        cx.dma("sp", xin, knd[i * 128:(i + 1) * 128, :], writes=[b_xin])
        pt, pb = pget()
        cx.op("pe", lambda e: e.transpose(pt[:, 0:128], xin, ident), reads=[b_xin, b_c], writes=[pb])
        cx.op("act", lambda e: e.activation(out=KnT[:, i * 128:(i + 1) * 128], in_=pt[:, 0:128], func=AF.Copy), reads=[pb], writes=[b_KnT])
    Vn = sb("Vn", [DS, NB, 130], BF16); b_Vn = Buf()
    cx.op("dve", lambda e: e.memset(Vn[:, :, 128:130], 1.0), writes=[b_Vn])
    cx.dma("pool", Vn[:, :, 0:128], vnd.rearrange("(b t) e -> t b e", t=DS), writes=[b_Vn])
    tblS = sb("tblS", [1, 512]); b_tbl = Buf(); nfar = sb("nfar", [1, 1])
    cx.op("dve", lambda e: e.memset(tblS, 0.0), writes=[b_tbl])
    rbb = sb("rbb", [128, 32]); bfar = sb("bfar", [128, 1]); b_bfar = Buf()
    with nc.allow_non_contiguous_dma(reason="bias table"):
        cx.dma("sp", rbb, rb.partition_broadcast(128), writes=[b_tbl])
    for (bk, d0, d1) in bucket_runs(384):
        cx.op("dve", lambda e, bk=bk, d0=d0, d1=d1: e.tensor_copy(out=tblS[:, 127 + d0:128 + d1],
                                                                 in_=rbb[0:1, bk:bk + 1].to_broadcast([1, d1 - d0 + 1])),
              reads=[b_tbl], writes=[b_tbl])
    cx.op("dve", lambda e: e.tensor_scalar(out=nfar, in0=rbb[0:1, 31:32], scalar1=-1.0, scalar2=None, op0=ALU.mult), reads=[b_tbl], writes=[b_tbl])
    cx.op("dve", lambda e: e.tensor_copy(out=bfar, in_=rbb[:, 31:32]), reads=[b_tbl], writes=[b_bfar])
    cx.op("act", lambda e: e.activation(out=tblS[:, 127:512], in_=tblS[:, 127:512], func=AF.Exp, bias=nfar[:, 0:1]), reads=[b_tbl], writes=[b_tbl])
    b_tbld = Buf()
    cx.dma("sp", tbl.rearrange("(a n) -> a n", a=1), tblS, reads=[b_tbl], writes=[b_tbld])
    Hs = sb("Hs", [128, 8]); Hn = sb("Hn", [8, 8]); b_H = Buf()
    Bs = sb("Bs", [128, 8]); Bn = sb("Bn", [8, 8]); b_B = Buf()
    with nc.allow_non_contiguous_dma(reason="hankel"):
        cx.dma("sp", Hs, bass.AP(tensor=tbl.tensor, offset=128, ap=[[1, 128], [1, 8]]), reads=[b_tbld], writes=[b_H])
        cx.dma("sp", Hn, bass.AP(tensor=tbl.tensor, offset=120, ap=[[1, 8], [1, 8]]), reads=[b_tbld], writes=[b_H])
    pt, pb = pget()
    cx.op("pe", lambda e: e.matmul(pt[:, 0:8], lhsT=anti, rhs=Hs, start=True, stop=True), reads=[b_H, b_c], writes=[pb])
    cx.op("act", lambda e: e.activation(out=Bs, in_=pt[:, 0:8], func=AF.Copy), reads=[pb], writes=[b_B])
    pt, pb = pget()
    cx.op("pe", lambda e: e.matmul(pt[0:8, 0:8], lhsT=anti[0:8, 120:128], rhs=Hn, start=True, stop=True), reads=[b_H, b_c], writes=[pb])
    cx.op("act", lambda e: e.activation(out=Bn, in_=pt[0:8, 0:8], func=AF.Copy), reads=[pb], writes=[b_B])
    lt = sb("lt", [8, 4, 64]); b_lt = Buf(); sm = sb("sm", [8, 4]); b_sm = Buf()
    gs = sb("gs", [8, 128]); b_gs = Buf()
    with nc.allow_non_contiguous_dma(reason="bcast"):
        for i, src in enumerate((lq1, lk1, lq2, lk2)):
            cx.dma("sp", lt[:, i, :], src.partition_broadcast(8), writes=[b_lt])
        cx.dma("sp", gs, gsd.partition_broadcast(8), writes=[b_gs])
    cx.op("dve", lambda e: e.tensor_tensor(out=lt[:, 0, :], in0=lt[:, 0, :], in1=lt[:, 1, :], op=ALU.mult), reads=[b_lt], writes=[b_lt])
    cx.op("dve", lambda e: e.tensor_tensor(out=lt[:, 2, :], in0=lt[:, 2, :], in1=lt[:, 3, :], op=ALU.mult), reads=[b_lt], writes=[b_lt])
    cx.op("dve", lambda e: e.tensor_reduce(out=sm[:, 0:1], in_=lt[:, 0, :], axis=AX.X, op=ALU.add), reads=[b_lt], writes=[b_sm])
    cx.op("dve", lambda e: e.tensor_reduce(out=sm[:, 1:2], in_=lt[:, 2, :], axis=AX.X, op=ALU.add), reads=[b_lt], writes=[b_sm])
    cx.op("act", lambda e: e.activation(out=sm[:, 0:2], in_=sm[:, 0:2], func=AF.Exp), reads=[b_sm], writes=[b_sm])
    cx.op("dve", lambda e: e.scalar_tensor_tensor(out=sm[:, 2:3], in0=sm[:, 1:2], scalar=-lambda_init(layer), in1=sm[:, 0:1],
                                                 op0=ALU.add, op1=ALU.subtract), reads=[b_sm], writes=[b_sm])
    cx.op("dve", lambda e: e.tensor_scalar(out=gs, in0=gs, scalar1=1.0 - lambda_init(layer), scalar2=None, op0=ALU.mult),
          reads=[b_gs], writes=[b_gs])
    NSL = 32
    kvs = []
    for i in range(NSL):
        t = sb("kv%d" % i, [128, 258], BF16)
        kvs.append((t, Buf()))
    b_ones = Buf()
    for t, _ in kvs:
        cx.op("dve", lambda e, t=t: e.memset(t[:, 256:258], 1.0), writes=[b_ones])
    for t, bf in kvs:
        bf.w = b_ones.w
    ktT = [(sb("ktT%d" % i, [128, NPG, 128], BF16), Buf()) for i in range(2)]
    pTt = [(sb("pT%d" % i, [128, 256], BF16), Buf()) for i in range(2)]
    pNt = [(sb("pN%d" % i, [8, 16], BF16), Buf()) for i in range(2)]
    HB = 64
    obuf = sb("obuf", [8, HB, 258]); b_ob = Buf()
    rden = sb("rden", [8, HB, 2]); b_rd = Buf()
    ss = sb("ss", [8, HB]); b_ss = Buf()

    def post(b0):
        ov = obuf.rearrange("p b (c e) -> p b c e", c=2)
        cx.op("dve", lambda e: e.reciprocal(out=rden, in_=ov[:, :, :, 128]), reads=[b_ob], writes=[b_rd])
        o1 = ov[:, :, 0, 0:128]
        o2 = ov[:, :, 1, 0:128]
        cx.op("dve", lambda e: e.tensor_tensor(out=o1, in0=o1, in1=rden[:, :, 0:1].to_broadcast([8, HB, 128]), op=ALU.mult), reads=[b_ob, b_rd], writes=[b_ob])
        cx.op("dve", lambda e: e.tensor_tensor(out=o2, in0=o2, in1=rden[:, :, 1:2].to_broadcast([8, HB, 128]), op=ALU.mult), reads=[b_ob, b_rd], writes=[b_ob])
        cx.op("dve", lambda e: e.scalar_tensor_tensor(out=o1, in0=o2, scalar=sm[:, 2:3], in1=o1, op0=ALU.mult, op1=ALU.add), reads=[b_ob, b_sm], writes=[b_ob])
        cx.op("dve", lambda e: e.tensor_tensor(out=o2, in0=o1, in1=o1, op=ALU.mult), reads=[b_ob], writes=[b_ob])
        cx.op("dve", lambda e: e.tensor_reduce(out=ss, in_=o2, axis=AX.X, op=ALU.add), reads=[b_ob], writes=[b_ss])
        cx.op("dve", lambda e: e.tensor_scalar(out=ss, in0=ss, scalar1=1.0 / 128, scalar2=EPS, op0=ALU.mult, op1=ALU.add), reads=[b_ss], writes=[b_ss])
        cx.op("pool", lambda e: e.tensor_tensor(out=ss, in0=ss, in1=nhalf[0:8, 0:HB], op=ALU.pow), reads=[b_ss, b_c], writes=[b_ss])
        cx.op("dve", lambda e: e.tensor_tensor(out=o1, in0=o1, in1=ss.unsqueeze(2).to_broadcast([8, HB, 128]), op=ALU.mult), reads=[b_ob, b_ss], writes=[b_ob])
        cx.op("dve", lambda e: e.tensor_tensor(out=o2, in0=o1, in1=gs.unsqueeze(1).to_broadcast([8, HB, 128]), op=ALU.mult), reads=[b_ob, b_gs], writes=[b_ob])
        cx.dma("sp", od[b0 * DS:(b0 + HB) * DS, :].rearrange("(b t) e -> t b e", t=DS), o2, reads=[b_ob])
    rows = ckv
    for b in range(NB):
        sl = []
        for j in range(NPG):
            t, bf = kvs[(b * NPG + j) % NSL]
            cx.dma("pool", t[:, 0:256], rows, reads=[b_idx], writes=[bf], indirect=idx[:, b * NPG + j:b * NPG + j + 1])
            sl.append((t, bf))
        kt_, ktb = ktT[b % 2]
        for g in range(4):
            pt, pb = pget()
            ptb = pt.bitcast(BF16)
            for jj in range(4):
                t, bf = sl[g * 4 + jj]
                cx.op("pe", lambda e, jj=jj, t=t: e.transpose(ptb[:, jj * 128:(jj + 1) * 128], t[:, 0:128], identb),
                      reads=[bf, b_c], writes=[pb], signal=(jj == 3))
            eng = "act" if g % 2 == 0 else "dve"
            if eng == "act":
                cx.op("act", lambda e, g=g: e.activation(out=kt_[:, g * 4:(g + 1) * 4, :], in_=ptb[:, 0:512].rearrange("p (a b) -> p a b", a=4), func=AF.Copy),
                      reads=[pb], writes=[ktb])
            else:
                cx.op("dve", lambda e, g=g: e.tensor_copy(out=kt_[:, g * 4:(g + 1) * 4, :], in_=ptb[:, 0:512].rearrange("p (a b) -> p a b", a=4)),
                      reads=[pb], writes=[ktb])
        sp_, spb = pget()
        for j in range(NPG):
            cx.op("pe", lambda e, j=j: e.matmul(sp_[:, j * 16:(j + 1) * 16], lhsT=kt_[:, j, :], rhs=Qblk[:, b, :], start=True, stop=True),
                  reads=[ktb, b_Q], writes=[spb], signal=False)
        cx.op("pe", lambda e: e.matmul(sp_[0:8, 256:272], lhsT=KnT[:, b * DS:(b + 1) * DS], rhs=Qblk[:, b, :], start=True, stop=True),
              reads=[b_KnT, b_Q], writes=[spb])
        pT, pTb = pTt[b % 2]
        pN, pNb = pNt[b % 2]
        cx.op("act", lambda e: e.activation(out=pT, in_=sp_[:, 0:256], func=AF.Exp, bias=bfar[:, 0:1], scale=0.125), reads=[spb, b_bfar], writes=[pTb])
        cx.op("act", lambda e: e.activation(out=pN, in_=sp_[0:8, 256:272], func=AF.Exp, bias=bfar[0:8, 0:1], scale=0.125), reads=[spb, b_bfar], writes=[pNb])
        cx.op("dve", lambda e: e.tensor_tensor(out=pT[:, 240:256].rearrange("p (c t) -> p c t", c=2), in0=pT[:, 240:256].rearrange("p (c t) -> p c t", c=2),
                                              in1=Bs.unsqueeze(1).to_broadcast([128, 2, 8]), op=ALU.mult), reads=[pTb, b_B], writes=[pTb])
        cx.op("dve", lambda e: e.tensor_tensor(out=pN.rearrange("p (c t) -> p c t", c=2), in0=pN.rearrange("p (c t) -> p c t", c=2),
                                              in1=Bn.unsqueeze(1).to_broadcast([8, 2, 8]), op=ALU.mult), reads=[pNb, b_B], writes=[pNb])
        ac, acb = pget()
        for c in range(2):
            for j in range(NPG):
                t, bf = sl[j]
                cx.op("pe", lambda e, c=c, j=j, t=t: e.matmul(ac[0:8, c * 129:(c + 1) * 129], lhsT=pT[:, j * 16 + c * 8:j * 16 + c * 8 + 8],
                                                             rhs=t[:, 128:257], start=(j == 0), stop=False),
                      reads=[pTb, bf], writes=[acb], signal=False)
            cx.op("pe", lambda e, c=c: e.matmul(ac[0:8, c * 129:(c + 1) * 129], lhsT=pN[0:8, c * 8:(c + 1) * 8], rhs=Vn[:, b, 0:129],
                                               start=False, stop=True), reads=[pNb, b_Vn], writes=[acb])
        cx.op("act", lambda e: e.activation(out=obuf[:, b % HB, :], in_=ac[0:8, 0:258], func=AF.Copy), reads=[acb], writes=[b_ob])
        if b % HB == HB - 1:
            post(b - HB + 1)
    cx.finish("sp")
    return nc


def kernel(x_prompt, x_sample, state_conv, cache_k, cache_v, page_table, rel_bias,
           conv_norm, w_pw1, b_pw1, w_dw, b_dw, conv_mid_norm, w_pw2, b_pw2,
           kv_norm, w_kv, k_norm, attn_norm, w_q, q_norm,
           lambda_q1, lambda_k1, lambda_q2, lambda_k2, sub_norm, w_o,
           ffn_norm, w_gate, w_up, w_down):
    f = lambda a: np.ascontiguousarray(np.asarray(a, dtype=np.float32))
    NCORE = 8
    x_prompt = f(x_prompt); x_sample = f(x_sample); state_conv = f(state_conv)
    B, P = x_prompt.shape[0], x_prompt.shape[1]
    assert B == NCORE and x_sample.shape[0] == NCORE * NBS
    ident = np.eye(128, dtype=np.float32)
    anti = np.ascontiguousarray(ident[::-1])
    wts = dict(rel_bias=f(rel_bias), conv_norm=f(conv_norm), w_pw1=f(w_pw1), b_pw1=f(b_pw1), w_dw=f(w_dw), b_dw=f(b_dw),
               conv_mid_norm=f(conv_mid_norm), w_pw2=f(w_pw2), b_pw2=f(b_pw2), kv_norm=f(kv_norm), w_kv=f(w_kv),
               k_norm=f(k_norm), attn_norm=f(attn_norm), w_q=f(w_q), q_norm=f(q_norm), lambda_q1=f(lambda_q1),
               lambda_k1=f(lambda_k1), lambda_q2=f(lambda_q2), lambda_k2=f(lambda_k2), sub_norm=f(sub_norm), w_o=f(w_o),
               ffn_norm=f(ffn_norm), w_gate=f(w_gate), w_up=f(w_up), w_down=f(w_down))
    cores = list(range(NCORE))
    if FUSED:
        ck = np.asarray(cache_k, dtype=np.float32); cv = np.asarray(cache_v, dtype=np.float32)
        npool = ck.shape[0]
        ckv = np.empty((npool * 128, NH, 256), np.float32)
        ckv[:, :, 0:128] = ck.reshape(npool * 128, NH, 128)
        ckv[:, :, 128:256] = cv.reshape(npool * 128, NH, 128)
        ckvA = np.ascontiguousarray(ckv[:, 0:4, :]).reshape(npool * 128, 1024)
        ckvB = np.ascontiguousarray(ckv[:, 4:8, :]).reshape(npool * 128, 1024)
        del ckv
        pt = np.asarray(page_table, dtype=np.int32)
        ncf = build_L1(P, fused=True, npool=npool)
        ins = []
        for c in cores:
            m = dict(wts)
            m.update(xp=x_prompt[c], xs=x_sample[c * NBS:(c + 1) * NBS].reshape(NS, D),
                     sc=np.ascontiguousarray(state_conv[:, c * NBS:(c + 1) * NBS]), ident=ident, anti=anti,
                     ckvA=ckvA, ckvB=ckvB, ptab=np.ascontiguousarray(pt[c * NBS:(c + 1) * NBS].reshape(-1)))
            ins.append(m)
        r1 = run_bass_kernel_spmd(ncf, ins, core_ids=cores).results
        y_prompt = np.stack([r1[c]["yp"] for c in cores])
        y_sample = np.concatenate([r1[c]["ys"] for c in cores], axis=0).reshape(NCORE * NBS, DS, D)
        conv_state_p = np.stack([r1[c]["csp"] for c in cores], axis=1)
        conv_state_s = np.concatenate([r1[c]["css"] for c in cores], axis=1)
        k_prompt = np.stack([r1[c]["kp"] for c in cores]).reshape(B, P, NH, 2, 64)
        v_prompt = np.stack([r1[c]["vp"] for c in cores]).reshape(B, P, NH, 128)
        k_sample = np.concatenate([r1[c]["ks"] for c in cores], axis=0).reshape(NCORE * NBS, DS, NH, 2, 64)
        v_sample = np.concatenate([r1[c]["vs"] for c in cores], axis=0).reshape(NCORE * NBS, DS, NH, 128)
        return (y_prompt, y_sample, conv_state_p, conv_state_s, k_prompt, v_prompt, k_sample, v_sample)
    nc1 = build_L1(P)
    ins = []
    for c in cores:
        m = dict(wts)
        m.update(xp=x_prompt[c], xs=x_sample[c * NBS:(c + 1) * NBS].reshape(NS, D),
                 sc=np.ascontiguousarray(state_conv[:, c * NBS:(c + 1) * NBS]), ident=ident, anti=anti)
        ins.append(m)
    r1 = run_bass_kernel_spmd(nc1, ins, core_ids=cores).results
    y_prompt = np.stack([r1[c]["yp"] for c in cores])
    conv_state_p = np.stack([r1[c]["csp"] for c in cores], axis=1)
    conv_state_s = np.concatenate([r1[c]["css"] for c in cores], axis=1)
    k_prompt = np.stack([r1[c]["kp"] for c in cores]).reshape(B, P, NH, 2, 64)
    v_prompt = np.stack([r1[c]["vp"] for c in cores]).reshape(B, P, NH, 128)
    ks_all = np.concatenate([r1[c]["ks"] for c in cores], axis=0)
    vs_all = np.concatenate([r1[c]["vs"] for c in cores], axis=0)
    k_sample = ks_all.reshape(NCORE * NBS, DS, NH, 2, 64)
    v_sample = vs_all.reshape(NCORE * NBS, DS, NH, 128)
    hs = [r1[c]["hs1"] for c in cores]
    q_all = np.concatenate([r1[c]["q2"] for c in cores], axis=0)
    ck = np.asarray(cache_k, dtype=np.float32); cv = np.asarray(cache_v, dtype=np.float32)
    npool = ck.shape[0]
    ckv = [np.ascontiguousarray(np.concatenate([ck[:, :, hd].reshape(npool * 128, 128), cv[:, :, hd].reshape(npool * 128, 128)], axis=1))
           for hd in range(NH)]
    ptab = np.ascontiguousarray(np.asarray(page_table, dtype=np.int32).reshape(-1))
    y_sample = None
    for j in range(2):
        nca = build_ATT(npool, 2 + j)
        ins = []
        for hd in cores:
            sl = slice(hd * 128, (hd + 1) * 128)
            ins.append(dict(ckv=ckv[hd], ptab=ptab, q=np.ascontiguousarray(q_all[:, sl]), kn=np.ascontiguousarray(ks_all[:, sl]),
                            vn=np.ascontiguousarray(vs_all[:, sl]), rb=np.ascontiguousarray(wts["rel_bias"][:, hd]),
                            lq1=wts["lambda_q1"][j], lk1=wts["lambda_k1"][j], lq2=wts["lambda_q2"][j], lk2=wts["lambda_k2"][j],
                            gsub=wts["sub_norm"][j], ident=ident, anti=anti))
        ra = run_bass_kernel_spmd(nca, ins, core_ids=cores).results
        o_all = np.concatenate([ra[hd]["o"] for hd in cores], axis=1)
        nc3 = build_L3(with_q=(j == 0))
        ins = []
        for c in cores:
            m = dict(hs=hs[c], oat=np.ascontiguousarray(o_all[c * NS:(c + 1) * NS]), w_o=wts["w_o"][j], ffn_norm=wts["ffn_norm"][2 + j],
                     w_gate=wts["w_gate"][2 + j], w_up=wts["w_up"][2 + j], w_down=wts["w_down"][2 + j], ident=ident)
            if j == 0:
                m.update(attn_norm=wts["attn_norm"][1], w_q=wts["w_q"][1], q_norm=wts["q_norm"][1])
            ins.append(m)
        r3 = run_bass_kernel_spmd(nc3, ins, core_ids=cores).results
        hs = [r3[c]["hs_out"] for c in cores]
        if j == 0:
            q_all = np.concatenate([r3[c]["q_out"] for c in cores], axis=0)
    y_sample = np.concatenate(hs, axis=0).reshape(NCORE * NBS, DS, D)
    return (y_prompt, y_sample, conv_state_p, conv_state_s, k_prompt, v_prompt, k_sample, v_sample)
```

```python
import math
import numpy as np
import concourse.bass as bass
import concourse.mybir as mybir
from concourse.bass_utils import run_bass_kernel_spmd

F32 = mybir.dt.float32
BF16 = mybir.dt.bfloat16
I32 = mybir.dt.int32
AF = mybir.ActivationFunctionType
ALU = mybir.AluOpType
AX = mybir.AxisListType

D = 1024
KC = 8
DFF = 2816
NH = 8
CW = 31
NBS = 16
DS = 8
NS = NBS * DS
EPS = 1e-6
NPG = 16
FUSED = True
SL = 38


def lambda_init(l):
    return 0.8 - 0.6 * math.exp(-0.3 * l)


def bucket_runs(maxd):
    n = np.arange(0, maxd + 1)
    nf = np.maximum(n, 1).astype(np.float32)
    large = 16 + (np.log(nf / np.float32(16)) / np.float32(math.log(8.0)) * np.float32(16)).astype(np.int32)
    large = np.minimum(large, 31)
    bk = np.where(n < 16, n, large)
    runs = []
    s = 0
    for i in range(1, maxd + 2):
        if i == maxd + 1 or bk[i] != bk[s]:
            runs.append((int(bk[s]), s, i - 1))
            s = i
    return runs


class Buf:
    __slots__ = ("w", "r")

    def __init__(self):
        self.w = None
        self.r = []


class Eng:
    def __init__(self, name, handle, sem):
        self.name = name
        self.h = handle
        self.sem = sem
        self.cnt = 0
        self.seen = {}
        self.pr = []
        self.pw = []


class Ctx:
    def __init__(self, nc, n_dma_sems=48):
        self.nc = nc
        self.E = {}
        for name, h in (("pe", nc.tensor), ("act", nc.scalar), ("dve", nc.vector),
                        ("pool", nc.gpsimd), ("sp", nc.sync)):
            self.E[name] = Eng(name, h, nc.alloc_semaphore("c_" + name))
        self.dsem = [[nc.alloc_semaphore("d%d" % i), 0] for i in range(n_dma_sems)]
        self.dnext = 0

    def _wait(self, eng, tok):
        if tok is None:
            return
        kind, key, val = tok
        k = (kind, key)
        if eng.seen.get(k, 0) >= val:
            return
        if kind == "e":
            if key == eng.name and key == "pe":
                return
            eng.h.wait_ge(self.E[key].sem, val)
        else:
            eng.h.wait_ge(self.dsem[key][0], val)
        eng.seen[k] = val

    def _deps(self, eng, reads, writes):
        for b in reads:
            self._wait(eng, b.w)
        for b in writes:
            self._wait(eng, b.w)
            for t in b.r:
                self._wait(eng, t)

    def _commit(self, tok, reads, writes):
        for b in reads:
            b.r = [t for t in b.r if not (t[0] == tok[0] and t[1] == tok[1])]
            b.r.append(tok)
        for b in writes:
            b.w = tok
            b.r = []

    def op(self, ename, fn, reads=(), writes=(), signal=True):
        eng = self.E[ename]
        self._deps(eng, reads, writes)
        inst = fn(eng.h)
        if signal:
            eng.cnt += 1
            inst.then_inc(eng.sem, 1)
            tok = ("e", ename, eng.cnt)
            self._commit(tok, list(reads) + eng.pr, list(writes) + eng.pw)
            eng.pr = []
            eng.pw = []
        else:
            eng.pr.extend(reads)
            eng.pw.extend(writes)
        return inst

    def dma(self, qname, out, in_, reads=(), writes=(), indirect=None):
        eng = self.E[qname]
        self._deps(eng, reads, writes)
        i = self.dnext
        self.dnext = (self.dnext + 1) % len(self.dsem)
        sem, val = self.dsem[i]
        if val > 0:
            self._wait(eng, ("d", i, val))
        if indirect is None:
            inst = eng.h.dma_start(out=out, in_=in_)
        else:
            inst = eng.h.indirect_dma_start(out=out, out_offset=None, in_=in_,
                                            in_offset=bass.IndirectOffsetOnAxis(ap=indirect, axis=0))
        inst.then_inc(sem, 16)
        self.dsem[i][1] = val + 16
        tok = ("d", i, val + 16)
        self._commit(tok, reads, writes)
        return tok

    def barrier(self):
        for eng in self.E.values():
            for i, (sem, val) in enumerate(self.dsem):
                if val > 0:
                    self._wait(eng, ("d", i, val))
            for name, e in self.E.items():
                if name != eng.name and e.cnt > 0:
                    self._wait(eng, ("e", name, e.cnt))

    def finish(self, qname="sp"):
        eng = self.E[qname]
        for i, (sem, val) in enumerate(self.dsem):
            if val > 0:
                self._wait(eng, ("d", i, val))
        for name, e in self.E.items():
            if name != qname and e.cnt > 0:
                self._wait(eng, ("e", name, e.cnt))


def tiles(t0, t1, step=512):
    out = []
    t = t0
    while t < t1:
        n = min(step, t1 - t)
        out.append((t, n))
        t += n
    return out


class Prog:
    def __init__(self, T, rbytes, xbytes):
        self.nc = nc = bass.Bass("TRN2", target_bir_lowering=False)
        self.cx = Ctx(nc)
        self.T = T
        self.banks = [(nc.alloc_psum_tensor("ps%d" % i, [128, 512], F32)[:], Buf()) for i in range(8)]
        self.ring_i = 0
        self.nring = 6
        self.ident_d = self.din("ident", [128, 128])
        self.ident = self.sb("ident_s", [128, 128])
        self.b_const = Buf()
        self.ones = self.sb("ones", [128, 128], BF16)
        self.nhalf = self.sb("nhalf", [128, 512])
        self.cx.dma("sp", self.ident, self.ident_d, writes=[self.b_const])
        self.cx.op("dve", lambda e: e.memset(self.ones, 1.0), writes=[self.b_const])
        self.cx.op("dve", lambda e: e.memset(self.nhalf, -0.5), writes=[self.b_const])
        self.h = self.sb("h", [128, KC, T])
        self.b_h = Buf()
        self.xn = self.sb("xn", [128, KC, T], BF16)
        self.b_xn = Buf()
        self.R = self.sb("R", [128, rbytes // 2], BF16)
        self.b_R = Buf()
        self.X = self.sb("X", [128, xbytes // 2], BF16)
        self.wring = [(self.sb("wr%d" % i, [128, 2048], BF16), Buf()) for i in range(4)]
        self.wi = 0
        self.sq = [(self.sb("sq%d" % i, [128, 512], BF16), Buf()) for i in range(2)]
        self.ms = self.sb("ms", [128, 512])
        self.b_ms = Buf()
        self.pcount = 0

    def sb(self, name, shape, dt=F32):
        return self.nc.alloc_sbuf_tensor(name, list(shape), dt)[:]

    def din(self, name, shape, dt=F32):
        return self.nc.dram_tensor(name, list(shape), dt, kind="ExternalInput").ap()

    def dout(self, name, shape, dt=F32):
        return self.nc.dram_tensor(name, list(shape), dt, kind="ExternalOutput").ap()

    def dint(self, name, shape, dt=F32):
        return self.nc.dram_tensor(name, list(shape), dt, kind="Internal").ap()

    def carve(self, arena, off, shape, dt):
        sz = 2 if dt == BF16 else 4
        n = int(np.prod(shape))
        assert off % 4 == 0
        v = arena[:, off // 2:(off + n * sz) // 2]
        if dt != BF16:
            v = v.bitcast(dt)
        if len(shape) == 2:
            v = v.rearrange("p (a b) -> p a b", a=shape[0])
        elif len(shape) == 3:
            v = v.rearrange("p (a b c) -> p a b c", a=shape[0], b=shape[1])
        return v

    def pget(self):
        t, b = self.banks[self.ring_i]
        self.ring_i = (self.ring_i + 1) % self.nring
        return t, b

    def param_fm(self, name, dram_vec, n):
        t = self.sb(name, [128, n])
        b = Buf()
        with self.nc.allow_non_contiguous_dma(reason="small param"):
            self.cx.dma("sp", t, dram_vec.rearrange("(k p) -> p k", p=128), writes=[b])
        return t, b

    def param_bc(self, name, dram_vec, n, parts=128):
        t = self.sb(name, [parts, n])
        b = Buf()
        with self.nc.allow_non_contiguous_dma(reason="bcast param"):
            self.cx.dma("sp", t, dram_vec.partition_broadcast(parts), writes=[b])
        return t, b

    def load_tm(self, rows_ap, nrows, dst, dstb, col0, xin, xinb):
        cx = self.cx
        cx.dma("sp", xin[:nrows, :], rows_ap, writes=[xinb])
        for k0 in range(0, KC, 4):
            pt, pb = self.pget()
            for j in range(4):
                cx.op("pe", lambda e, j=j: e.transpose(pt[:, j * 128:j * 128 + nrows],
                                                       xin[:nrows, (k0 + j) * 128:(k0 + j + 1) * 128],
                                                       self.ident[:nrows, :nrows]),
                      reads=[xinb, self.b_const], writes=[pb], signal=(j == 3))
            src = pt[:, :].rearrange("p (a b) -> p a b", a=4)[:, :, :nrows]
            cx.op("act", lambda e: e.activation(out=dst[:, k0:k0 + 4, col0:col0 + nrows], in_=src, func=AF.Copy),
                  reads=[pb], writes=[dstb])

    def store_tm(self, src, srcb, col0, nrows, rows_ap, stg, stgb, nk=KC):
        cx = self.cx
        for k0 in range(0, nk, 4):
            pt, pb = self.pget()
            for j in range(4):
                cx.op("pe", lambda e, j=j: e.transpose(pt[:nrows, j * 128:(j + 1) * 128],
                                                       src[:, k0 + j, col0:col0 + nrows], self.ident),
                      reads=[srcb, self.b_const], writes=[pb], signal=(j == 3))
            cx.op("dve", lambda e: e.tensor_copy(out=stg[:nrows, k0 * 128:(k0 + 4) * 128], in_=pt[:nrows, :]),
                  reads=[pb], writes=[stgb])
        cx.dma("sp", rows_ap, stg[:nrows, :nk * 128], reads=[stgb])

    def rmsnorm(self, g, gb, t0, t1, src=None, srcb=None):
        cx = self.cx
        h = self.h if src is None else src
        hb = self.b_h if srcb is None else srcb
        for (a, n) in tiles(t0, t1):
            pt, pb = self.pget()
            for kc in range(KC):
                s, sbf = self.sq[kc % 2]
                cx.op("act", lambda e, kc=kc, s=s: e.activation(out=s[:, :n], in_=h[:, kc, a:a + n], func=AF.Square),
                      reads=[hb], writes=[sbf])
                cx.op("pe", lambda e, kc=kc, s=s: e.matmul(pt[:, :n], lhsT=self.ones, rhs=s[:, :n],
                                                          start=(kc == 0), stop=(kc == KC - 1)),
                      reads=[sbf, self.b_const], writes=[pb])
            self.rstd_from(pt, pb, n, 1.0 / D)
            for kc in range(KC):
                cx.op("dve", lambda e, kc=kc: e.scalar_tensor_tensor(
                    out=self.xn[:, kc, a:a + n], in0=h[:, kc, a:a + n], scalar=g[:, kc:kc + 1],
                    in1=self.ms[:, :n], op0=ALU.mult, op1=ALU.mult),
                    reads=[hb, self.b_ms, gb], writes=[self.b_xn])

    def rstd_from(self, pt, pb, n, inv):
        cx = self.cx
        cx.op("dve", lambda e: e.tensor_scalar(out=self.ms[:, :n], in0=pt[:, :n], scalar1=inv, scalar2=EPS,
                                              op0=ALU.mult, op1=ALU.add), reads=[pb], writes=[self.b_ms])
        cx.op("pool", lambda e: e.tensor_tensor(out=self.ms[:, :n], in0=self.ms[:, :n], in1=self.nhalf[:, :n],
                                               op=ALU.pow), reads=[self.b_ms, self.b_const], writes=[self.b_ms])

    def wfetch(self, W_rows, c0, cb, kcs):
        slot, sbf = self.wring[self.wi]
        self.wi = (self.wi + 1) % len(self.wring)
        wv = slot[:, :kcs * 256].rearrange("p (k c) -> p k c", c=256)
        self.cx.dma("pool", wv[:, :, :cb], W_rows[:, c0:c0 + cb].rearrange("(k p) c -> p k c", p=128), writes=[sbf])
        return wv, sbf

    def linear(self, x, xb, kcs, W_rows, ncols, tok_tiles, epi, LA=2):
        cx = self.cx
        blocks = [(c0, min(256, ncols - c0)) for c0 in range(0, ncols, 256)]
        fetched = []
        for i in range(min(LA, len(blocks))):
            fetched.append(self.wfetch(W_rows, blocks[i][0], blocks[i][1], kcs))
        for i, (c0, cb) in enumerate(blocks):
            if i + LA < len(blocks):
                fetched.append(self.wfetch(W_rows, blocks[i + LA][0], blocks[i + LA][1], kcs))
            wv, sbf = fetched[i]
            for m in range(cb // 128):
                for (a, n) in tok_tiles:
                    pt, pb = self.pget()
                    for k in range(kcs):
                        cx.op("pe", lambda e, k=k: e.matmul(pt[:, :n], lhsT=wv[:, k, m * 128:(m + 1) * 128],
                                                           rhs=x[:, k, a:a + n], start=(k == 0), stop=(k == kcs - 1)),
                              reads=[sbf, xb], writes=[pb], signal=(k == kcs - 1))
                    epi((c0 // 128) + m, a, n, pt, pb)

    def add_to_h(self, m, a, n, pt, pb):
        self.cx.op("dve", lambda e: e.tensor_tensor(out=self.h[:, m, a:a + n], in0=pt[:, :n], in1=self.h[:, m, a:a + n],
                                                   op=ALU.add), reads=[pb, self.b_h], writes=[self.b_h])

    def ffn(self, g, gb, Wg, Wu, Wd, t0, t1, sgt):
        cx = self.cx
        self.rmsnorm(g, gb, t0, t1)
        tt = tiles(t0, t1)
        hid = self.carve(self.R, 0, [KC, self.T], BF16)
        for (f0, f1) in ((0, 8), (8, 15), (15, 22)):
            nf = f1 - f0
            blocks = [(c0, min(256, f1 * 128 - c0)) for c0 in range(f0 * 128, f1 * 128, 256)]
            fetched = []

            def fetch(i):
                c0, cb = blocks[i]
                fetched.append((self.wfetch(Wg, c0, cb, KC), self.wfetch(Wu, c0, cb, KC)))
            fetch(0)
            for i, (c0, cb) in enumerate(blocks):
                if i + 1 < len(blocks):
                    fetch(i + 1)
                (wg, gbf), (wu, ubf) = fetched[i]
                for m in range(cb // 128):
                    fi = (c0 - f0 * 128) // 128 + m
                    for (a, n) in tt:
                        pg, pgb = self.pget()
                        pu, pub = self.pget()
                        for k in range(KC):
                            cx.op("pe", lambda e, k=k: e.matmul(pg[:, :n], lhsT=wg[:, k, m * 128:(m + 1) * 128],
                                                               rhs=self.xn[:, k, a:a + n], start=(k == 0), stop=(k == KC - 1)),
                                  reads=[gbf, self.b_xn], writes=[pgb], signal=(k == KC - 1))
                        for k in range(KC):
                            cx.op("pe", lambda e, k=k: e.matmul(pu[:, :n], lhsT=wu[:, k, m * 128:(m + 1) * 128],
                                                               rhs=self.xn[:, k, a:a + n], start=(k == 0), stop=(k == KC - 1)),
                                  reads=[ubf, self.b_xn], writes=[pub], signal=(k == KC - 1))
                        sg, sgb = sgt[self.pcount % 2]
                        self.pcount += 1
                        cx.op("act", lambda e: e.activation(out=sg[:, :n], in_=pg[:, :n], func=AF.Silu),
                              reads=[pgb], writes=[sgb])
                        cx.op("dve", lambda e: e.tensor_tensor(out=hid[:, fi, a:a + n], in0=sg[:, :n], in1=pu[:, :n],
                                                              op=ALU.mult), reads=[sgb, pub], writes=[self.b_R])
            self.linear(hid, self.b_R, nf, Wd[f0 * 128:f1 * 128, :], D, tt, self.add_to_h)


def build_L1(P, fused=False, npool=0):
    T = P + NS
    UP = 30 + P
    UL = UP + NBS * SL
    rbytes = ((KC * UL * 2 + 63) // 64) * 64
    if fused:
        rbytes = max(rbytes, 40192)
    pr = Prog(T, rbytes, 30720)
    nc, cx = pr.nc, pr.cx
    din, dout = pr.din, pr.dout
    xp = din("xp", [P, D]); xs = din("xs", [NS, D]); sc = din("sc", [2, NBS, 30, D])
    anti_d = din("anti", [128, 128])
    rel_bias = din("rel_bias", [32, 8])
    conv_norm = din("conv_norm", [2, D]); w_pw1 = din("w_pw1", [2, D, 2 * D]); b_pw1 = din("b_pw1", [2, 2 * D])
    w_dw = din("w_dw", [2, CW, D]); b_dw = din("b_dw", [2, D]); conv_mid = din("conv_mid_norm", [2, D])
    w_pw2 = din("w_pw2", [2, D, D]); b_pw2 = din("b_pw2", [2, D])
    kv_norm = din("kv_norm", [D]); w_kv = din("w_kv", [D, 2 * D]); k_norm = din("k_norm", [64])
    attn_norm = din("attn_norm", [2, D]); w_q = din("w_q", [2, D, D]); q_norm = din("q_norm", [2, 64])
    lq1 = din("lambda_q1", [2, 64]); lk1 = din("lambda_k1", [2, 64]); lq2 = din("lambda_q2", [2, 64]); lk2 = din("lambda_k2", [2, 64])
    sub_norm = din("sub_norm", [2, 128]); w_o = din("w_o", [2, D, D])
    ffn_norm = din("ffn_norm", [4, D]); w_gate = din("w_gate", [4, D, DFF]); w_up = din("w_up", [4, D, DFF])
    w_down = din("w_down", [4, DFF, D])
    yp = dout("yp", [P, D]); csp = dout("csp", [2, 30, D]); css = dout("css", [2, NBS, 30, D])
    kp = dout("kp", [P, D]); vp = dout("vp", [P, D]); ks = dout("ks", [NS, D]); vs = dout("vs", [NS, D])
    if fused:
        ys = dout("ys", [NS, D])
        ckvA = din("ckvA", [npool * 128, 2048]); ptab = din("ptab", [NBS * NPG], I32)
        qs = pr.dint("qs", [NS, D]); ksi = pr.dint("ksi", [NS, D]); vsi = pr.dint("vsi", [NS, D]); osd = pr.dint("osd", [NS, D])
    else:
        hs1 = dout("hs1", [NS, D]); q2 = dout("q2", [NS, D])
    KTs = pr.dint("KTs", [NH, 128, P], BF16); Vsc = pr.dint("Vsc", [P, D], BF16); tbl = pr.dint("tbl", [8, 512])

    h, b_h, xn, b_xn = pr.h, pr.b_h, pr.xn, pr.b_xn
    X, R = pr.X, pr.R
    identb = pr.sb("identb", [128, 128], BF16)
    cx.op("dve", lambda e: e.tensor_copy(out=identb, in_=pr.ident), reads=[pr.b_const], writes=[pr.b_const])
    anti = pr.sb("anti_s", [128, 128])
    cx.dma("sp", anti, anti_d, writes=[pr.b_const])

    pfm = {}
    for nm, ap, n in (("cn0", conv_norm[0], 8), ("cn1", conv_norm[1], 8), ("b10", b_pw1[0], 16), ("b11", b_pw1[1], 16),
                      ("bd0", b_dw[0], 8), ("bd1", b_dw[1], 8), ("cm0", conv_mid[0], 8), ("cm1", conv_mid[1], 8),
                      ("b20", b_pw2[0], 8), ("b21", b_pw2[1], 8), ("kvn", kv_norm, 8), ("an0", attn_norm[0], 8),
                      ("an1", attn_norm[1], 8), ("fn0", ffn_norm[0], 8), ("fn1", ffn_norm[1], 8),
                      ("fn2", ffn_norm[2], 8), ("fn3", ffn_norm[3], 8)):
        pfm[nm] = pr.param_fm(nm, ap, n)

    xin = [(pr.carve(X, 0, [D], F32), Buf()), (pr.carve(X, 4096, [D], F32), Buf())]
    ti = 0
    for a in range(0, P, 128):
        xi, xb = xin[ti % 2]; ti += 1
        pr.load_tm(xp[a:a + 128, :], 128, h, b_h, a, xi, xb)
    xi, xb = xin[ti % 2]; ti += 1
    pr.load_tm(xs, NS, h, b_h, P, xi, xb)

    all_tiles = tiles(0, P) + [(P, NS)]

    wdw_t = pr.sb("wdw", [128, KC, CW])
    for l in range(2):
        cx.barrier()
        cn, cnb = pfm["cn%d" % l]; b1, b1b = pfm["b1%d" % l]; bd, bdb = pfm["bd%d" % l]
        cm, cmb = pfm["cm%d" % l]; b2, b2b = pfm["b2%d" % l]
        dg = [(pr.carve(X, 0, [CW, 128], BF16), Buf()), (pr.carve(X, 7936, [CW, 128], BF16), Buf())]
        glu = [(pr.carve(X, 15872, [512], F32), Buf())]
        sig = [(pr.carve(X, 17920, [512], F32), Buf())]
        stA = pr.carve(X, 19968, [KC, 30 + NS], F32); b_stA = Buf()
        sqc = pr.carve(X, 19968 + KC * (30 + NS) * 4, [KC, 256], BF16); b_sqc = Buf()
        tmpc = [(pr.carve(X, 29120, [256], F32), Buf())]
        u = pr.carve(R, 0, [KC, UL], BF16); b_u = pr.b_R
        wdr = pr.carve(X, 0, [D], F32)
        b_wdr = Buf()
        wdw = wdw_t; b_wdw = Buf()
        cx.dma("sp", wdr[:CW, :], w_dw[l], writes=[b_wdr])
        for k0 in range(0, KC, 4):
            pt, pb = pr.pget()
            for j in range(4):
                cx.op("pe", lambda e, j=j: e.transpose(pt[:, j * 128:j * 128 + CW], wdr[:CW, (k0 + j) * 128:(k0 + j + 1) * 128],
                                                       pr.ident[:CW, :CW]), reads=[b_wdr, pr.b_const], writes=[pb], signal=(j == 3))
            cx.op("act", lambda e: e.activation(out=wdw[:, k0:k0 + 4, :], in_=pt[:, :].rearrange("p (a b) -> p a b", a=4)[:, :, :CW],
                                               func=AF.Copy), reads=[pb], writes=[b_wdw])
        cx.barrier()
        pr.rmsnorm(cn, cnb, 0, T)
        cx.op("pool", lambda e: e.memset(u[:, :, 0:30], 0.0), writes=[b_u])
        for g4 in range(NBS // 4):
            xi, xb = xin[ti % 2]; ti += 1
            xi = xin[1][0]; xb = xin[1][1]
            cx.dma("sp", xi[:120, :], sc[l, g4 * 4:(g4 + 1) * 4].rearrange("b s f -> (b s) f"), writes=[xb])
            for k0 in range(0, KC, 4):
                pt, pb = pr.pget()
                for j in range(4):
                    cx.op("pe", lambda e, j=j: e.transpose(pt[:, j * 128:j * 128 + 120], xi[:120, (k0 + j) * 128:(k0 + j + 1) * 128],
                                                           pr.ident[:120, :120]), reads=[xb, pr.b_const], writes=[pb], signal=(j == 3))
                for j in range(4):
                    dstv = u[:, k0 + j, UP + g4 * 4 * SL:UP + (g4 + 1) * 4 * SL].rearrange("p (b s) -> p b s", s=SL)[:, :, 0:30]
                    cx.op("act", lambda e, j=j, dstv=dstv: e.activation(
                        out=dstv, in_=pt[:, j * 128:j * 128 + 120].rearrange("p (b s) -> p b s", s=30), func=AF.Copy),
                        reads=[pb], writes=[b_u])
        W1 = w_pw1[l]
        nblk = D // 256
        fetched = []

        def fetch1(i):
            fetched.append((pr.wfetch(W1, i * 256, 256, KC), pr.wfetch(W1, D + i * 256, 256, KC)))
        fetch1(0)
        for i in range(nblk):
            if i + 1 < nblk:
                fetch1(i + 1)
            (wa, abf), (wg, gbf) = fetched[i]
            for m in range(2):
                mc = i * 2 + m
                for (a, n) in all_tiles:
                    pa, pab = pr.pget()
                    pg, pgb = pr.pget()
                    for k in range(KC):
                        cx.op("pe", lambda e, k=k: e.matmul(pa[:, :n], lhsT=wa[:, k, m * 128:(m + 1) * 128], rhs=xn[:, k, a:a + n],
                                                           start=(k == 0), stop=(k == KC - 1)), reads=[abf, b_xn], writes=[pab], signal=(k == KC - 1))
                    for k in range(KC):
                        cx.op("pe", lambda e, k=k: e.matmul(pg[:, :n], lhsT=wg[:, k, m * 128:(m + 1) * 128], rhs=xn[:, k, a:a + n],
                                                           start=(k == 0), stop=(k == KC - 1)), reads=[gbf, b_xn], writes=[pgb], signal=(k == KC - 1))
                    sg, sgb = sig[0]
                    gl, glb = glu[0]
                    cx.op("act", lambda e: e.activation(out=sg[:, :n], in_=pg[:, :n], func=AF.Sigmoid, bias=b1[:, 8 + mc:9 + mc]),
                          reads=[pgb, b1b], writes=[sgb])
                    cx.op("dve", lambda e: e.scalar_tensor_tensor(out=gl[:, :n], in0=pa[:, :n], scalar=b1[:, mc:mc + 1], in1=sg[:, :n],
                                                                 op0=ALU.add, op1=ALU.mult), reads=[pab, sgb, b1b], writes=[glb])
                    if a < P:
                        cx.op("pool", lambda e: e.tensor_copy(out=u[:, mc, 30 + a:30 + a + n], in_=gl[:, :n]), reads=[glb], writes=[b_u])
                        if a + n == P:
                            cx.op("pool", lambda e: e.tensor_copy(out=stA[:, mc, 0:30], in_=gl[:, n - 30:n]), reads=[glb], writes=[b_stA])
                    else:
                        dstv = u[:, mc, UP:UP + NBS * SL].rearrange("p (b s) -> p b s", s=SL)[:, :, 30:SL]
                        cx.op("pool", lambda e, dstv=dstv: e.tensor_copy(out=dstv, in_=gl[:, :NS].rearrange("p (b t) -> p b t", t=DS)),
                              reads=[glb], writes=[b_u])
                        cx.op("pool", lambda e: e.tensor_copy(out=stA[:, mc, 30:30 + NS], in_=gl[:, :NS]), reads=[glb], writes=[b_stA])
        stg, stgb = xin[1]
        for k0 in range(0, KC, 4):
            pt, pb = pr.pget()
            for j in range(4):
                cx.op("pe", lambda e, j=j: e.transpose(pt[:30, j * 128:(j + 1) * 128], stA[:, k0 + j, 0:30], pr.ident),
                      reads=[b_stA, pr.b_const], writes=[pb], signal=(j == 3))
            cx.op("dve", lambda e: e.tensor_copy(out=stg[:30, k0 * 128:(k0 + 4) * 128], in_=pt[:30, :]), reads=[pb], writes=[stgb])
        cx.dma("sp", csp[l], stg[:30, :], reads=[stgb])
        for k0 in range(0, KC, 4):
            pt, pb = pr.pget()
            for j in range(4):
                cx.op("pe", lambda e, j=j: e.transpose(pt[:, j * 128:(j + 1) * 128], stA[:, k0 + j, 30:30 + NS], pr.ident),
                      reads=[b_stA, pr.b_const], writes=[pb], signal=(j == 3))
            cx.op("dve", lambda e: e.tensor_copy(out=stg[:, k0 * 128:(k0 + 4) * 128], in_=pt[:, :]), reads=[pb], writes=[stgb])
        for b in range(NBS):
            cx.dma("sp", css[l, b, 22:30, :], stg[b * DS:(b + 1) * DS, :], reads=[stgb])
        cx.dma("sp", css[l, :, 0:22, :], sc[l, :, 8:30, :])
        cx.barrier()
        ctiles = [(256 * i, 256, 256, 256 * i, None) for i in range(P // 256)]
        for b0 in range(0, NBS, 6):
            nb = min(6, NBS - b0)
            ctiles.append((UP + SL * b0, SL * nb - 30, DS * nb, P + DS * b0, nb))
        di = 0
        for (o0, nmm, ntok, tok0, nb) in ctiles:
            cb4 = [pr.pget() for _ in range(4)]

            def view(kc):
                bt, _ = cb4[kc // 2]
                off = (kc % 2) * 256
                if nb is None:
                    return bt[:, off:off + 256]
                return bt[:, off:off + SL * nb].rearrange("p (b s) -> p b s", s=SL)[:, :, 0:DS]

            def shp(ap2):
                if nb is None:
                    return ap2
                return ap2.rearrange("p (b t) -> p b t", t=DS)
            for kc in range(KC):
                bt, btb = cb4[kc // 2]
                off = (kc % 2) * 256
                dgt, dgb = dg[di % 2]; di += 1
                cx.op("pool", lambda e, kc=kc, dgt=dgt: e.tensor_tensor(
                    out=dgt, in0=identb.unsqueeze(1).to_broadcast([128, CW, 128]),
                    in1=wdw[:, kc, :].unsqueeze(2).to_broadcast([128, CW, 128]), op=ALU.mult),
                    reads=[pr.b_const, b_wdw], writes=[dgb])
                for j in range(CW):
                    cx.op("pe", lambda e, kc=kc, j=j, dgt=dgt: e.matmul(bt[:, off:off + nmm], lhsT=dgt[:, j, :],
                                                                     rhs=u[:, kc, o0 + j:o0 + j + nmm], start=(j == 0), stop=(j == CW - 1)),
                          reads=[dgb, b_u], writes=[btb], signal=(j == CW - 1))
                cx.op("act", lambda e, kc=kc: e.activation(out=shp(sqc[:, kc, :ntok]), in_=view(kc), func=AF.Square, bias=bd[:, kc:kc + 1]),
                      reads=[btb, bdb], writes=[b_sqc])
            ps_, psb = pr.pget()
            for kc in range(KC):
                cx.op("pe", lambda e, kc=kc: e.matmul(ps_[:, :ntok], lhsT=pr.ones, rhs=sqc[:, kc, :ntok], start=(kc == 0), stop=(kc == KC - 1)),
                      reads=[b_sqc, pr.b_const], writes=[psb], signal=(kc == KC - 1))
            pr.rstd_from(ps_, psb, ntok, 1.0 / D)
            for kc in range(KC):
                bt, btb = cb4[kc // 2]
                tm, tmb = tmpc[0]
                cx.op("dve", lambda e, kc=kc: e.scalar_tensor_tensor(out=shp(tm[:, :ntok]), in0=view(kc), scalar=bd[:, kc:kc + 1],
                                                                    in1=shp(pr.ms[:, :ntok]), op0=ALU.add, op1=ALU.mult),
                      reads=[btb, bdb, pr.b_ms], writes=[tmb])
                cx.op("act", lambda e, kc=kc: e.activation(out=xn[:, kc, tok0:tok0 + ntok], in_=tm[:, :ntok], func=AF.Silu, scale=cm[:, kc:kc + 1]),
                      reads=[tmb, cmb], writes=[b_xn])

        def epi_pw2(m, a, n, pt, pb):
            cx.op("dve", lambda e: e.scalar_tensor_tensor(out=h[:, m, a:a + n], in0=pt[:, :n], scalar=b2[:, m:m + 1], in1=h[:, m, a:a + n],
                                                         op0=ALU.add, op1=ALU.add), reads=[pb, b2b, b_h], writes=[b_h])
        pr.linear(xn, b_xn, KC, w_pw2[l], D, all_tiles, epi_pw2)
        cx.barrier()
        sgt = [(pr.carve(X, 0, [512], F32), Buf()), (pr.carve(X, 2048, [512], F32), Buf())]
        fn, fnb = pfm["fn%d" % l]
        pr.ffn(fn, fnb, w_gate[l], w_up[l], w_down[l], 0, T, sgt)

    cx.barrier()
    stg, stgb = xin[1]
    if not fused:
        pr.store_tm(h, b_h, P, NS, hs1, stg, stgb)

    wres = pr.carve(X, 0, [2, KC, 512], BF16); b_wres = Buf()
    kout = pr.carve(X, 16384, [D], F32); b_kout = Buf()
    sqk = pr.carve(X, 20480, [512], F32); b_sqk = Buf()
    tmpk = pr.carve(X, 22528, [512], F32); b_tmpk = Buf()
    vb = pr.carve(X, 24576, [D], BF16); b_vb = Buf()
    ktile = pr.carve(X, 26624, [NH, 128], BF16); b_ktile = Buf()
    Bp = pr.sb("Bp", [128, NH, 240], BF16); b_Bp = Buf()
    pTs = [(pr.carve(X, 1024 * i, [512], BF16), Buf()) for i in range(4)]
    o1n = pr.carve(X, 4096, [128], F32); b_o1n = Buf()
    odf = pr.carve(X, 4608, [128], F32); b_odf = Buf()
    onr = pr.carve(X, 5120, [128], F32); b_onr = Buf()
    junk = pr.carve(X, 5632, [128], F32); b_junk = Buf()
    QT = pr.carve(R, 0, [NH, P], BF16); b_QT = pr.b_R
    qoff = NH * P * 2
    KTh = pr.carve(R, qoff, [P], BF16); b_KTh = Buf()
    va = pr.carve(R, qoff + P * 2, [P // 128, 130], BF16); b_va = Buf()
    ssk = pr.sb("ssk", [128, 8]); b_ssk = Buf()
    small = pr.sb("small", [128, 8]); b_small = Buf()

    gk, gkb = pr.param_bc("gk", k_norm, 64)
    gq = [pr.param_bc("gq%d" % j, q_norm[j], 64) for j in range(2)]
    gsub = [pr.param_bc("gsub%d" % j, sub_norm[j], 128) for j in range(2)]
    bfar, bfarb = pr.param_bc("bfar", rel_bias[31], 8)
    lam = pr.sb("lam", [128, 4]); b_lam = Buf()
    lt = pr.carve(X, 4096, [4, 64], F32); b_lt = Buf()
    for j in range(2):
        with nc.allow_non_contiguous_dma(reason="bcast"):
            for i, src in enumerate((lq1[j], lk1[j], lq2[j], lk2[j])):
                cx.dma("sp", lt[:, i, :], src.partition_broadcast(128), writes=[b_lt])
        cx.op("dve", lambda e: e.tensor_tensor(out=lt[:, 0, :], in0=lt[:, 0, :], in1=lt[:, 1, :], op=ALU.mult), reads=[b_lt], writes=[b_lt])
        cx.op("dve", lambda e: e.tensor_tensor(out=lt[:, 2, :], in0=lt[:, 2, :], in1=lt[:, 3, :], op=ALU.mult), reads=[b_lt], writes=[b_lt])
        cx.op("dve", lambda e: e.tensor_reduce(out=small[:, 0:1], in_=lt[:, 0, :], axis=AX.X, op=ALU.add), reads=[b_lt], writes=[b_small])
        cx.op("dve", lambda e: e.tensor_reduce(out=small[:, 1:2], in_=lt[:, 2, :], axis=AX.X, op=ALU.add), reads=[b_lt], writes=[b_small])
        cx.op("act", lambda e: e.activation(out=small[:, 0:2], in_=small[:, 0:2], func=AF.Exp), reads=[b_small], writes=[b_small])
        cx.op("dve", lambda e, j=j: e.scalar_tensor_tensor(out=lam[:, j:j + 1], in0=small[:, 1:2], scalar=-lambda_init(2 + j),
                                                          in1=small[:, 0:1], op0=ALU.add, op1=ALU.subtract),
              reads=[b_small], writes=[b_lam])
        g_, gb_ = gsub[j]
        cx.op("dve", lambda e, j=j, g_=g_: e.tensor_scalar(out=g_, in0=g_, scalar1=1.0 - lambda_init(2 + j), scalar2=None, op0=ALU.mult),
              reads=[gb_], writes=[gb_])

    cx.barrier()
    tblS = pr.carve(X, 8192, [512], F32)[0:8, :]; b_tbl = Buf()
    nfar = pr.sb("nfar", [8, 1])
    cx.op("dve", lambda e: e.memset(tblS, 0.0), writes=[b_tbl])
    rbT = pr.sb("rbT", [8, 32])
    with nc.allow_non_contiguous_dma(reason="bias table"):
        cx.dma("sp", rbT, rel_bias.rearrange("b h -> h b"), writes=[b_tbl])
    for (bk, d0, d1) in bucket_runs(384):
        cx.op("dve", lambda e, bk=bk, d0=d0, d1=d1: e.tensor_copy(out=tblS[:, 127 + d0:128 + d1],
                                                                 in_=rbT[:, bk:bk + 1].to_broadcast([8, d1 - d0 + 1])),
              reads=[b_tbl], writes=[b_tbl])
    cx.op("dve", lambda e: e.tensor_scalar(out=nfar, in0=rbT[:, 31:32], scalar1=-1.0, scalar2=None, op0=ALU.mult), reads=[b_tbl], writes=[b_tbl])
    cx.op("act", lambda e: e.activation(out=tblS[:, 127:512], in_=tblS[:, 127:512], func=AF.Exp, bias=nfar[:, 0:1]), reads=[b_tbl], writes=[b_tbl])
    b_tbld = Buf()
    cx.dma("sp", tbl, tblS, reads=[b_tbl], writes=[b_tbld])
    for hd in range(NH):
        hk = pr.carve(X, 0, [240], F32)
        cx.dma("sp", hk, bass.AP(tensor=tbl.tensor, offset=hd * 512, ap=[[1, 128], [1, 240]]), reads=[b_tbld], writes=[b_wres])
        pt, pb = pr.pget()
        cx.op("pe", lambda e: e.matmul(pt[:, :240], lhsT=anti, rhs=hk, start=True, stop=True), reads=[b_wres, pr.b_const], writes=[pb])
        cx.op("act", lambda e, hd=hd: e.activation(out=Bp[:, hd, :], in_=pt[:, :240], func=AF.Copy), reads=[pb], writes=[b_Bp])
    cx.op("dve", lambda e: e.memset(va[:, :, 128:130], 1.0), writes=[b_va])
    if fused:
        Bs_all = pr.sb("Bs_all", [128, NH, 8]); Bn_all = pr.sb("Bn_all", [8, NH, 8]); b_Bsn = Buf()
        for hd in range(NH):
            hs_ = pr.carve(X, 0, [8], F32); hn_ = pr.carve(X, 1024, [8], F32)[0:8, :]
            bh_ = Buf()
            with nc.allow_non_contiguous_dma(reason="hankel"):
                cx.dma("sp", hs_, bass.AP(tensor=tbl.tensor, offset=hd * 512 + 128, ap=[[1, 128], [1, 8]]), reads=[b_tbld], writes=[bh_, b_wres])
                cx.dma("sp", hn_, bass.AP(tensor=tbl.tensor, offset=hd * 512 + 120, ap=[[1, 8], [1, 8]]), reads=[b_tbld], writes=[bh_, b_wres])
            pt, pb = pr.pget()
            cx.op("pe", lambda e: e.matmul(pt[:, 0:8], lhsT=anti, rhs=hs_, start=True, stop=True), reads=[bh_, pr.b_const], writes=[pb])
            cx.op("act", lambda e, hd=hd: e.activation(out=Bs_all[:, hd, :], in_=pt[:, 0:8], func=AF.Copy), reads=[pb], writes=[b_Bsn])
            pt, pb = pr.pget()
            cx.op("pe", lambda e: e.matmul(pt[0:8, 0:8], lhsT=anti[0:8, 120:128], rhs=hn_, start=True, stop=True), reads=[bh_, pr.b_const], writes=[pb])
            cx.op("act", lambda e, hd=hd: e.activation(out=Bn_all[:, hd, :], in_=pt[0:8, 0:8], func=AF.Copy), reads=[pb], writes=[b_Bsn])
    cx.barrier()

    def proj_tm(Wcols, g64, g64b, tok_list, sink):
        for hf in range(2):
            cx.dma("pool", wres[:, hf, :, :], Wcols[:, hf * 512:(hf + 1) * 512].rearrange("(k p) c -> p k c", p=128), writes=[b_wres])
        for a in tok_list:
            for hf in range(2):
                pt, pb = pr.pget()
                for k in range(KC):
                    cx.op("pe", lambda e, k=k: e.matmul(pt[:, :], lhsT=xn[:, k, a:a + 128], rhs=wres[:, hf, k, :],
                                                       start=(k == 0), stop=(k == KC - 1)), reads=[b_xn, b_wres], writes=[pb], signal=(k == KC - 1))
                ko = kout[:, hf * 512:(hf + 1) * 512]
                if g64 is None:
                    cx.op("act", lambda e: e.activation(out=ko, in_=pt[:, :], func=AF.Copy), reads=[pb], writes=[b_kout])
                else:
                    cx.op("act", lambda e: e.activation(out=sqk, in_=pt[:, :], func=AF.Square), reads=[pb], writes=[b_sqk])
                    cx.op("dve", lambda e: e.tensor_reduce(out=ssk, in_=sqk.rearrange("p (g d) -> p g d", d=64), axis=AX.X, op=ALU.add),
                          reads=[b_sqk], writes=[b_ssk])
                    cx.op("dve", lambda e: e.tensor_scalar(out=ssk, in0=ssk, scalar1=1.0 / 64, scalar2=EPS, op0=ALU.mult, op1=ALU.add),
                          reads=[b_ssk], writes=[b_ssk])
                    cx.op("pool", lambda e: e.tensor_tensor(out=ssk, in0=ssk, in1=pr.nhalf[:, 0:8], op=ALU.pow),
                          reads=[b_ssk, pr.b_const], writes=[b_ssk])
                    cx.op("dve", lambda e: e.tensor_tensor(out=tmpk.rearrange("p (g d) -> p g d", d=64),
                                                          in0=pt[:, :].rearrange("p (g d) -> p g d", d=64),
                                                          in1=ssk.unsqueeze(2).to_broadcast([128, 8, 64]), op=ALU.mult),
                          reads=[pb, b_ssk], writes=[b_tmpk])
                    cx.op("dve", lambda e: e.tensor_tensor(out=ko.rearrange("p (g d) -> p g d", d=64),
                                                          in0=tmpk.rearrange("p (g d) -> p g d", d=64),
                                                          in1=g64.unsqueeze(1).to_broadcast([128, 8, 64]), op=ALU.mult),
                          reads=[b_tmpk, g64b], writes=[b_kout])
            sink(a)

    def head_transposes(a, dst_fn, dstb):
        for h0 in range(0, NH, 4):
            pt, pb = pr.pget()
            for j in range(4):
                cx.op("pe", lambda e, j=j: e.transpose(pt[:, j * 128:(j + 1) * 128], kout[:, (h0 + j) * 128:(h0 + j + 1) * 128], pr.ident),
                      reads=[b_kout, pr.b_const], writes=[pb], signal=(j == 3))
            cx.op("act", lambda e: e.activation(out=dst_fn(h0), in_=pt[:, :].rearrange("p (a b) -> p a b", a=4), func=AF.Copy),
                  reads=[pb], writes=[dstb])


    def sample_attention(j):
        idx = pr.carve(R, 0, [NBS * NPG], I32); b_idx = Buf()
        Qblk = pr.carve(R, 1024, [NBS * NH, 16], BF16); b_Q = Buf()
        KnT = pr.carve(R, 5120, [NH, NS], BF16); b_KnT = Buf()
        kvs = [(pr.carve(R, 7168 + 4096 * i, [NH, 256], BF16), Buf()) for i in range(8)]
        xq = pr.carve(X, 0, [D], F32); b_xq = Buf()
        ktq = [(pr.carve(X, 4096 + 1024 * i, [4, 128], BF16), Buf()) for i in range(4)]
        pTt = [(pr.carve(X, 8192 + 128 * i, [64], BF16), Buf()) for i in range(4)]
        pNt = [(pr.carve(X, 8704 + 64 * i, [16], BF16)[0:8, :], Buf()) for i in range(4)]
        Vnt = [(pr.carve(X, 8960 + 2048 * i, [D], BF16)[0:8, :], Buf()) for i in range(2)]
        accS = pr.carve(X, 13056, [NH, 258], F32)[0:8]; b_acc = Buf()
        t1 = pr.carve(X, 21312, [NH, 128], F32)[0:8]; b_t1 = Buf()
        t2 = pr.carve(X, 25408, [NH, 128], F32)[0:8]; b_t2 = Buf()
        s8 = pr.carve(X, 29504, [4, NH], F32)[0:8]; b_s8 = Buf()
        otm = pr.carve(X, 21312, [KC, NS], F32); b_otm = Buf()
        pts = pr.carve(X, 29632, [NBS * NPG], I32); b_pts = Buf()
        iotp = pr.carve(X, 30656, [1], I32); iotpf = pr.carve(X, 30688, [1], F32); b_io = Buf()
        with nc.allow_non_contiguous_dma(reason="bcast"):
            cx.dma("sp", pts, ptab.partition_broadcast(128), writes=[b_pts])
        cx.op("pool", lambda e: e.iota(iotp, pattern=[[0, 1]], base=0, channel_multiplier=1), writes=[b_io])
        cx.op("dve", lambda e: e.tensor_copy(out=iotpf, in_=iotp), reads=[b_io], writes=[b_io])
        cx.op("dve", lambda e: e.tensor_scalar(out=idx, in0=pts, scalar1=128.0, scalar2=iotpf[:, 0:1], op0=ALU.mult, op1=ALU.add),
              reads=[b_pts, b_io], writes=[b_idx])
        cx.op("dve", lambda e: e.memset(Qblk, 0.0), writes=[b_Q])
        cx.dma("sp", xq, qs, writes=[b_xq])
        for hd in range(NH):
            pt, pb = pr.pget()
            cx.op("pe", lambda e, hd=hd: e.transpose(pt[:, 0:128], xq[:, hd * 128:(hd + 1) * 128], pr.ident), reads=[b_xq, pr.b_const], writes=[pb])
            for c in range(2):
                dst = Qblk.rearrange("p (b h) s -> p b h s", h=NH)[c * 64:(c + 1) * 64, :, hd, c * 8:(c + 1) * 8]
                cx.op("act", lambda e, c=c, dst=dst: e.activation(out=dst, in_=pt[c * 64:(c + 1) * 64, 0:128].rearrange("p (b t) -> p b t", t=DS), func=AF.Copy),
                      reads=[pb], writes=[b_Q])
        cx.dma("sp", xq, ksi, writes=[b_xq])
        for hd in range(NH):
            pt, pb = pr.pget()
            cx.op("pe", lambda e, hd=hd: e.transpose(pt[:, 0:128], xq[:, hd * 128:(hd + 1) * 128], pr.ident), reads=[b_xq, pr.b_const], writes=[pb])
            cx.op("act", lambda e, hd=hd: e.activation(out=KnT[:, hd, :], in_=pt[:, 0:128], func=AF.Copy), reads=[pb], writes=[b_KnT])
        gs_, gsb_ = gsub[j]
        cnt = 0
        for b in range(NBS):
            vn_, vnb = Vnt[b % 2]
            cx.dma("pool", vn_, vsi[b * DS:(b + 1) * DS, :], writes=[vnb])
            for g in range(4):
                sl = []
                for i in range(4):
                    t_, bf_ = kvs[(b * NPG + g * 4 + i) % 8]
                    col = b * NPG + g * 4 + i
                    cx.dma("pool", t_.rearrange("p h c -> p (h c)"), ckvA, reads=[b_idx], writes=[bf_], indirect=idx[:, col:col + 1])
                    sl.append((t_, bf_))
                for hd in range(NH):
                    pair = b * NH + hd
                    kt_, ktb = ktq[cnt % 4]
                    pT, pTb = pTt[cnt % 4]
                    pN, pNb = pNt[cnt % 4]
                    cnt += 1
                    pt, pb = pr.pget()
                    ptb = pt.bitcast(BF16)
                    for i in range(4):
                        t_, bf_ = sl[i]
                        cx.op("pe", lambda e, i=i, t_=t_: e.transpose(ptb[:, i * 128:(i + 1) * 128], t_[:, hd, 0:128], identb),
                              reads=[bf_, pr.b_const], writes=[pb], signal=(i == 3))
                    if cnt % 2 == 0:
                        cx.op("act", lambda e: e.activation(out=kt_, in_=ptb[:, 0:512].rearrange("p (a b) -> p a b", a=4), func=AF.Copy), reads=[pb], writes=[ktb])
                    else:
                        cx.op("dve", lambda e: e.tensor_copy(out=kt_, in_=ptb[:, 0:512].rearrange("p (a b) -> p a b", a=4)), reads=[pb], writes=[ktb])
                    sp_, spb = pr.pget()
                    for i in range(4):
                        cx.op("pe", lambda e, i=i: e.matmul(sp_[:, i * 16:(i + 1) * 16], lhsT=kt_[:, i, :], rhs=Qblk[:, pair, :], start=True, stop=True),
                              reads=[ktb, b_Q], writes=[spb], signal=(i == 3 and g < 3))
                    if g == 3:
                        cx.op("pe", lambda e: e.matmul(sp_[0:8, 64:80], lhsT=KnT[:, hd, b * DS:(b + 1) * DS], rhs=Qblk[:, pair, :], start=True, stop=True),
                              reads=[b_KnT, b_Q], writes=[spb])
                    cx.op("act", lambda e: e.activation(out=pT, in_=sp_[:, 0:64], func=AF.Exp, bias=bfar[:, hd:hd + 1], scale=0.125), reads=[spb, bfarb], writes=[pTb])
                    if g == 3:
                        cx.op("act", lambda e: e.activation(out=pN, in_=sp_[0:8, 64:80], func=AF.Exp, bias=bfar[0:8, hd:hd + 1], scale=0.125), reads=[spb, bfarb], writes=[pNb])
                        cx.op("dve", lambda e: e.tensor_tensor(out=pT[:, 48:64].rearrange("p (c t) -> p c t", c=2), in0=pT[:, 48:64].rearrange("p (c t) -> p c t", c=2),
                                                              in1=Bs_all[:, hd, :].unsqueeze(1).to_broadcast([128, 2, 8]), op=ALU.mult), reads=[pTb, b_Bsn], writes=[pTb])
                        cx.op("dve", lambda e: e.tensor_tensor(out=pN.rearrange("p (c t) -> p c t", c=2), in0=pN.rearrange("p (c t) -> p c t", c=2),
                                                              in1=Bn_all[:, hd, :].unsqueeze(1).to_broadcast([8, 2, 8]), op=ALU.mult), reads=[pNb, b_Bsn], writes=[pNb])
                    ac, acb = pr.pget()
                    for c in range(2):
                        for i in range(4):
                            t_, bf_ = sl[i]
                            cx.op("pe", lambda e, c=c, i=i, t_=t_: e.matmul(ac[0:8, c * 129:c * 129 + 128], lhsT=pT[:, i * 16 + c * 8:i * 16 + c * 8 + 8],
                                                                           rhs=t_[:, hd, 128:256], start=(i == 0), stop=(i == 3 and g < 3)),
                                  reads=[pTb, bf_], writes=[acb], signal=False)
                        if g == 3:
                            cx.op("pe", lambda e, c=c: e.matmul(ac[0:8, c * 129:c * 129 + 128], lhsT=pN[:, c * 8:(c + 1) * 8], rhs=vn_[:, hd * 128:(hd + 1) * 128],
                                                               start=False, stop=True), reads=[pNb, vnb], writes=[acb], signal=False)
                        for i in range(4):
                            cx.op("pe", lambda e, c=c, i=i: e.matmul(ac[0:8, c * 129 + 128:c * 129 + 129], lhsT=pT[:, i * 16 + c * 8:i * 16 + c * 8 + 8],
                                                                    rhs=pr.ones[:, 0:1], start=(i == 0), stop=(i == 3 and g < 3)),
                                  reads=[pTb, pr.b_const], writes=[acb], signal=(i == 3 and g < 3 and c == 1))
                        if g == 3:
                            cx.op("pe", lambda e, c=c: e.matmul(ac[0:8, c * 129 + 128:c * 129 + 129], lhsT=pN[:, c * 8:(c + 1) * 8], rhs=pr.ones[0:8, 0:1],
                                                               start=False, stop=True), reads=[pNb, pr.b_const], writes=[acb], signal=(c == 1))
                    if g == 0:
                        cx.op("dve", lambda e: e.tensor_copy(out=accS[:, hd, :], in_=ac[0:8, 0:258]), reads=[acb], writes=[b_acc])
                    else:
                        cx.op("dve", lambda e: e.tensor_tensor(out=accS[:, hd, :], in0=ac[0:8, 0:258], in1=accS[:, hd, :], op=ALU.add),
                              reads=[acb, b_acc], writes=[b_acc])
            av = accS.rearrange("p h (c e) -> p h c e", c=2)
            cx.op("dve", lambda e: e.reciprocal(out=s8[:, 0:2, :].rearrange("p c h -> p h c"), in_=av[:, :, :, 128]), reads=[b_acc], writes=[b_s8])
            cx.op("dve", lambda e: e.tensor_scalar(out=s8[:, 2, :], in0=s8[:, 1, :], scalar1=lam[0:8, j:j + 1], scalar2=None, op0=ALU.mult), reads=[b_s8, b_lam], writes=[b_s8])
            cx.op("dve", lambda e: e.tensor_tensor(out=t1, in0=av[:, :, 0, 0:128], in1=s8[:, 0, :].unsqueeze(2).to_broadcast([8, NH, 128]), op=ALU.mult),
                  reads=[b_acc, b_s8], writes=[b_t1])
            cx.op("dve", lambda e: e.tensor_tensor(out=t2, in0=av[:, :, 1, 0:128], in1=s8[:, 2, :].unsqueeze(2).to_broadcast([8, NH, 128]), op=ALU.mult),
                  reads=[b_acc, b_s8], writes=[b_t2])
            cx.op("dve", lambda e: e.tensor_tensor(out=t2, in0=t2, in1=t1, op=ALU.add), reads=[b_t1, b_t2], writes=[b_t2])
            cx.op("dve", lambda e: e.tensor_tensor(out=t1, in0=t2, in1=t2, op=ALU.mult), reads=[b_t2], writes=[b_t1])
            cx.op("dve", lambda e: e.tensor_reduce(out=s8[:, 3, :], in_=t1, axis=AX.X, op=ALU.add), reads=[b_t1], writes=[b_s8])
            cx.op("dve", lambda e: e.tensor_scalar(out=s8[:, 3, :], in0=s8[:, 3, :], scalar1=1.0 / 128, scalar2=EPS, op0=ALU.mult, op1=ALU.add), reads=[b_s8], writes=[b_s8])
            cx.op("pool", lambda e: e.tensor_tensor(out=s8[:, 3, :], in0=s8[:, 3, :], in1=pr.nhalf[0:8, 0:NH], op=ALU.pow), reads=[b_s8, pr.b_const], writes=[b_s8])
            cx.op("dve", lambda e: e.tensor_tensor(out=t1, in0=t2, in1=s8[:, 3, :].unsqueeze(2).to_broadcast([8, NH, 128]), op=ALU.mult), reads=[b_t2, b_s8], writes=[b_t1])
            cx.op("dve", lambda e: e.tensor_tensor(out=t2, in0=t1, in1=gs_[0:8, :].unsqueeze(1).to_broadcast([8, NH, 128]), op=ALU.mult), reads=[b_t1, gsb_], writes=[b_t2])
            cx.dma("sp", osd[b * DS:(b + 1) * DS, :], t2, reads=[b_t2])
        cx.barrier()
        pr.load_tm(osd, NS, otm, b_otm, 0, xq, b_xq)
        for kc in range(KC):
            cx.op("dve", lambda e, kc=kc: e.tensor_copy(out=xn[:, kc, P:T], in_=otm[:, kc, :]), reads=[b_otm], writes=[b_xn])

    ptok = list(range(0, P, 128))
    for j in range(2):
        l = 2 + j
        if j == 0:
            kvn, kvnb = pfm["kvn"]
            pr.rmsnorm(kvn, kvnb, 0, T)

            def sink_k(a):
                if a < P:
                    cx.dma("sp", kp[a:a + 128, :], kout, reads=[b_kout])
                    head_transposes(a, lambda h0: ktile[:, h0:h0 + 4, :], b_ktile)
                    cx.dma("sp", KTs[:, :, a:a + 128].rearrange("h p t -> p h t"), ktile, reads=[b_ktile])
                else:
                    cx.dma("sp", ks, kout, reads=[b_kout])
                    if fused:
                        cx.dma("sp", ksi, kout, reads=[b_kout])
            proj_tm(w_kv[:, 0:D], gk, gkb, ptok + [P], sink_k)
            cx.barrier()

            def sink_v(a):
                if a < P:
                    cx.dma("sp", vp[a:a + 128, :], kout, reads=[b_kout])
                    cx.op("act", lambda e: e.activation(out=vb, in_=kout, func=AF.Copy), reads=[b_kout], writes=[b_vb])
                    cx.dma("sp", Vsc[a:a + 128, :], vb, reads=[b_vb])
                else:
                    cx.dma("sp", vs, kout, reads=[b_kout])
                    if fused:
                        cx.dma("sp", vsi, kout, reads=[b_kout])
            proj_tm(w_kv[:, D:2 * D], None, None, ptok + [P], sink_v)
            cx.barrier()
        b_scr = Buf()
        an, anb = pfm["an%d" % j]
        pr.rmsnorm(an, anb, 0, T if (j == 0 or fused) else P)

        def sink_q(a):
            if a < P:
                head_transposes(a, lambda h0: QT[:, h0:h0 + 4, a:a + 128], b_QT)
            else:
                cx.dma("sp", qs if fused else q2, kout, reads=[b_kout])
        gqj, gqjb = gq[j]
        proj_tm(w_q[j], gqj, gqjb, ptok + ([P] if (j == 0 or fused) else []), sink_q)
        cx.barrier()
        QC = 256
        acc = [pr.banks[6], pr.banks[7]]
        gs_, gsb_ = gsub[j]
        pti = 0
        for hd in range(NH):
            cx.dma("sp", KTh, KTs[hd], writes=[b_KTh])
            cx.dma("sp", va[:, :, 0:128], Vsc.rearrange("(kt p) (h e) -> p kt h e", p=128, h=NH)[:, :, hd, :], writes=[b_va])
            for q0 in range(0, P, QC):
                nqb = QC // 128
                for c in range(2):
                    cx.op("dve", lambda e, c=c: e.memset(acc[c][0][:, 0:nqb * 129], 0.0), writes=[acc[c][1]])
                for kt in range((q0 + QC) // 128):
                    ks_ = kt * 128
                    q_lo = max(q0, ks_)
                    n = q0 + QC - q_lo
                    for c in range(2):
                        pt, pb = pr.pget()
                        cx.op("pe", lambda e, c=c: e.matmul(pt[:, :n], lhsT=KTh[c * 64:(c + 1) * 64, ks_:ks_ + 128],
                                                           rhs=QT[c * 64:(c + 1) * 64, hd, q_lo:q_lo + n], start=True, stop=True),
                              reads=[b_KTh, b_QT], writes=[pb])
                        pT, pTb = pTs[pti % 4]; pti += 1
                        cx.op("act", lambda e, pT=pT: e.activation(out=pT[:, :n], in_=pt[:, :n], func=AF.Exp, bias=bfar[:, hd:hd + 1], scale=0.125),
                              reads=[pb, bfarb], writes=[pTb])
                        w0 = max(q_lo, ks_); w1 = min(q_lo + n, ks_ + 240)
                        if w1 > w0:
                            cx.op("dve", lambda e, pT=pT: e.tensor_tensor(out=pT[:, w0 - q_lo:w1 - q_lo], in0=pT[:, w0 - q_lo:w1 - q_lo],
                                                                        in1=Bp[:, hd, w0 - ks_:w1 - ks_], op=ALU.mult),
                                  reads=[pTb, b_Bp], writes=[pTb])
                        for qb in range((q_lo - q0) // 128, nqb):
                            gqb = q0 // 128 + qb
                            col = q0 + qb * 128 - q_lo
                            cx.op("pe", lambda e, c=c, qb=qb, col=col, pT=pT: e.matmul(
                                acc[c][0][:, qb * 129:(qb + 1) * 129], lhsT=pT[:, col:col + 128], rhs=va[:, kt, 0:129],
                                start=False, stop=(kt == gqb), skip_group_check=True), reads=[pTb, b_va], writes=[acc[c][1]])
                for qb in range(nqb):
                    a = q0 + qb * 128
                    o0_, o1_ = qb * 129, qb * 129 + 128
                    for c in range(2):
                        cx.op("dve", lambda e, c=c: e.reciprocal(out=small[:, c:c + 1], in_=acc[c][0][:, o1_:o1_ + 1]),
                              reads=[acc[c][1]], writes=[b_small])
                    cx.op("dve", lambda e: e.tensor_tensor(out=small[:, 2:3], in0=small[:, 1:2], in1=lam[:, j:j + 1], op=ALU.mult),
                          reads=[b_small, b_lam], writes=[b_small])
                    cx.op("dve", lambda e: e.tensor_scalar(out=o1n, in0=acc[0][0][:, o0_:o1_], scalar1=small[:, 0:1], scalar2=None, op0=ALU.mult),
                          reads=[acc[0][1], b_small], writes=[b_o1n])
                    cx.op("dve", lambda e: e.scalar_tensor_tensor(out=odf, in0=acc[1][0][:, o0_:o1_], scalar=small[:, 2:3], in1=o1n,
                                                                 op0=ALU.mult, op1=ALU.add), reads=[acc[1][1], b_small, b_o1n], writes=[b_odf])
                    cx.op("act", lambda e: e.activation(out=junk, in_=odf, func=AF.Square, accum_out=small[:, 3:4]),
                          reads=[b_odf], writes=[b_junk, b_small])
                    cx.op("dve", lambda e: e.tensor_scalar(out=small[:, 3:4], in0=small[:, 3:4], scalar1=1.0 / 128, scalar2=EPS,
                                                          op0=ALU.mult, op1=ALU.add), reads=[b_small], writes=[b_small])
                    cx.op("pool", lambda e: e.tensor_tensor(out=small[:, 3:4], in0=small[:, 3:4], in1=pr.nhalf[:, 0:1], op=ALU.pow),
                          reads=[b_small, pr.b_const], writes=[b_small])
                    cx.op("dve", lambda e: e.scalar_tensor_tensor(out=onr, in0=odf, scalar=small[:, 3:4], in1=gs_, op0=ALU.mult, op1=ALU.mult),
                          reads=[b_odf, b_small, gsb_], writes=[b_onr])
                    pt, pb = pr.pget()
                    cx.op("pe", lambda e: e.transpose(pt[:, 0:128], onr, pr.ident), reads=[b_onr, pr.b_const], writes=[pb])
                    cx.op("act", lambda e: e.activation(out=xn[:, hd, a:a + 128], in_=pt[:, 0:128], func=AF.Copy), reads=[pb], writes=[b_xn])
        cx.barrier()
        if fused:
            sample_attention(j)
            cx.barrier()
        pr.linear(xn, b_xn, KC, w_o[j], D, (tiles(0, P) + [(P, NS)]) if fused else tiles(0, P), pr.add_to_h)
        sgt = [(pr.carve(X, 0, [512], F32), Buf()), (pr.carve(X, 2048, [512], F32), Buf())]
        fn, fnb = pfm["fn%d" % l]
        pr.ffn(fn, fnb, w_gate[l], w_up[l], w_down[l], 0, T if fused else P, sgt)
        cx.barrier()
        if j == 0:
            cx.op("dve", lambda e: e.memset(va[:, :, 128:130], 1.0), writes=[b_va])
            cx.barrier()

    if fused:
        stg, stgb = xin[1]
        pr.store_tm(h, b_h, P, NS, ys, stg, stgb)
    for i, a in enumerate(range(0, P, 128)):
        stg, stgb = xin[i % 2]
        pr.store_tm(h, b_h, a, 128, yp[a:a + 128, :], stg, stgb)
    cx.finish("sp")
    return nc


def build_L3(with_q):
    T = NS
    pr = Prog(T, KC * T * 2, 32768)
    nc, cx = pr.nc, pr.cx
    din, dout = pr.din, pr.dout
    hs = din("hs", [NS, D]); oat = din("oat", [NS, D])
    w_o = din("w_o", [D, D]); ffn_norm = din("ffn_norm", [D])
    wg = din("w_gate", [D, DFF]); wu = din("w_up", [D, DFF]); wd = din("w_down", [DFF, D])
    hs_out = dout("hs_out", [NS, D])
    if with_q:
        attn_norm = din("attn_norm", [D]); w_q = din("w_q", [D, D]); q_norm = din("q_norm", [64])
        q_out = dout("q_out", [NS, D])
    h, b_h, xn, b_xn, X = pr.h, pr.b_h, pr.xn, pr.b_xn, pr.X
    xin = (pr.carve(X, 0, [D], F32), Buf())
    otm = pr.sb("otm", [128, KC, T]); b_otm = Buf()
    pr.load_tm(hs, NS, h, b_h, 0, xin[0], xin[1])
    pr.load_tm(oat, NS, otm, b_otm, 0, xin[0], xin[1])
    for kc in range(KC):
        cx.op("dve", lambda e, kc=kc: e.tensor_copy(out=xn[:, kc, :], in_=otm[:, kc, :]), reads=[b_otm], writes=[b_xn])
    pr.linear(xn, b_xn, KC, w_o, D, [(0, NS)], pr.add_to_h)
    fn, fnb = pr.param_fm("fn", ffn_norm, 8)
    sgt = [(pr.carve(X, 8192, [512], F32), Buf()), (pr.carve(X, 10240, [512], F32), Buf())]
    pr.ffn(fn, fnb, wg, wu, wd, 0, T, sgt)
    stg = (pr.carve(X, 4096, [D], F32), Buf())
    pr.store_tm(h, b_h, 0, NS, hs_out, stg[0], stg[1])
    if with_q:
        an, anb = pr.param_fm("an", attn_norm, 8)
        pr.rmsnorm(an, anb, 0, T)
        gq, gqb = pr.param_bc("gq", q_norm, 64)
        wres = pr.carve(X, 12288, [2, KC, 512], BF16); b_wres = Buf()
        kout = pr.carve(X, 28672, [D], F32); b_kout = Buf()
        sqk = pr.sb("sqk", [128, 512]); b_sqk = Buf()
        tmpk = pr.sb("tmpk", [128, 512]); b_tmpk = Buf()
        ssk = pr.sb("ssk", [128, 8]); b_ssk = Buf()
        for hf in range(2):
            cx.dma("pool", wres[:, hf, :, :], w_q[:, hf * 512:(hf + 1) * 512].rearrange("(k p) c -> p k c", p=128), writes=[b_wres])
        for hf in range(2):
            pt, pb = pr.pget()
            for k in range(KC):
                cx.op("pe", lambda e, k=k: e.matmul(pt[:, :], lhsT=xn[:, k, 0:128], rhs=wres[:, hf, k, :], start=(k == 0), stop=(k == KC - 1)),
                      reads=[b_xn, b_wres], writes=[pb], signal=(k == KC - 1))
            ko = kout[:, hf * 512:(hf + 1) * 512]
            cx.op("act", lambda e: e.activation(out=sqk, in_=pt[:, :], func=AF.Square), reads=[pb], writes=[b_sqk])
            cx.op("dve", lambda e: e.tensor_reduce(out=ssk, in_=sqk.rearrange("p (g d) -> p g d", d=64), axis=AX.X, op=ALU.add),
                  reads=[b_sqk], writes=[b_ssk])
            cx.op("dve", lambda e: e.tensor_scalar(out=ssk, in0=ssk, scalar1=1.0 / 64, scalar2=EPS, op0=ALU.mult, op1=ALU.add),
                  reads=[b_ssk], writes=[b_ssk])
            cx.op("pool", lambda e: e.tensor_tensor(out=ssk, in0=ssk, in1=pr.nhalf[:, 0:8], op=ALU.pow), reads=[b_ssk, pr.b_const], writes=[b_ssk])
            cx.op("dve", lambda e: e.tensor_tensor(out=tmpk.rearrange("p (g d) -> p g d", d=64), in0=pt[:, :].rearrange("p (g d) -> p g d", d=64),
                                                  in1=ssk.unsqueeze(2).to_broadcast([128, 8, 64]), op=ALU.mult), reads=[pb, b_ssk], writes=[b_tmpk])
            cx.op("dve", lambda e: e.tensor_tensor(out=ko.rearrange("p (g d) -> p g d", d=64), in0=tmpk.rearrange("p (g d) -> p g d", d=64),
                                                  in1=gq.unsqueeze(1).to_broadcast([128, 8, 64]), op=ALU.mult), reads=[b_tmpk, gqb], writes=[b_kout])
        cx.dma("sp", q_out, kout, reads=[b_kout])
    cx.finish("sp")
    return nc


def build_ATT(npool, layer):
    NB = 128
    nc = bass.Bass("TRN2", target_bir_lowering=False)
    cx = Ctx(nc)
    sb = lambda name, shape, dt=F32: nc.alloc_sbuf_tensor(name, list(shape), dt)[:]
    din = lambda name, shape, dt=F32: nc.dram_tensor(name, list(shape), dt, kind="ExternalInput").ap()
    ckv = din("ckv", [npool * 128, 256]); ptab = din("ptab", [NB * NPG], I32)
    qd = din("q", [NB * DS, 128]); knd = din("kn", [NB * DS, 128]); vnd = din("vn", [NB * DS, 128])
    rb = din("rb", [32]); lq1 = din("lq1", [64]); lk1 = din("lk1", [64]); lq2 = din("lq2", [64]); lk2 = din("lk2", [64])
    gsd = din("gsub", [128]); ident_d = din("ident", [128, 128]); anti_d = din("anti", [128, 128])
    od = nc.dram_tensor("o", [NB * DS, 128], F32, kind="ExternalOutput").ap()
    tbl = nc.dram_tensor("tbl", [512], F32, kind="Internal").ap()
    banks = [(nc.alloc_psum_tensor("ps%d" % i, [128, 512], F32)[:], Buf()) for i in range(8)]
    st = {"i": 0}

    def pget():
        t, b = banks[st["i"]]
        st["i"] = (st["i"] + 1) % 8
        return t, b
    b_c = Buf()
    ident = sb("ident_s", [128, 128]); anti = sb("anti_s", [128, 128]); identb = sb("identb", [128, 128], BF16)
    nhalf = sb("nhalf", [128, 128])
    xin = sb("xin", [128, 128]); b_xin = Buf()
    cx.dma("sp", ident, ident_d, writes=[b_c])
    cx.dma("sp", anti, anti_d, writes=[b_c])
    cx.op("dve", lambda e: e.tensor_copy(out=identb, in_=ident), reads=[b_c], writes=[b_c])
    cx.op("dve", lambda e: e.memset(nhalf, -0.5), writes=[b_c])
    pts = sb("pts", [128, NB * NPG], I32); b_pts = Buf()
    with nc.allow_non_contiguous_dma(reason="bcast"):
        cx.dma("sp", pts, ptab.partition_broadcast(128), writes=[b_pts])
    ioti = sb("ioti", [128, 1], I32); iotf = sb("iotf", [128, 1]); b_io = Buf()
    cx.op("pool", lambda e: e.iota(ioti, pattern=[[0, 1]], base=0, channel_multiplier=1), writes=[b_io])
    cx.op("dve", lambda e: e.tensor_copy(out=iotf, in_=ioti), reads=[b_io], writes=[b_io])
    idx = sb("idx", [128, NB * NPG], I32); b_idx = Buf()
    cx.op("dve", lambda e: e.tensor_scalar(out=idx, in0=pts, scalar1=128.0, scalar2=iotf[:, 0:1], op0=ALU.mult, op1=ALU.add),
          reads=[b_pts, b_io], writes=[b_idx])
    Qblk = sb("Qblk", [128, NB, 16], BF16); b_Q = Buf()
    KnT = sb("KnT", [128, NB * DS], BF16); b_KnT = Buf()
    cx.op("dve", lambda e: e.memset(Qblk, 0.0), writes=[b_Q])
    for i in range(NB * DS // 128):
        cx.dma("sp", xin, qd[i * 128:(i + 1) * 128, :], writes=[b_xin])
        pt, pb = pget()
        cx.op("pe", lambda e: e.transpose(pt[:, 0:128], xin, ident), reads=[b_xin, b_c], writes=[pb])
        for c in range(2):
            cx.op("act", lambda e, c=c: e.activation(out=Qblk[c * 64:(c + 1) * 64, i * 16:(i + 1) * 16, c * 8:(c + 1) * 8],
                                                     in_=pt[c * 64:(c + 1) * 64, 0:128].rearrange("p (b t) -> p b t", t=DS), func=AF.Copy),
                  reads=[pb], writes=[b_Q])
        cx.dma("sp", xin, knd[i * 128:(i + 1) * 128, :], writes=[b_xin])
        pt, pb = pget()
        cx.op("pe", lambda e: e.transpose(pt[:, 0:128], xin, ident), reads=[b_xin, b_c], writes=[pb])
        cx.op("act", lambda e: e.activation(out=KnT[:, i * 128:(i + 1) * 128], in_=pt[:, 0:128], func=AF.Copy), reads=[pb], writes=[b_KnT])
    Vn = sb("Vn", [DS, NB, 130], BF16); b_Vn = Buf()
    cx.op("dve", lambda e: e.memset(Vn[:, :, 128:130], 1.0), writes=[b_Vn])
    cx.dma("pool", Vn[:, :, 0:128], vnd.rearrange("(b t) e -> t b e", t=DS), writes=[b_Vn])
    tblS = sb("tblS", [1, 512]); b_tbl = Buf(); nfar = sb("nfar", [1, 1])
    cx.op("dve", lambda e: e.memset(tblS, 0.0), writes=[b_tbl])
    rbb = sb("rbb", [128, 32]); bfar = sb("bfar", [128, 1]); b_bfar = Buf()
    with nc.allow_non_contiguous_dma(reason="bias table"):
        cx.dma("sp", rbb, rb.partition_broadcast(128), writes=[b_tbl])
    for (bk, d0, d1) in bucket_runs(384):
        cx.op("dve", lambda e, bk=bk, d0=d0, d1=d1: e.tensor_copy(out=tblS[:, 127 + d0:128 + d1],
                                                                 in_=rbb[0:1, bk:bk + 1].to_broadcast([1, d1 - d0 + 1])),
              reads=[b_tbl], writes=[b_tbl])
    cx.op("dve", lambda e: e.tensor_scalar(out=nfar, in0=rbb[0:1, 31:32], scalar1=-1.0, scalar2=None, op0=ALU.mult), reads=[b_tbl], writes=[b_tbl])
    cx.op("dve", lambda e: e.tensor_copy(out=bfar, in_=rbb[:, 31:32]), reads=[b_tbl], writes=[b_bfar])
    cx.op("act", lambda e: e.activation(out=tblS[:, 127:512], in_=tblS[:, 127:512], func=AF.Exp, bias=nfar[:, 0:1]), reads=[b_tbl], writes=[b_tbl])
    b_tbld = Buf()
    cx.dma("sp", tbl.rearrange("(a n) -> a n", a=1), tblS, reads=[b_tbl], writes=[b_tbld])
    Hs = sb("Hs", [128, 8]); Hn = sb("Hn", [8, 8]); b_H = Buf()
    Bs = sb("Bs", [128, 8]); Bn = sb("Bn", [8, 8]); b_B = Buf()
    with nc.allow_non_contiguous_dma(reason="hankel"):
        cx.dma("sp", Hs, bass.AP(tensor=tbl.tensor, offset=128, ap=[[1, 128], [1, 8]]), reads=[b_tbld], writes=[b_H])
        cx.dma("sp", Hn, bass.AP(tensor=tbl.tensor, offset=120, ap=[[1, 8], [1, 8]]), reads=[b_tbld], writes=[b_H])
    pt, pb = pget()
    cx.op("pe", lambda e: e.matmul(pt[:, 0:8], lhsT=anti, rhs=Hs, start=True, stop=True), reads=[b_H, b_c], writes=[pb])
    cx.op("act", lambda e: e.activation(out=Bs, in_=pt[:, 0:8], func=AF.Copy), reads=[pb], writes=[b_B])
    pt, pb = pget()
    cx.op("pe", lambda e: e.matmul(pt[0:8, 0:8], lhsT=anti[0:8, 120:128], rhs=Hn, start=True, stop=True), reads=[b_H, b_c], writes=[pb])
    cx.op("act", lambda e: e.activation(out=Bn, in_=pt[0:8, 0:8], func=AF.Copy), reads=[pb], writes=[b_B])
    lt = sb("lt", [8, 4, 64]); b_lt = Buf(); sm = sb("sm", [8, 4]); b_sm = Buf()
    gs = sb("gs", [8, 128]); b_gs = Buf()
    with nc.allow_non_contiguous_dma(reason="bcast"):
        for i, src in enumerate((lq1, lk1, lq2, lk2)):
            cx.dma("sp", lt[:, i, :], src.partition_broadcast(8), writes=[b_lt])
        cx.dma("sp", gs, gsd.partition_broadcast(8), writes=[b_gs])
    cx.op("dve", lambda e: e.tensor_tensor(out=lt[:, 0, :], in0=lt[:, 0, :], in1=lt[:, 1, :], op=ALU.mult), reads=[b_lt], writes=[b_lt])
    cx.op("dve", lambda e: e.tensor_tensor(out=lt[:, 2, :], in0=lt[:, 2, :], in1=lt[:, 3, :], op=ALU.mult), reads=[b_lt], writes=[b_lt])
    cx.op("dve", lambda e: e.tensor_reduce(out=sm[:, 0:1], in_=lt[:, 0, :], axis=AX.X, op=ALU.add), reads=[b_lt], writes=[b_sm])
    cx.op("dve", lambda e: e.tensor_reduce(out=sm[:, 1:2], in_=lt[:, 2, :], axis=AX.X, op=ALU.add), reads=[b_lt], writes=[b_sm])
    cx.op("act", lambda e: e.activation(out=sm[:, 0:2], in_=sm[:, 0:2], func=AF.Exp), reads=[b_sm], writes=[b_sm])
    cx.op("dve", lambda e: e.scalar_tensor_tensor(out=sm[:, 2:3], in0=sm[:, 1:2], scalar=-lambda_init(layer), in1=sm[:, 0:1],
                                                 op0=ALU.add, op1=ALU.subtract), reads=[b_sm], writes=[b_sm])
    cx.op("dve", lambda e: e.tensor_scalar(out=gs, in0=gs, scalar1=1.0 - lambda_init(layer), scalar2=None, op0=ALU.mult),
          reads=[b_gs], writes=[b_gs])
    NSL = 32
    kvs = []
    for i in range(NSL):
        t = sb("kv%d" % i, [128, 258], BF16)
        kvs.append((t, Buf()))
    b_ones = Buf()
    for t, _ in kvs:
        cx.op("dve", lambda e, t=t: e.memset(t[:, 256:258], 1.0), writes=[b_ones])
    for t, bf in kvs:
        bf.w = b_ones.w
    ktT = [(sb("ktT%d" % i, [128, NPG, 128], BF16), Buf()) for i in range(2)]
    pTt = [(sb("pT%d" % i, [128, 256], BF16), Buf()) for i in range(2)]
    pNt = [(sb("pN%d" % i, [8, 16], BF16), Buf()) for i in range(2)]
    HB = 64
    obuf = sb("obuf", [8, HB, 258]); b_ob = Buf()
    rden = sb("rden", [8, HB, 2]); b_rd = Buf()
    ss = sb("ss", [8, HB]); b_ss = Buf()

    def post(b0):
        ov = obuf.rearrange("p b (c e) -> p b c e", c=2)
        cx.op("dve", lambda e: e.reciprocal(out=rden, in_=ov[:, :, :, 128]), reads=[b_ob], writes=[b_rd])
        o1 = ov[:, :, 0, 0:128]
        o2 = ov[:, :, 1, 0:128]
        cx.op("dve", lambda e: e.tensor_tensor(out=o1, in0=o1, in1=rden[:, :, 0:1].to_broadcast([8, HB, 128]), op=ALU.mult), reads=[b_ob, b_rd], writes=[b_ob])
        cx.op("dve", lambda e: e.tensor_tensor(out=o2, in0=o2, in1=rden[:, :, 1:2].to_broadcast([8, HB, 128]), op=ALU.mult), reads=[b_ob, b_rd], writes=[b_ob])
        cx.op("dve", lambda e: e.scalar_tensor_tensor(out=o1, in0=o2, scalar=sm[:, 2:3], in1=o1, op0=ALU.mult, op1=ALU.add), reads=[b_ob, b_sm], writes=[b_ob])
        cx.op("dve", lambda e: e.tensor_tensor(out=o2, in0=o1, in1=o1, op=ALU.mult), reads=[b_ob], writes=[b_ob])
        cx.op("dve", lambda e: e.tensor_reduce(out=ss, in_=o2, axis=AX.X, op=ALU.add), reads=[b_ob], writes=[b_ss])
        cx.op("dve", lambda e: e.tensor_scalar(out=ss, in0=ss, scalar1=1.0 / 128, scalar2=EPS, op0=ALU.mult, op1=ALU.add), reads=[b_ss], writes=[b_ss])
        cx.op("pool", lambda e: e.tensor_tensor(out=ss, in0=ss, in1=nhalf[0:8, 0:HB], op=ALU.pow), reads=[b_ss, b_c], writes=[b_ss])
        cx.op("dve", lambda e: e.tensor_tensor(out=o1, in0=o1, in1=ss.unsqueeze(2).to_broadcast([8, HB, 128]), op=ALU.mult), reads=[b_ob, b_ss], writes=[b_ob])
        cx.op("dve", lambda e: e.tensor_tensor(out=o2, in0=o1, in1=gs.unsqueeze(1).to_broadcast([8, HB, 128]), op=ALU.mult), reads=[b_ob, b_gs], writes=[b_ob])
        cx.dma("sp", od[b0 * DS:(b0 + HB) * DS, :].rearrange("(b t) e -> t b e", t=DS), o2, reads=[b_ob])
    rows = ckv
    for b in range(NB):
        sl = []
        for j in range(NPG):
            t, bf = kvs[(b * NPG + j) % NSL]
            cx.dma("pool", t[:, 0:256], rows, reads=[b_idx], writes=[bf], indirect=idx[:, b * NPG + j:b * NPG + j + 1])
            sl.append((t, bf))
        kt_, ktb = ktT[b % 2]
        for g in range(4):
            pt, pb = pget()
            ptb = pt.bitcast(BF16)
            for jj in range(4):
                t, bf = sl[g * 4 + jj]
                cx.op("pe", lambda e, jj=jj, t=t: e.transpose(ptb[:, jj * 128:(jj + 1) * 128], t[:, 0:128], identb),
                      reads=[bf, b_c], writes=[pb], signal=(jj == 3))
            eng = "act" if g % 2 == 0 else "dve"
            if eng == "act":
                cx.op("act", lambda e, g=g: e.activation(out=kt_[:, g * 4:(g + 1) * 4, :], in_=ptb[:, 0:512].rearrange("p (a b) -> p a b", a=4), func=AF.Copy),
                      reads=[pb], writes=[ktb])
            else:
                cx.op("dve", lambda e, g=g: e.tensor_copy(out=kt_[:, g * 4:(g + 1) * 4, :], in_=ptb[:, 0:512].rearrange("p (a b) -> p a b", a=4)),
                      reads=[pb], writes=[ktb])
        sp_, spb = pget()
        for j in range(NPG):
            cx.op("pe", lambda e, j=j: e.matmul(sp_[:, j * 16:(j + 1) * 16], lhsT=kt_[:, j, :], rhs=Qblk[:, b, :], start=True, stop=True),
                  reads=[ktb, b_Q], writes=[spb], signal=False)
        cx.op("pe", lambda e: e.matmul(sp_[0:8, 256:272], lhsT=KnT[:, b * DS:(b + 1) * DS], rhs=Qblk[:, b, :], start=True, stop=True),
              reads=[b_KnT, b_Q], writes=[spb])
        pT, pTb = pTt[b % 2]
        pN, pNb = pNt[b % 2]
        cx.op("act", lambda e: e.activation(out=pT, in_=sp_[:, 0:256], func=AF.Exp, bias=bfar[:, 0:1], scale=0.125), reads=[spb, b_bfar], writes=[pTb])
        cx.op("act", lambda e: e.activation(out=pN, in_=sp_[0:8, 256:272], func=AF.Exp, bias=bfar[0:8, 0:1], scale=0.125), reads=[spb, b_bfar], writes=[pNb])
        cx.op("dve", lambda e: e.tensor_tensor(out=pT[:, 240:256].rearrange("p (c t) -> p c t", c=2), in0=pT[:, 240:256].rearrange("p (c t) -> p c t", c=2),
                                              in1=Bs.unsqueeze(1).to_broadcast([128, 2, 8]), op=ALU.mult), reads=[pTb, b_B], writes=[pTb])
        cx.op("dve", lambda e: e.tensor_tensor(out=pN.rearrange("p (c t) -> p c t", c=2), in0=pN.rearrange("p (c t) -> p c t", c=2),
                                              in1=Bn.unsqueeze(1).to_broadcast([8, 2, 8]), op=ALU.mult), reads=[pNb, b_B], writes=[pNb])
        ac, acb = pget()
        for c in range(2):
            for j in range(NPG):
                t, bf = sl[j]
                cx.op("pe", lambda e, c=c, j=j, t=t: e.matmul(ac[0:8, c * 129:(c + 1) * 129], lhsT=pT[:, j * 16 + c * 8:j * 16 + c * 8 + 8],
                                                             rhs=t[:, 128:257], start=(j == 0), stop=False),
                      reads=[pTb, bf], writes=[acb], signal=False)
            cx.op("pe", lambda e, c=c: e.matmul(ac[0:8, c * 129:(c + 1) * 129], lhsT=pN[0:8, c * 8:(c + 1) * 8], rhs=Vn[:, b, 0:129],
                                               start=False, stop=True), reads=[pNb, b_Vn], writes=[acb])
        cx.op("act", lambda e: e.activation(out=obuf[:, b % HB, :], in_=ac[0:8, 0:258], func=AF.Copy), reads=[acb], writes=[b_ob])
        if b % HB == HB - 1:
            post(b - HB + 1)
    cx.finish("sp")
    return nc


def kernel(x_prompt, x_sample, state_conv, cache_k, cache_v, page_table, rel_bias,
           conv_norm, w_pw1, b_pw1, w_dw, b_dw, conv_mid_norm, w_pw2, b_pw2,
           kv_norm, w_kv, k_norm, attn_norm, w_q, q_norm,
           lambda_q1, lambda_k1, lambda_q2, lambda_k2, sub_norm, w_o,
           ffn_norm, w_gate, w_up, w_down):
    f = lambda a: np.ascontiguousarray(np.asarray(a, dtype=np.float32))
    NCORE = 8
    x_prompt = f(x_prompt); x_sample = f(x_sample); state_conv = f(state_conv)
    B, P = x_prompt.shape[0], x_prompt.shape[1]
    assert B == NCORE and x_sample.shape[0] == NCORE * NBS
    ident = np.eye(128, dtype=np.float32)
    anti = np.ascontiguousarray(ident[::-1])
    wts = dict(rel_bias=f(rel_bias), conv_norm=f(conv_norm), w_pw1=f(w_pw1), b_pw1=f(b_pw1), w_dw=f(w_dw), b_dw=f(b_dw),
               conv_mid_norm=f(conv_mid_norm), w_pw2=f(w_pw2), b_pw2=f(b_pw2), kv_norm=f(kv_norm), w_kv=f(w_kv),
               k_norm=f(k_norm), attn_norm=f(attn_norm), w_q=f(w_q), q_norm=f(q_norm), lambda_q1=f(lambda_q1),
               lambda_k1=f(lambda_k1), lambda_q2=f(lambda_q2), lambda_k2=f(lambda_k2), sub_norm=f(sub_norm), w_o=f(w_o),
               ffn_norm=f(ffn_norm), w_gate=f(w_gate), w_up=f(w_up), w_down=f(w_down))
    cores = list(range(NCORE))
    if FUSED:
        ck = np.asarray(cache_k, dtype=np.float32); cv = np.asarray(cache_v, dtype=np.float32)
        npool = ck.shape[0]
        ckv = np.empty((npool * 128, NH, 256), np.float32)
        ckv[:, :, 0:128] = ck.reshape(npool * 128, NH, 128)
        ckv[:, :, 128:256] = cv.reshape(npool * 128, NH, 128)
        ckvA = ckv.reshape(npool * 128, NH * 256)
        pt = np.asarray(page_table, dtype=np.int32)
        ncf = build_L1(P, fused=True, npool=npool)
        ins = []
        for c in cores:
            m = dict(wts)
            m.update(xp=x_prompt[c], xs=x_sample[c * NBS:(c + 1) * NBS].reshape(NS, D),
                     sc=np.ascontiguousarray(state_conv[:, c * NBS:(c + 1) * NBS]), ident=ident, anti=anti,
                     ckvA=ckvA, ptab=np.ascontiguousarray(pt[c * NBS:(c + 1) * NBS].reshape(-1)))
            ins.append(m)
        r1 = run_bass_kernel_spmd(ncf, ins, core_ids=cores).results
        y_prompt = np.stack([r1[c]["yp"] for c in cores])
        y_sample = np.concatenate([r1[c]["ys"] for c in cores], axis=0).reshape(NCORE * NBS, DS, D)
        conv_state_p = np.stack([r1[c]["csp"] for c in cores], axis=1)
        conv_state_s = np.concatenate([r1[c]["css"] for c in cores], axis=1)
        k_prompt = np.stack([r1[c]["kp"] for c in cores]).reshape(B, P, NH, 2, 64)
        v_prompt = np.stack([r1[c]["vp"] for c in cores]).reshape(B, P, NH, 128)
        k_sample = np.concatenate([r1[c]["ks"] for c in cores], axis=0).reshape(NCORE * NBS, DS, NH, 2, 64)
        v_sample = np.concatenate([r1[c]["vs"] for c in cores], axis=0).reshape(NCORE * NBS, DS, NH, 128)
        return (y_prompt, y_sample, conv_state_p, conv_state_s, k_prompt, v_prompt, k_sample, v_sample)
    nc1 = build_L1(P)
    ins = []
    for c in cores:
        m = dict(wts)
        m.update(xp=x_prompt[c], xs=x_sample[c * NBS:(c + 1) * NBS].reshape(NS, D),
                 sc=np.ascontiguousarray(state_conv[:, c * NBS:(c + 1) * NBS]), ident=ident, anti=anti)
        ins.append(m)
    r1 = run_bass_kernel_spmd(nc1, ins, core_ids=cores).results
    y_prompt = np.stack([r1[c]["yp"] for c in cores])
    conv_state_p = np.stack([r1[c]["csp"] for c in cores], axis=1)
    conv_state_s = np.concatenate([r1[c]["css"] for c in cores], axis=1)
    k_prompt = np.stack([r1[c]["kp"] for c in cores]).reshape(B, P, NH, 2, 64)
    v_prompt = np.stack([r1[c]["vp"] for c in cores]).reshape(B, P, NH, 128)
    ks_all = np.concatenate([r1[c]["ks"] for c in cores], axis=0)
    vs_all = np.concatenate([r1[c]["vs"] for c in cores], axis=0)
    k_sample = ks_all.reshape(NCORE * NBS, DS, NH, 2, 64)
    v_sample = vs_all.reshape(NCORE * NBS, DS, NH, 128)
    hs = [r1[c]["hs1"] for c in cores]
    q_all = np.concatenate([r1[c]["q2"] for c in cores], axis=0)
    ck = np.asarray(cache_k, dtype=np.float32); cv = np.asarray(cache_v, dtype=np.float32)
    npool = ck.shape[0]
    ckv = [np.ascontiguousarray(np.concatenate([ck[:, :, hd].reshape(npool * 128, 128), cv[:, :, hd].reshape(npool * 128, 128)], axis=1))
           for hd in range(NH)]
    ptab = np.ascontiguousarray(np.asarray(page_table, dtype=np.int32).reshape(-1))
    y_sample = None
    for j in range(2):
        nca = build_ATT(npool, 2 + j)
        ins = []
        for hd in cores:
            sl = slice(hd * 128, (hd + 1) * 128)
            ins.append(dict(ckv=ckv[hd], ptab=ptab, q=np.ascontiguousarray(q_all[:, sl]), kn=np.ascontiguousarray(ks_all[:, sl]),
                            vn=np.ascontiguousarray(vs_all[:, sl]), rb=np.ascontiguousarray(wts["rel_bias"][:, hd]),
                            lq1=wts["lambda_q1"][j], lk1=wts["lambda_k1"][j], lq2=wts["lambda_q2"][j], lk2=wts["lambda_k2"][j],
                            gsub=wts["sub_norm"][j], ident=ident, anti=anti))
        ra = run_bass_kernel_spmd(nca, ins, core_ids=cores).results
        o_all = np.concatenate([ra[hd]["o"] for hd in cores], axis=1)
        nc3 = build_L3(with_q=(j == 0))
        ins = []
        for c in cores:
            m = dict(hs=hs[c], oat=np.ascontiguousarray(o_all[c * NS:(c + 1) * NS]), w_o=wts["w_o"][j], ffn_norm=wts["ffn_norm"][2 + j],
                     w_gate=wts["w_gate"][2 + j], w_up=wts["w_up"][2 + j], w_down=wts["w_down"][2 + j], ident=ident)
            if j == 0:
                m.update(attn_norm=wts["attn_norm"][1], w_q=wts["w_q"][1], q_norm=wts["q_norm"][1])
            ins.append(m)
        r3 = run_bass_kernel_spmd(nc3, ins, core_ids=cores).results
        hs = [r3[c]["hs_out"] for c in cores]
        if j == 0:
            q_all = np.concatenate([r3[c]["q_out"] for c in cores], axis=0)
    y_sample = np.concatenate(hs, axis=0).reshape(NCORE * NBS, DS, D)
    return (y_prompt, y_sample, conv_state_p, conv_state_s, k_prompt, v_prompt, k_sample, v_sample)
```
